# Optimizing a Trainium2 kernel written in Bass

```python
import jax, jax.numpy as jnp
from jax import lax
import numpy as np

D_MODEL = 1024
BATCH = 2
SEQ = 8192
DEPTH = 2

CHUNK = 64
N_MIXERS = 2
N_POOL_LAYERS = (DEPTH + 1) // 2
N_ATT_LAYERS = DEPTH // 2

POOL_WIDTH = 2 * D_MODEL
POOL_WINDOWS = (2, 4, 8, 16)
N_POOL_GROUPS = len(POOL_WINDOWS)
POOL_GROUP = POOL_WIDTH // N_POOL_GROUPS

HEAD_DIM = 64
ATT_WIDTH = D_MODEL
N_HEADS = ATT_WIDTH // HEAD_DIM
LEFT_CHUNKS = 8
BAND = (LEFT_CHUNKS + 1) * CHUNK
MAX_REL = 256

RMS_EPS = 1e-6

kernel_name = "hybrid_pool_chunkattn_sandwich"


def rms_norm(x, g):
    xf = x.astype(jnp.float32)
    y = xf * lax.rsqrt(jnp.mean(xf * xf, axis=-1, keepdims=True) + RMS_EPS)
    return (y * g.astype(jnp.float32)).astype(x.dtype)


def pool_mixer(h, w_in, w_group, scale, w_out):
    b, s, _ = h.shape
    u = h @ w_in
    a, z = jnp.split(u, 2, axis=-1)
    ag = a.astype(jnp.float32).reshape(b, s, N_POOL_GROUPS, POOL_GROUP)
    cs = jnp.cumsum(ag, axis=1)
    cs0 = jnp.concatenate([jnp.zeros((b, 1, N_POOL_GROUPS, POOL_GROUP), jnp.float32), cs], axis=1)
    pos = jnp.arange(s)
    pooled = []
    for gi, w in enumerate(POOL_WINDOWS):
        cg = cs0[:, :, gi]
        lagged = jnp.concatenate([jnp.zeros((b, w, POOL_GROUP), jnp.float32), cg], axis=1)[:, 1:s + 1]
        cnt = jnp.minimum(pos + 1, w).astype(jnp.float32)[None, :, None]
        pooled.append((cg[:, 1:] - lagged) / cnt)
    mixed = (jnp.stack(pooled, axis=2) - ag).astype(a.dtype)
    mixed = jnp.einsum('bsgc,gcd->bsgd', mixed, w_group).reshape(b, s, POOL_WIDTH) * scale
    return (mixed * jax.nn.silu(z)) @ w_out


def chunk_attention(h, w_in, rel_bias, w_out):
    b, s, _ = h.shape
    nc = s // CHUNK
    pad = LEFT_CHUNKS * CHUNK
    u = h @ w_in
    q, k, v, z = jnp.split(u, 4, axis=-1)
    q = q.reshape(b, s, N_HEADS, HEAD_DIM)
    k = k.reshape(b, s, N_HEADS, HEAD_DIM)
    v = v.reshape(b, s, N_HEADS, HEAD_DIM)
    kp = jnp.pad(k, ((0, 0), (pad, 0), (0, 0), (0, 0)))
    vp = jnp.pad(v, ((0, 0), (pad, 0), (0, 0), (0, 0)))
    qc = q.reshape(b, nc, CHUNK, N_HEADS, HEAD_DIM).transpose(1, 0, 2, 3, 4)
    rel = jnp.arange(CHUNK)[:, None] + pad - jnp.arange(BAND)[None, :]
    idx = jnp.clip(rel, -MAX_REL, MAX_REL) + MAX_REL
    bias = rel_bias.astype(jnp.float32)[:, idx]
    qk_scale = HEAD_DIM ** -0.5

    def one_chunk(args):
        c, qb = args
        start = c * CHUNK
        kb = lax.dynamic_slice_in_dim(kp, start, BAND, axis=1)
        vb = lax.dynamic_slice_in_dim(vp, start, BAND, axis=1)
        sc = jnp.einsum('bqhd,bkhd->bhqk', qb, kb).astype(jnp.float32) * qk_scale + bias
        valid = (start - pad + jnp.arange(BAND)) >= 0
        sc = jnp.where(valid[None, None, None, :], sc, -jnp.inf)
        p = jax.nn.softmax(sc, axis=-1).astype(vb.dtype)
        return jnp.einsum('bhqk,bkhd->bqhd', p, vb)

    o = lax.map(one_chunk, (jnp.arange(nc), qc))
    o = o.transpose(1, 0, 2, 3, 4).reshape(b, s, ATT_WIDTH)
    return (o * jax.nn.silu(z)) @ w_out


def setup_inputs(seed: int = 0) -> dict:
    key = jax.random.key(seed)
    ks = jax.random.split(key, 12)
    f32 = jnp.float32
    x = jax.random.normal(ks[0], (BATCH, SEQ, D_MODEL), f32)
    norm_pre = 1.0 + 0.05 * jax.random.normal(ks[1], (DEPTH, D_MODEL), f32)
    norm_post = 1.0 + 0.05 * jax.random.normal(ks[2], (DEPTH, D_MODEL), f32)
    pool_w_in = jax.random.normal(ks[3], (N_POOL_LAYERS, D_MODEL, 2 * POOL_WIDTH), f32) * D_MODEL ** -0.5
    pool_w_group = jax.random.normal(ks[4], (N_POOL_LAYERS, N_POOL_GROUPS, POOL_GROUP, POOL_GROUP), f32) * POOL_GROUP ** -0.5
    pool_scale = 1.0 + 0.1 * jax.random.normal(ks[5], (N_POOL_LAYERS, POOL_WIDTH), f32)
    pool_w_out = jax.random.normal(ks[6], (N_POOL_LAYERS, POOL_WIDTH, D_MODEL), f32) * POOL_WIDTH ** -0.5
    att_w_in = jax.random.normal(ks[7], (N_ATT_LAYERS, D_MODEL, 4 * ATT_WIDTH), f32) * D_MODEL ** -0.5
    att_rel_bias = 0.5 * jax.random.normal(ks[8], (N_ATT_LAYERS, N_HEADS, 2 * MAX_REL + 1), f32)
    att_w_out = jax.random.normal(ks[9], (N_ATT_LAYERS, ATT_WIDTH, D_MODEL), f32) * ATT_WIDTH ** -0.5
    return {"x": x, "norm_pre": norm_pre, "norm_post": norm_post,
            "pool_w_in": pool_w_in, "pool_w_group": pool_w_group, "pool_scale": pool_scale,
            "pool_w_out": pool_w_out, "att_w_in": att_w_in, "att_rel_bias": att_rel_bias,
            "att_w_out": att_w_out}


def reference(x, norm_pre, norm_post, pool_w_in, pool_w_group, pool_scale, pool_w_out,
              att_w_in, att_rel_bias, att_w_out):
    for i in range(DEPTH):
        h = rms_norm(x, norm_pre[i])
        j = i // N_MIXERS
        if i % N_MIXERS == 0:
            y = pool_mixer(h, pool_w_in[j], pool_w_group[j], pool_scale[j], pool_w_out[j])
        else:
            y = chunk_attention(h, att_w_in[j], att_rel_bias[j], att_w_out[j])
        x = x + rms_norm(y, norm_post[i])
    return x
```

```python
from contextlib import ExitStack

import ml_dtypes
import numpy as np

import concourse.bass as bass
import concourse.mybir as mybir
from concourse.bass_utils import run_bass_kernel_spmd

F32 = mybir.dt.float32
BF16 = mybir.dt.bfloat16
AF = mybir.ActivationFunctionType
ALU = mybir.AluOpType

D = 1024
SEQ = 8192
NCORES = 8
SEG = 2048
HALO = 512
NT0 = 21
NT1 = 20
EPS = 1e-6
WINDOWS = (2, 4, 8, 16)


class _Buf:
    __slots__ = ("w", "rs")

    def __init__(self):
        self.w = None
        self.rs = []


class Sched:
    ENGS = ("pe", "act", "dve", "pool", "sp")

    def __init__(self, nc, es):
        self.nc = nc
        self.es = es
        self.ops = {e: [] for e in self.ENGS}
        self.sems = {}
        self.cnt = {}
        self.waited = {e: {} for e in self.ENGS}
        self.bufs = {}
        self.name_alloc = {}
        self.allocs = {}
        self.alloc_names = {}
        for e in ("pe", "act", "dve", "pool"):
            self._sem("eng_" + e)

    def _sem(self, key):
        if key not in self.sems:
            self.sems[key] = self.es.enter_context(self.nc.semaphore(key))
            self.cnt[key] = 0
        return self.sems[key]

    def B(self, name):
        b = self.bufs.get(name)
        if b is None:
            b = self.bufs[name] = _Buf()
        return b

    def _alias_deps(self, name):
        if name in self.name_alloc:
            return []
        best = None
        for an in self.allocs:
            if name.startswith(an) and (best is None or len(an) > len(best)):
                best = an
        self.name_alloc[name] = best
        if best is None:
            return []
        self.alloc_names.setdefault(best, set()).add(name)
        s0, e0, ph = self.allocs[best]
        deps = []
        for an, (s1, e1, ph1) in self.allocs.items():
            if ph1 < ph and ph1 >= 0 and s1 < e0 and s0 < e1:
                deps += [self.B(n) for n in sorted(self.alloc_names.get(an, ()))]
        return deps

    def _need(self, eng, tok, waits):
        if tok is None:
            return
        key, val = tok
        if self.waited[eng].get(key, 0) >= val:
            return
        self.waited[eng][key] = val
        waits[key] = max(waits.get(key, 0), val)

    def op(self, eng, fns, reads=(), writes=(), dma_sem=None):
        if callable(fns):
            fns = [fns]
        waits = {}
        rb = [self.B(n) for n in reads]
        wb = [self.B(n) for n in writes]
        xb = []
        for n in list(reads) + list(writes):
            xb += self._alias_deps(n)
        for b in rb:
            self._need(eng, b.w, waits)
        for b in wb + xb:
            self._need(eng, b.w, waits)
            for t in b.rs:
                self._need(eng, t, waits)
        if dma_sem is not None:
            key = "dma_" + dma_sem
            self._sem(key)
            self.cnt[key] += 16
            inc = 16
        else:
            key = "eng_" + eng
            self.cnt[key] += 1
            inc = 1
        tok = (key, self.cnt[key])
        for b in rb:
            b.rs.append(tok)
        for b in wb:
            b.w = tok
            b.rs = []
        self.ops[eng].append((sorted(waits.items()), fns, key, inc))
        return tok

    def wait_all(self, eng, toks):
        waits = {}
        for t in toks:
            self._need(eng, t, waits)
        self.ops[eng].append((sorted(waits.items()), [], None, 0))

    def emit(self):
        sems = self.sems

        def run(e, lst):
            for waits, fns, key, inc in lst:
                for k, v in waits:
                    e.wait_ge(sems[k], v)
                ins = None
                for f in fns:
                    ins = f(e)
                if fns and key is not None:
                    ins.then_inc(sems[key], inc)

        with self.nc.Block() as block:
            @block.tensor
            def _(e):
                run(e, self.ops["pe"])

            @block.scalar
            def _(e):
                run(e, self.ops["act"])

            @block.vector
            def _(e):
                run(e, self.ops["dve"])

            @block.gpsimd
            def _(e):
                run(e, self.ops["pool"])

            @block.sync
            def _(e):
                run(e, self.ops["sp"])


def MM(out, lhsT, rhs, start=True, stop=True):
    return lambda e: e.matmul(out, lhsT=lhsT, rhs=rhs, start=start, stop=stop)


def TR(out, in_, ident):
    return lambda e: e.transpose(out=out, in_=in_, identity=ident)


def ACT(out, in_, func, **kw):
    return lambda e: e.activation(out=out, in_=in_, func=func, **kw)


def CP(out, in_):
    return lambda e: e.tensor_copy(out=out, in_=in_)


def TS(out, in0, s1, s2, op0, op1=None):
    if op1 is None:
        return lambda e: e.tensor_scalar(out=out, in0=in0, scalar1=s1, scalar2=None, op0=op0)
    return lambda e: e.tensor_scalar(out=out, in0=in0, scalar1=s1, scalar2=s2, op0=op0, op1=op1)


def STT(out, in0, scalar, in1, op0, op1):
    return lambda e: e.scalar_tensor_tensor(out=out, in0=in0, scalar=scalar, in1=in1, op0=op0, op1=op1)


def TT(out, in0, in1, op):
    return lambda e: e.tensor_tensor(out=out, in0=in0, in1=in1, op=op)


def DMA(out, in_):
    return lambda e: e.dma_start(out=out, in_=in_)


def MEMSET(ap, v):
    return lambda e: e.memset(ap, v)


def RECIP(out, in_):
    return lambda e: e.reciprocal(out=out, in_=in_)


class Ctx:
    ARENA_BYTES = 207 * 1024

    def __init__(self, nc, es):
        self.nc = nc
        self.es = es
        self.S = Sched(nc, es)
        self.psum = es.enter_context(nc.psum_tensor("psum_all", [128, 4096], F32))
        self.arena = es.enter_context(nc.sbuf_tensor("arena", [128, self.ARENA_BYTES // 2], BF16))
        self.off = 0
        self.phase = -1
        self.bank_ptr = 0
        self.ring = list(range(8))

    def sb(self, name, shape, dt):
        esz = 4 if dt == F32 else 2
        n = 1
        for s in shape[1:]:
            n *= s
        nbytes = (n * esz + 63) // 64 * 64
        assert self.off + nbytes <= self.ARENA_BYTES, (name, self.off, nbytes)
        a = self.arena[:, self.off // 2:self.off // 2 + n * esz // 2]
        self.S.allocs[name] = (self.off, self.off + nbytes, self.phase)
        self.off += nbytes
        if dt != BF16:
            a = a.bitcast(dt)
        if len(shape) == 3:
            a = a.rearrange("p (a b) -> p a b", a=shape[1])
        elif len(shape) == 4:
            a = a.rearrange("p (a b c) -> p a b c", a=shape[1], b=shape[2])
        return a

    def banks(self, n=1):
        r = self.ring
        if n == 1:
            b = r[self.bank_ptr % len(r)]
            self.bank_ptr += 1
            return b, [f"bank{b}"]
        assert n == 2
        for _ in range(len(r) + 1):
            b = r[self.bank_ptr % len(r)]
            if b % 2 == 0 and r[(self.bank_ptr + 1) % len(r)] == b + 1:
                self.bank_ptr += 2
                return b, [f"bank{b}", f"bank{b + 1}"]
            self.bank_ptr += 1
        raise AssertionError("no adjacent PSUM bank pair in ring")

    def pf32(self, b, ncols=512):
        return self.psum[:, b * 512:b * 512 + ncols]

    def pbf16(self, b):
        return self.psum[:, b * 512:(b + 1) * 512].bitcast(BF16)

    def barrier(self):
        S = self.S
        toks = [(k, v) for k, v in S.cnt.items() if v > 0]
        for e in S.ENGS:
            S.wait_all(e, toks)


def load_consts(C, aps):
    S = C.S
    K = {}
    K["ident"] = C.sb("ident", [128, 128], BF16)
    S.op("sp", DMA(K["ident"], aps["ident"]), writes=["ident"], dma_sem="c_ident")
    K["ident8"] = C.sb("ident8", [128, 128], BF16)
    S.op("sp", DMA(K["ident8"], aps["ident8"]), writes=["ident8"], dma_sem="c_ident8")
    K["gpre"] = C.sb("gpre", [128, 2, 8], F32)
    S.op("sp", DMA(K["gpre"], aps["gpreT"]), writes=["gpre"], dma_sem="c_gpre")
    K["mhalf"] = C.sb("mhalf", [128, 1], F32)
    S.op("pool", MEMSET(K["mhalf"], -0.5), writes=["mhalf"])
    K["junk"] = C.sb("junk", [128, 1024], BF16)
    return K


def merge(*lists):
    lists = [l for l in lists if l]
    out = []
    pos = [0] * len(lists)
    total = sum(len(l) for l in lists)
    for _ in range(total):
        best, bi = None, -1
        for i, l in enumerate(lists):
            if pos[i] < len(l):
                frac = (pos[i] + 0.5) / len(l)
                if best is None or frac < best:
                    best, bi = frac, i
        out.append(lists[bi][pos[bi]])
        pos[bi] += 1
    return out


def run(thunks):
    for t in thunks:
        t()


def norm_transpose_thunks(C, K, layer, xs_ap, xs_name, hT, hT_name, col0, st, st_name, hb, hb_name, after_hb=None):
    S = C.S
    ss, ms, rs = st[:, 0:1], st[:, 1:2], st[:, 2:3]
    th = []
    th.append(lambda: S.op("act", ACT(K["junk"], xs_ap, AF.Square, accum_out=ss), reads=[xs_name], writes=["junk", st_name + "a"]))
    th.append(lambda: S.op("dve", TS(ms, ss, 1.0 / D, EPS, ALU.mult, ALU.add), reads=[st_name + "a"], writes=[st_name + "b"]))
    th.append(lambda: S.op("pool", TT(rs, ms, K["mhalf"], ALU.pow), reads=[st_name + "b", "mhalf"], writes=[st_name + "c"]))

    def hb_():
        S.op("act", ACT(hb, xs_ap, AF.Copy, scale=rs), reads=[xs_name, st_name + "c"], writes=[hb_name])
        if after_hb is not None:
            after_hb()
    th.append(hb_)

    def tr_():
        b, bn = C.banks(1)
        pt = C.pbf16(b)
        S.op("pe", [TR(pt[:, k * 128:(k + 1) * 128], hb[:, k * 128:(k + 1) * 128], K["ident"]) for k in range(8)],
             reads=[hb_name, "ident"], writes=bn)
        S.op("dve", [TS(hT[:, k, col0:col0 + 128], pt[:, k * 128:(k + 1) * 128], K["gpre"][:, layer, k:k + 1], None, ALU.mult)
                     for k in range(8)],
             reads=bn + ["gpre"], writes=[hT_name])
    th.append(tr_)
    return th


def post_norm_thunks(C, K, layer, yb, ybn, st, st_name, tmp, tmp_name, xr_ap, xr_name, gpost=None, gpost_name=None):
    S = C.S
    y = C.pf32(yb, 1024)
    ss, ms, rs = st[:, 0:1], st[:, 1:2], st[:, 2:3]
    if gpost is None:
        gpost, gpost_name = K["gpost"], f"gpost{layer}"
    th = []
    th.append(lambda: S.op("act", ACT(K["junk"], y, AF.Square, accum_out=ss), reads=ybn, writes=["junk", st_name + "a"]))
    th.append(lambda: S.op("dve", TS(ms, ss, 1.0 / D, EPS, ALU.mult, ALU.add), reads=[st_name + "a"], writes=[st_name + "b"]))
    th.append(lambda: S.op("pool", TT(rs, ms, K["mhalf"], ALU.pow), reads=[st_name + "b", "mhalf"], writes=[st_name + "c"]))
    th.append(lambda: S.op("dve", STT(tmp, y, rs, gpost, ALU.mult, ALU.mult),
                           reads=ybn + [st_name + "c", gpost_name], writes=[tmp_name]))
    th.append(lambda: S.op("pool", TT(tmp, tmp, xr_ap, ALU.add), reads=[tmp_name, xr_name], writes=[tmp_name]))
    return th


def build_layer0(C, K, aps, x_in, x1_out, hoist=None, early=None):
    S = C.S
    sb = C.sb
    C.ring = list(range(8))
    C.bank_ptr = 0
    W0 = sb("W0", [128, 8, 4096], BF16)
    NXS = 2
    xs = [sb(f"l0xs{i}", [128, 1024], F32) for i in range(NXS)]
    hb = [sb(f"l0hb{i}", [128, 1024], BF16) for i in range(2)]
    hT = [sb(f"l0hT{i}", [128, 8, 256], BF16) for i in range(2)]
    NA = 5
    at = [sb(f"l0a{i}", [128, 2048], BF16) for i in range(NA)]
    mixT = sb("l0mx", [128, 16, 256], BF16)
    Wg = sb("Wg", [128, 16, 512], BF16)
    Wo = sb("Wo0", [128, 16, 1024], BF16)
    scl = sb("scl0", [128, 16], F32)
    band = sb("band", [128, 3, 4, 128], BF16)
    K["gpost"] = sb("gpost0", [128, 1024], F32)
    S.op("sp", DMA(K["gpost"], aps["gpost_bc"][:, 0, :]), writes=["gpost0"], dma_sem="c_gpost0")
    xr = [sb(f"l0xr{i}", [128, 1024], F32) for i in range(2)]
    szT = [sb(f"l0sz{i}", [128, 16, 256], BF16) for i in range(2)]
    tmp = [sb(f"l0tmp{i}", [128, 1024], F32) for i in range(2)]
    stat = sb("l0stat", [128, NT0, 2, 4], F32)

    w_in = aps["pool_w_in"].rearrange("(k p) n -> p k n", p=128)
    for n in range(4):
        for kh in range(2):
            S.op("pool", DMA(W0[:, kh * 4:(kh + 1) * 4, n * 512:(n + 1) * 512], w_in[:, kh * 4:(kh + 1) * 4, n * 512:(n + 1) * 512]),
                 writes=[f"W0a{n}_{kh}"], dma_sem=f"W0a{n}_{kh}")
    S.op("pool", DMA(band.rearrange("p a g t -> p (a g t)"), aps["band"]), writes=["band"], dma_sem="c_band")
    for n in range(4):
        for kh in range(2):
            c0 = 2048 + n * 512
            S.op("pool", DMA(W0[:, kh * 4:(kh + 1) * 4, c0:c0 + 512], w_in[:, kh * 4:(kh + 1) * 4, c0:c0 + 512]),
                 writes=[f"W0z{n}_{kh}"], dma_sem=f"W0z{n}_{kh}")
    wg_in = aps["pool_w_group"].rearrange("g (kc p) d -> p (g kc) d", p=128)
    for g in range(4):
        S.op("pool", DMA(Wg[:, g * 4:(g + 1) * 4, :], wg_in[:, g * 4:(g + 1) * 4, :]), writes=[f"Wg{g}"], dma_sem=f"Wg{g}")
    S.op("sp", DMA(scl, aps["pool_scaleT"]), writes=["scl0"], dma_sem="c_scl")
    wo_in = aps["pool_w_out"].rearrange("(k p) n -> p k n", p=128)
    for kq in range(4):
        S.op("pool", DMA(Wo[:, kq * 4:(kq + 1) * 4, :], wo_in[:, kq * 4:(kq + 1) * 4, :]), writes=[f"Wo0_{kq}"], dma_sem=f"Wo0_{kq}")
    Wo_names = [f"Wo0_{kq}" for kq in range(4)]

    def load_x(t):
        sl = t % NXS
        S.op("sp", DMA(xs[sl], x_in[t * 128:(t + 1) * 128, :]), writes=[f"l0xs{sl}"], dma_sem=f"l0xs{sl}")

    def front_tile(t, hTb, hTn, col0):
        nxt = (lambda: load_x(t + 2)) if t + 2 < NT0 else None
        return norm_transpose_thunks(C, K, 0, xs[t % NXS], f"l0xs{t % NXS}", hTb, hTn, col0,
                                     stat[:, t, 0, :], f"l0stat{t}", hb[t % 2], f"l0hb{t % 2}", after_hb=nxt)

    def proj_a_thunks(t, hTb, hTn, col0):
        a = at[t % NA]
        th = []
        for n in range(4):
            def f(n=n):
                b, bn = C.banks(1)
                pa = C.pf32(b)
                S.op("pe", [MM(pa, hTb[:, k, col0:col0 + 128], W0[:, k, n * 512:(n + 1) * 512], k == 0, k == 7) for k in range(8)],
                     reads=[hTn, f"W0a{n}_0", f"W0a{n}_1"], writes=bn)
                if n % 2 == 0:
                    S.op("act", ACT(a[:, n * 512:(n + 1) * 512], pa, AF.Copy), reads=bn, writes=[f"l0a{t % NA}_{n}"])
                else:
                    S.op("dve", CP(a[:, n * 512:(n + 1) * 512], pa), reads=bn, writes=[f"l0a{t % NA}_{n}"])
            th.append(f)
        return th

    def tiles_of(blk):
        return [1 + 2 * blk, 2 + 2 * blk]

    def stage_F(blk):
        p = blk % 2
        th = []
        for j, t in enumerate(tiles_of(blk)):
            th += front_tile(t, hT[p], f"l0hT{p}", j * 128)
        return th

    def stage_A(blk):
        p = blk % 2
        th = []
        for j, t in enumerate(tiles_of(blk)):
            th += proj_a_thunks(t, hT[p], f"l0hT{p}", j * 128)
        return th

    def stage_Z(blk):
        p = blk % 2
        th = []
        for d in range(16):
            def f(d=d):
                b, bn = C.banks(1)
                pz = C.pf32(b, 256)
                S.op("pe", [MM(pz, W0[:, k, 2048 + d * 128:2048 + (d + 1) * 128], hT[p][:, k, :], k == 0, k == 7) for k in range(8)],
                     reads=[f"l0hT{p}", f"W0z{d // 4}_0", f"W0z{d // 4}_1"], writes=bn)
                S.op("act", ACT(szT[p][:, d, :], pz, AF.Silu), reads=bn, writes=[f"l0sz{p}_{d}"])
            th.append(f)
        return th

    def stage_M(blk):
        th = []
        for j, t in enumerate(tiles_of(blk)):
            a_cur = at[t % NA]
            a_prev = at[(t - 1) % NA]
            cur_sel = 2 if t == 5 else 0
            for cq in range(4):
                def f(j=j, t=t, cq=cq, a_cur=a_cur, a_prev=a_prev, cur_sel=cur_sel):
                    b, bn = C.banks(1)
                    pm = C.pf32(b)
                    fns = []
                    for ci in range(4):
                        c = cq * 4 + ci
                        fns.append(MM(pm[:, ci * 128:(ci + 1) * 128], a_cur[:, c * 128:(c + 1) * 128], band[:, cur_sel, cq, :], True, False))
                        fns.append(MM(pm[:, ci * 128:ci * 128 + 16], a_prev[:, c * 128:(c + 1) * 128], band[:, 1, cq, 0:16], False, True))
                    S.op("pe", fns, reads=[f"l0a{t % NA}_{cq}", f"l0a{(t - 1) % NA}_{cq}", "band"], writes=bn)
                    src = pm.rearrange("p (c t) -> p c t", c=4)
                    dst = mixT[:, cq * 4:(cq + 1) * 4, j * 128:(j + 1) * 128]
                    if cq % 2 == 0:
                        S.op("dve", CP(dst, src), reads=bn, writes=[f"l0mx_{cq}_{j}"])
                    else:
                        S.op("act", ACT(dst, src, AF.Copy), reads=bn, writes=[f"l0mx_{cq}_{j}"])
                th.append(f)
        return th

    def stage_G(blk):
        p = blk % 2
        th = []
        for d in range(16):
            def f(d=d):
                g = d // 4
                b, bn = C.banks(1)
                pg = C.pf32(b, 256)
                S.op("pe", [MM(pg, Wg[:, g * 4 + kc, (d % 4) * 128:(d % 4 + 1) * 128], mixT[:, g * 4 + kc, :], kc == 0, kc == 3) for kc in range(4)],
                     reads=[f"l0mx_{g}_0", f"l0mx_{g}_1", f"Wg{g}"], writes=bn)
                S.op("dve", STT(szT[p][:, d, :], pg, scl[:, d:d + 1], szT[p][:, d, :], ALU.mult, ALU.mult),
                     reads=bn + [f"l0sz{p}_{d}", "scl0"], writes=[f"l0sz{p}_{d}"])
            th.append(f)
        return th

    out_toks = []

    def stage_Y(blk):
        p = blk % 2
        th = []
        for j, t in enumerate(tiles_of(blk)):
            hold = {}

            def mmy(j=j, t=t, hold=hold):
                S.op("sp", DMA(xr[j], x_in[t * 128:(t + 1) * 128, :]), writes=[f"l0xr{j}"], dma_sem=f"l0xr{j}")
                yb, ybn = C.banks(2)
                hold["y"] = (yb, ybn)
                for n in range(2):
                    py = C.pf32(yb + n)
                    S.op("pe", [MM(py, szT[p][:, kd, j * 128:(j + 1) * 128], Wo[:, kd, n * 512:(n + 1) * 512], kd == 0, kd == 15) for kd in range(16)],
                         reads=[f"l0sz{p}_{d}" for d in range(16)] + Wo_names, writes=[ybn[n]])
                yb, ybn = hold["y"]
                tm = tmp[t % 2]
                run(post_norm_thunks(C, K, 0, yb, ybn, stat[:, t, 1, :], f"l0stat{t}y", tm, f"l0tmp{t % 2}", xr[j], f"l0xr{j}"))
                tok = S.op("sp", DMA(x1_out[(t - 1) * 128:t * 128, :], tm), reads=[f"l0tmp{t % 2}"], writes=[f"x1d{t - 1}"],
                           dma_sem=f"l0out{t % 2}")
                out_toks.append(tok)
            th.append(mmy)
        return th

    load_x(0)
    load_x(1)
    run(front_tile(0, hT[1], "l0hT1", 0))
    run(proj_a_thunks(0, hT[1], "l0hT1", 0))
    NB = 10
    run(stage_F(0))
    run(stage_A(0))
    for blk in range(NB):
        nxt = blk + 1 < NB
        run(merge(stage_Z(blk), stage_F(blk + 1) if nxt else []))
        if not nxt and hoist is not None:
            run(hoist())
        run(merge(stage_M(blk), stage_A(blk + 1) if nxt else []))
        run(merge(stage_G(blk), stage_Y(blk - 1) if blk >= 1 else []))
    run(merge(stage_Y(NB - 1), early() if early is not None else []))
    return out_toks


def make_layer1(C, K, aps, x1_in, out):
    S = C.S
    sb = C.sb
    W1 = sb("W1", [128, 8, 4096], BF16)
    kT = sb("kT", [128, 8, 1024], BF16)
    vx = sb("vx", [128, 8, 16, 65], BF16)
    kval = sb("kval", [128, NT1], F32)
    NXS = 2
    xs = [sb(f"l1xs{i}", [128, 1024], F32) for i in range(NXS)]
    hb = sb("l1hb", [128, 1024], BF16)
    hT = [sb(f"l1hT{i}", [128, 8, 256], BF16) for i in range(2)]
    stat = sb("l1stat", [128, NT1, 2, 4], F32)
    Wo = sb("Wo1", [128, 8, 1024], BF16)
    E = sb("Etab", [128, 16, 640], BF16)
    gpost = sb("gpost1", [128, 1024], F32)
    xr = sb("l1xr", [128, 1024], F32)
    qE = [sb(f"l1qE{i}", [128, 8, 256], BF16) for i in range(2)]
    qO = [sb(f"l1qO{i}", [128, 8, 256], BF16) for i in range(2)]
    sz = [sb(f"l1sz{i}", [128, 2, 1024], BF16) for i in range(2)]
    zt = [sb(f"l1zt{i}", [128, 512], BF16) for i in range(2)]
    SKEW = 3
    NPE, NPT, NST = 2, SKEW + 2, 2
    pt_ = [sb(f"l1pt{i}", [128, 640], BF16) for i in range(NPT)]
    rden = sb("rden", [128, 4, 4], F32)
    gt = sb("l1g", [128, 1024], BF16)
    gT = sb("l1gT", [128, 8, 128], BF16)
    tmp = sb("l1tmp", [128, 1024], F32)

    w_in = aps["att_w_in"].rearrange("(k p) n -> p k n", p=128)

    def hoist():
        th = []
        for n in [2, 3, 4, 5, 0, 1, 6, 7]:
            for kh in range(2):
                th.append(lambda n=n, kh=kh: S.op(
                    "pool", DMA(W1[:, kh * 4:(kh + 1) * 4, n * 512:(n + 1) * 512], w_in[:, kh * 4:(kh + 1) * 4, n * 512:(n + 1) * 512]),
                    writes=[f"W1_{n}_{kh}"], dma_sem=f"W1_{n}_{kh}"))
        return th

    wo_in = aps["att_w_out"].rearrange("(k p) n -> p k n", p=128)
    bg = aps["biasG"]

    def prologue():
        S.op("sp", DMA(gpost, aps["gpost_bc"][:, 1, :]), writes=["gpost1"], dma_sem="c_gpost1")
        for kh in range(2):
            S.op("pool", DMA(Wo[:, kh * 4:(kh + 1) * 4, :], wo_in[:, kh * 4:(kh + 1) * 4, :]), writes=[f"Wo1_{kh}"], dma_sem=f"Wo1_{kh}")

    def etab_thunks():
        th = []
        for hq4 in range(4):
            th.append(lambda hq4=hq4: S.op("pool", DMA(E[:, hq4 * 4:(hq4 + 1) * 4, :], bg[:, hq4 * 4:(hq4 + 1) * 4, :]),
                                           writes=[f"Etab{hq4}"], dma_sem=f"Etab{hq4}"))

        def edges():
            S.op("pool", MEMSET(E[0:64, :, 64:128], -30000.0), writes=[f"Etab{q}" for q in range(4)])
            S.op("pool", MEMSET(E[64:128, :, 512:576], -30000.0), writes=[f"Etab{q}" for q in range(4)])
        th.append(edges)
        for i in range(2):
            th.append(lambda i=i: S.op("pool", MEMSET(qE[i], 0.0), writes=[f"l1qE{i}_{hp}" for hp in range(8)]))
            th.append(lambda i=i: S.op("pool", MEMSET(qO[i], 0.0), writes=[f"l1qO{i}_{hp}" for hp in range(8)]))
        return th

    Wn = lambda n: [f"W1_{n}_0", f"W1_{n}_1"]

    def load_x(u):
        sl = u % NXS
        S.op("sp", DMA(xs[sl], x1_in[u * 128:(u + 1) * 128, :]), reads=[f"x1d{u}"], writes=[f"l1xs{sl}"], dma_sem=f"l1xs{sl}")

    out_toks = []
    OBS = [7, 7]

    def finish_tile(u, j, p):
        b, bn = C.banks(1)
        ptr = C.pbf16(b)
        S.op("pe", [TR(ptr[:, k * 128:(k + 1) * 128], gt[:, k * 128:(k + 1) * 128], K["ident"]) for k in range(8)],
             reads=[f"l1g_{hg}" for hg in range(4)] + ["ident"], writes=bn)
        S.op("act", ACT(gT.rearrange("p k t -> p (k t)"), ptr, AF.Copy), reads=bn, writes=["l1gT"])
        S.op("sp", DMA(xr, x1_in[u * 128:(u + 1) * 128, :]), reads=[f"x1d{u}"], writes=["l1xr"], dma_sem="l1xr")
        yb, ybn = C.banks(2)
        for n in range(2):
            py = C.pf32(yb + n)
            S.op("pe", [MM(py, gT[:, k, :], Wo[:, k, n * 512:(n + 1) * 512], k == 0, k == 7) for k in range(8)],
                 reads=["l1gT", "Wo1_0", "Wo1_1"], writes=[ybn[n]])
        run(post_norm_thunks(C, K, 1, yb, ybn, stat[:, u, 1, :], f"l1stat{u}y", tmp, "l1tmp", xr, "l1xr", gpost, "gpost1"))
        tok = S.op("sp", DMA(out[(u - 4) * 128:(u - 3) * 128, :], tmp), reads=["l1tmp"], dma_sem="l1out")
        out_toks.append(tok)

    pending = []

    def emit_pv(unit):
        u, j, p, h, T0, pti = unit
        ptb = pt_[pti]
        hq, hg = h % 4, h // 4
        OB = OBS[hg % 2]
        obn = [f"bank{OB}"]
        po = C.pf32(OB)
        S.op("pe", [MM(po[:, hq * 65:hq * 65 + 65], ptb[:, jt * 128:(jt + 1) * 128], vx[:, (T0 + jt) % 8, h, :], jt == 0, jt == 4)
                    for jt in range(5)],
             reads=[f"l1pt{pti}"] + [f"vx{(T0 + jt) % 8}" for jt in range(5)], writes=obn)
        if hq == 3:
            pov = po[:, 0:260].rearrange("p (h c) -> p h c", h=4)
            S.op("dve", RECIP(rden[:, hg, :], pov[:, :, 64]), reads=obn, writes=[f"rden{hg}"])
            S.op("dve", [STT(gt[:, (hg * 4 + q4) * 64:(hg * 4 + q4 + 1) * 64], pov[:, q4, 0:64], rden[:, hg, q4:q4 + 1],
                             sz[p][:, j, (hg * 4 + q4) * 64:(hg * 4 + q4 + 1) * 64], ALU.mult, ALU.mult) for q4 in range(4)],
                 reads=obn + [f"rden{hg}", f"l1sz{p}_{j}"], writes=[f"l1g_{hg}"])
        if h == 15:
            pending.append([2, (u, j, p)])
        for pf in list(pending):
            pf[0] -= 1
            if pf[0] < 0:
                pending.remove(pf)
                finish_tile(*pf[1])

    def stage_F(blk):
        p = blk % 2
        th = []
        for j, u in enumerate([2 * blk, 2 * blk + 1]):
            nxt = (lambda u=u: load_x(u + 2)) if u + 2 < NT1 else None
            th += norm_transpose_thunks(C, K, 1, xs[u % NXS], f"l1xs{u % NXS}", hT[p], f"l1hT{p}", j * 128,
                                        stat[:, u, 0, :], f"l1stat{u}", hb, "l1hb", after_hb=nxt)
        return th

    def stage_P(blk):
        p = blk % 2
        own = blk >= 2
        tiles = [2 * blk, 2 * blk + 1]
        th = []
        ring0 = (tiles[0] % 8) * 128
        for hp in range(8):
            def fk(hp=hp):
                b, bn = C.banks(1)
                pk = C.pf32(b, 256)
                S.op("pe", [MM(pk, W1[:, k, 1024 + hp * 128:1024 + (hp + 1) * 128], hT[p][:, k, :], k == 0, k == 7) for k in range(8)],
                     reads=[f"l1hT{p}"] + Wn(2 + hp // 4), writes=bn)
                dst = kT[:, hp, ring0:ring0 + 256]
                wn = [f"kT{tiles[0] % 8}_{hp}", f"kT{tiles[1] % 8}_{hp}"]
                if hp % 2 == 0:
                    S.op("act", ACT(dst, pk, AF.Copy), reads=bn, writes=wn)
                else:
                    S.op("dve", CP(dst, pk), reads=bn, writes=wn)
            th.append(fk)
        if own:
            for hp in range(8):
                def fq(hp=hp):
                    b, bn = C.banks(1)
                    pq = C.pf32(b, 256)
                    S.op("pe", [MM(pq, W1[:, k, hp * 128:(hp + 1) * 128], hT[p][:, k, :], k == 0, k == 7) for k in range(8)],
                         reads=[f"l1hT{p}"] + Wn(hp // 4), writes=bn)
                    S.op("act", ACT(qE[p][0:64, hp, :], pq[0:64, :], AF.Copy), reads=bn, writes=[f"l1qE{p}_{hp}"])
                    S.op("dve", CP(qO[p][64:128, hp, :], pq[64:128, :]), reads=bn, writes=[f"l1qO{p}_{hp}"])
                th.append(fq)
        for j, u in enumerate(tiles):
            rs_ = u % 8
            for n in range(2):
                def fv(j=j, u=u, n=n, rs_=rs_):
                    b, bn = C.banks(1)
                    pv = C.pf32(b)
                    S.op("pe", [MM(pv, hT[p][:, k, j * 128:(j + 1) * 128], W1[:, k, 2048 + n * 512:2048 + (n + 1) * 512], k == 0, k == 7) for k in range(8)],
                         reads=[f"l1hT{p}"] + Wn(4 + n), writes=bn)
                    src = pv.rearrange("p (h c) -> p h c", h=8)
                    dst = vx[:, rs_, n * 8:(n + 1) * 8, 0:64]
                    if n == 0:
                        S.op("dve", CP(dst, src), reads=bn, writes=[f"vx{rs_}"])
                    else:
                        S.op("act", ACT(dst, src, AF.Copy), reads=bn, writes=[f"vx{rs_}"])
                        S.op("pool", TS(vx[:, rs_, :, 64], kval[:, u:u + 1].to_broadcast([128, 16]), 2.0, None, ALU.mult), reads=["kval"], writes=[f"vx{rs_}"])
                th.append(fv)
            if own:
                for n in range(2):
                    def fz(j=j, n=n):
                        b, bn = C.banks(1)
                        pz = C.pf32(b)
                        S.op("pe", [MM(pz, hT[p][:, k, j * 128:(j + 1) * 128], W1[:, k, 3072 + n * 512:3072 + (n + 1) * 512], k == 0, k == 7) for k in range(8)],
                             reads=[f"l1hT{p}"] + Wn(6 + n), writes=bn)
                        zi = (2 * j + n) % 2
                        S.op("act", ACT(zt[zi], pz, AF.Tanh, scale=0.5), reads=bn, writes=[f"l1zt{zi}"])
                        S.op("dve", STT(sz[p][:, j, n * 512:(n + 1) * 512], zt[zi], 1.0, pz, ALU.add, ALU.mult),
                             reads=bn + [f"l1zt{zi}"], writes=[f"l1sz{p}_{j}"])
                    th.append(fz)
        return th

    units = []
    ucount = [0]

    def stage_ATT(blk):
        p = blk % 2
        th = []
        for j, u in enumerate([2 * blk, 2 * blk + 1]):
            T0 = u - 4
            for h in range(16):
                def f(j=j, u=u, T0=T0, h=h):
                    hp = h // 2
                    qsrc = (qE if h % 2 == 0 else qO)[p]
                    qn = f"l1q{'E' if h % 2 == 0 else 'O'}{p}_{hp}"
                    n = ucount[0]
                    ucount[0] += 1
                    sl, pti = n % NST, n % NPT
                    c0 = sl * 1024
                    fns = []
                    fns.append(MM(C.psum[:, c0:c0 + 512], K["ident8"], E[:, h, 0:512], True, False))
                    fns.append(MM(C.psum[:, c0 + 512:c0 + 640], K["ident8"], E[:, h, 512:640], True, False))
                    for jt in range(5):
                        rk = ((T0 + jt) % 8) * 128
                        dst = C.psum[:, c0 + jt * 128:c0 + (jt + 1) * 128]
                        fns.append(MM(dst, kT[:, hp, rk:rk + 128], qsrc[:, hp, j * 128:(j + 1) * 128], False, jt in (3, 4)))
                    S.op("pe", fns, reads=[qn, f"Etab{h // 4}", "ident8"] + [f"kT{(T0 + jt) % 8}_{hp}" for jt in range(5)], writes=[f"sT{sl}", f"bank{2 * sl}", f"bank{2 * sl + 1}"])
                    S.op("act", ACT(pt_[pti], C.psum[:, c0:c0 + 640], AF.Exp, scale=0.125),
                         reads=[f"sT{sl}", f"bank{2 * sl}", f"bank{2 * sl + 1}"], writes=[f"l1pt{pti}"])
                    units.append((u, j, p, h, T0, pti))
                    if len(units) > SKEW:
                        emit_pv(units.pop(0))
                th.append(f)

        def flush():
            while units:
                emit_pv(units.pop(0))
            while pending:
                finish_tile(*pending.pop(0)[1])
        th.append(flush)
        return th

    def early():
        def ld():
            S.op("sp", DMA(kval, aps["kvalid"]), writes=["kval"], dma_sem="c_kval")
            load_x(0)
            load_x(1)
        return [ld] + stage_F(0) + merge(stage_P(0), stage_F(1))

    def body():
        C.ring = [4, 5, 6]
        C.bank_ptr = 0
        prologue()
        NB = 10
        et = etab_thunks()
        run(merge(stage_P(1), stage_F(2), et))
        run(merge(stage_P(2), stage_F(3)))
        for blk in range(2, NB):
            run(merge(stage_ATT(blk),
                      stage_P(blk + 1) if blk + 1 < NB else [],
                      stage_F(blk + 2) if blk + 2 < NB else []))
        return out_toks

    return hoist, body, early


def build_program(mode="fused"):
    nc = bass.Bass("TRN2", target_bir_lowering=False)
    aps = {}

    def din(name, shape, dt=F32):
        aps[name] = nc.dram_tensor(name, list(shape), dt, kind="ExternalInput").ap()

    din("ident", [128, 128], BF16)
    din("ident8", [128, 128], BF16)
    din("gpreT", [128, 2, 8])
    din("gpost_bc", [128, 2, 1024])
    if mode in ("fused", "l0"):
        din("xin", [NT0 * 128, D])
        din("pool_w_in", [D, 4096])
        din("pool_w_group", [4, 512, 512])
        din("pool_scaleT", [128, 16])
        din("pool_w_out", [2048, D])
        din("band", [128, 3 * 4 * 128])
    if mode in ("fused", "l1"):
        din("att_w_in", [D, 4096])
        din("att_w_out", [D, D])
        din("biasG", [128, 16, 640])
        din("kvalid", [128, NT1])
    if mode == "l1":
        din("x1", [NT1 * 128, D])
        x1 = aps["x1"]
    elif mode == "l0":
        x1 = nc.dram_tensor("x1", [NT1 * 128, D], F32, kind="ExternalOutput").ap()
    else:
        x1 = nc.dram_tensor("x1_scratch", [NT1 * 128, D], F32, kind="Internal").ap()
    if mode in ("fused", "l1"):
        out = nc.dram_tensor("out", [SEG, D], F32, kind="ExternalOutput").ap()

    with ExitStack() as es:
        C = Ctx(nc, es)
        K = load_consts(C, aps)
        base = C.off
        toks = []
        hoist = body = early = None
        if mode in ("fused", "l1"):
            C.phase = 1
            hoist, body, early = make_layer1(C, K, aps, x1, out)
            C.off = base
        if mode in ("fused", "l0"):
            C.phase = 0
            toks = build_layer0(C, K, aps, aps["xin"], x1, hoist=hoist, early=early)
        elif hoist is not None:
            run(hoist())
            run(early())
        if body is not None:
            toks = body()
        C.S.wait_all("sp", toks)
        C.S.emit()
    return nc


def _band_consts(seg):
    band = np.zeros((128, 3, 4, 128), np.float32)
    tp = np.arange(128)[:, None]
    t = np.arange(128)[None, :]
    for g, w in enumerate(WINDOWS):
        win = ((tp <= t) & (tp > t - w)).astype(np.float32)
        band[:, 0, g, :] = win / w - np.eye(128, dtype=np.float32)
        band[:, 1, g, :] = ((tp - 128) > (t - w)).astype(np.float32) / w
        if seg == 0:
            cnt = np.minimum(t + 1, w).astype(np.float32)
            band[:, 2, g, :] = win / cnt - np.eye(128, dtype=np.float32)
        else:
            band[:, 2, g, :] = band[:, 0, g, :]
    return band.reshape(128, -1)


def _bias_gather(rel_bias):
    k = np.arange(128)[:, None, None]
    jt = np.arange(5)[None, :, None]
    q = np.arange(128)[None, None, :]
    idx = np.clip(q - k + 512 - 128 * jt, -256, 256) + 256
    g = rel_bias[:, idx]
    return np.ascontiguousarray(g.transpose(1, 0, 2, 3).reshape(128, 16, 640))


def make_in_maps(inputs, mode="fused", x1_full=None):
    x = np.asarray(inputs["x"], np.float32)
    norm_pre = np.asarray(inputs["norm_pre"], np.float32)
    norm_post = np.asarray(inputs["norm_post"], np.float32)
    ident = np.eye(128, dtype=np.float32).astype(ml_dtypes.bfloat16)
    gpreT = np.ascontiguousarray(norm_pre.reshape(2, 8, 128).transpose(2, 0, 1))
    gpost_bc = np.ascontiguousarray(np.broadcast_to(norm_post[None], (128, 2, 1024)))
    ident8 = (8.0 * np.eye(128, dtype=np.float32)).astype(ml_dtypes.bfloat16)
    common = {"ident": ident, "ident8": ident8, "gpreT": gpreT, "gpost_bc": gpost_bc}
    if mode in ("fused", "l0"):
        common.update({
            "pool_w_in": np.ascontiguousarray(inputs["pool_w_in"][0], np.float32),
            "pool_w_group": np.ascontiguousarray(inputs["pool_w_group"][0], np.float32),
            "pool_scaleT": np.ascontiguousarray(np.asarray(inputs["pool_scale"][0], np.float32).reshape(16, 128).T),
            "pool_w_out": np.ascontiguousarray(inputs["pool_w_out"][0], np.float32),
        })
    if mode in ("fused", "l1"):
        common.update({
            "att_w_in": np.ascontiguousarray(inputs["att_w_in"][0], np.float32),
            "att_w_out": np.ascontiguousarray(inputs["att_w_out"][0], np.float32),
            "biasG": _bias_gather(np.asarray(inputs["att_rel_bias"][0], np.float32)),
        })
    maps = []
    for c in range(NCORES):
        b, seg = c // 4, c % 4
        s = seg * SEG
        m = dict(common)
        if mode in ("fused", "l0"):
            xin = np.zeros((NT0 * 128, D), np.float32)
            lo = s - (HALO + 128)
            src_lo = max(lo, 0)
            xin[src_lo - lo:] = x[b, src_lo:s + SEG]
            m["xin"] = xin
            m["band"] = _band_consts(seg)
        if mode in ("fused", "l1"):
            pos = s - HALO + np.arange(NT1 * 128)
            m["kvalid"] = np.ascontiguousarray((pos >= 0).astype(np.float32).reshape(NT1, 128).T)
        if mode == "l1":
            x1c = np.zeros((NT1 * 128, D), np.float32)
            lo = s - HALO
            src_lo = max(lo, 0)
            x1c[src_lo - lo:] = x1_full[b, src_lo:s + SEG]
            m["x1"] = x1c
        maps.append(m)
    return maps


_NC_CACHE = {}


def _get_nc(mode):
    if mode not in _NC_CACHE:
        _NC_CACHE[mode] = build_program(mode)
    return _NC_CACHE[mode]


def kernel(x, norm_pre, norm_post, pool_w_in, pool_w_group, pool_scale, pool_w_out,
           att_w_in, att_rel_bias, att_w_out):
    inputs = dict(x=x, norm_pre=norm_pre, norm_post=norm_post, pool_w_in=pool_w_in,
                  pool_w_group=pool_w_group, pool_scale=pool_scale, pool_w_out=pool_w_out,
                  att_w_in=att_w_in, att_rel_bias=att_rel_bias, att_w_out=att_w_out)
    inputs = {k: np.asarray(v) for k, v in inputs.items()}
    nc = _get_nc("fused")
    maps = make_in_maps(inputs, "fused")
    res = run_bass_kernel_spmd(nc, maps, core_ids=list(range(NCORES)))
    out = np.empty((2, SEQ, D), np.float32)
    for c in range(NCORES):
        b, seg = c // 4, c % 4
        out[b, seg * SEG:(seg + 1) * SEG] = res.results[c]["out"]
    return out
```

```python
from contextlib import ExitStack

import ml_dtypes
import numpy as np

import concourse.bass as bass
import concourse.mybir as mybir
from concourse.bass_utils import run_bass_kernel_spmd

F32 = mybir.dt.float32
BF16 = mybir.dt.bfloat16
AF = mybir.ActivationFunctionType
ALU = mybir.AluOpType

D = 1024
SEQ = 8192
NCORES = 8
SEG = 2048
HALO = 512
NT0 = 21
NT1 = 20
EPS = 1e-6
WINDOWS = (2, 4, 8, 16)


class _Buf:
    __slots__ = ("w", "rs")

    def __init__(self):
        self.w = None
        self.rs = []


class Sched:
    ENGS = ("pe", "act", "dve", "pool", "sp")

    def __init__(self, nc, es):
        self.nc = nc
        self.es = es
        self.ops = {e: [] for e in self.ENGS}
        self.sems = {}
        self.cnt = {}
        self.waited = {e: {} for e in self.ENGS}
        self.bufs = {}
        self.name_alloc = {}
        self.allocs = {}
        self.alloc_names = {}
        for e in ("pe", "act", "dve", "pool"):
            self._sem("eng_" + e)

    def _sem(self, key):
        if key not in self.sems:
            self.sems[key] = self.es.enter_context(self.nc.semaphore(key))
            self.cnt[key] = 0
        return self.sems[key]

    def B(self, name):
        b = self.bufs.get(name)
        if b is None:
            b = self.bufs[name] = _Buf()
        return b

    def _alias_deps(self, name):
        if name in self.name_alloc:
            return []
        best = None
        for an in self.allocs:
            if name.startswith(an) and (best is None or len(an) > len(best)):
                best = an
        self.name_alloc[name] = best
        if best is None:
            return []
        self.alloc_names.setdefault(best, set()).add(name)
        s0, e0, ph = self.allocs[best]
        deps = []
        for an, (s1, e1, ph1) in self.allocs.items():
            if ph1 < ph and ph1 >= 0 and s1 < e0 and s0 < e1:
                deps += [self.B(n) for n in sorted(self.alloc_names.get(an, ()))]
        return deps

    def _need(self, eng, tok, waits):
        if tok is None:
            return
        key, val = tok
        if self.waited[eng].get(key, 0) >= val:
            return
        self.waited[eng][key] = val
        waits[key] = max(waits.get(key, 0), val)

    def op(self, eng, fns, reads=(), writes=(), dma_sem=None):
        if callable(fns):
            fns = [fns]
        waits = {}
        rb = [self.B(n) for n in reads]
        wb = [self.B(n) for n in writes]
        xb = []
        for n in list(reads) + list(writes):
            xb += self._alias_deps(n)
        for b in rb:
            self._need(eng, b.w, waits)
        for b in wb + xb:
            self._need(eng, b.w, waits)
            for t in b.rs:
                self._need(eng, t, waits)
        if dma_sem is not None:
            key = "dma_" + dma_sem
            self._sem(key)
            self.cnt[key] += 16
            inc = 16
        else:
            key = "eng_" + eng
            self.cnt[key] += 1
            inc = 1
        tok = (key, self.cnt[key])
        for b in rb:
            b.rs.append(tok)
        for b in wb:
            b.w = tok
            b.rs = []
        self.ops[eng].append((sorted(waits.items()), fns, key, inc))
        return tok

    def wait_all(self, eng, toks):
        waits = {}
        for t in toks:
            self._need(eng, t, waits)
        self.ops[eng].append((sorted(waits.items()), [], None, 0))

    def emit(self):
        sems = self.sems

        def run(e, lst):
            for waits, fns, key, inc in lst:
                for k, v in waits:
                    e.wait_ge(sems[k], v)
                ins = None
                for f in fns:
                    ins = f(e)
                if fns and key is not None:
                    ins.then_inc(sems[key], inc)

        with self.nc.Block() as block:
            @block.tensor
            def _(e):
                run(e, self.ops["pe"])

            @block.scalar
            def _(e):
                run(e, self.ops["act"])

            @block.vector
            def _(e):
                run(e, self.ops["dve"])

            @block.gpsimd
            def _(e):
                run(e, self.ops["pool"])

            @block.sync
            def _(e):
                run(e, self.ops["sp"])


def MM(out, lhsT, rhs, start=True, stop=True):
    return lambda e: e.matmul(out, lhsT=lhsT, rhs=rhs, start=start, stop=stop)


def TR(out, in_, ident):
    return lambda e: e.transpose(out=out, in_=in_, identity=ident)


def ACT(out, in_, func, **kw):
    return lambda e: e.activation(out=out, in_=in_, func=func, **kw)


def CP(out, in_):
    return lambda e: e.tensor_copy(out=out, in_=in_)


def TS(out, in0, s1, s2, op0, op1=None):
    if op1 is None:
        return lambda e: e.tensor_scalar(out=out, in0=in0, scalar1=s1, scalar2=None, op0=op0)
    return lambda e: e.tensor_scalar(out=out, in0=in0, scalar1=s1, scalar2=s2, op0=op0, op1=op1)


def STT(out, in0, scalar, in1, op0, op1):
    return lambda e: e.scalar_tensor_tensor(out=out, in0=in0, scalar=scalar, in1=in1, op0=op0, op1=op1)


def TT(out, in0, in1, op):
    return lambda e: e.tensor_tensor(out=out, in0=in0, in1=in1, op=op)


def DMA(out, in_):
    return lambda e: e.dma_start(out=out, in_=in_)


def MEMSET(ap, v):
    return lambda e: e.memset(ap, v)


def RECIP(out, in_):
    return lambda e: e.reciprocal(out=out, in_=in_)


class Ctx:
    ARENA_BYTES = 207 * 1024

    def __init__(self, nc, es):
        self.nc = nc
        self.es = es
        self.S = Sched(nc, es)
        self.psum = es.enter_context(nc.psum_tensor("psum_all", [128, 4096], F32))
        self.arena = es.enter_context(nc.sbuf_tensor("arena", [128, self.ARENA_BYTES // 2], BF16))
        self.off = 0
        self.phase = -1
        self.bank_ptr = 0
        self.ring = list(range(8))

    def sb(self, name, shape, dt):
        esz = 4 if dt == F32 else 2
        n = 1
        for s in shape[1:]:
            n *= s
        nbytes = (n * esz + 63) // 64 * 64
        assert self.off + nbytes <= self.ARENA_BYTES, (name, self.off, nbytes)
        a = self.arena[:, self.off // 2:self.off // 2 + n * esz // 2]
        self.S.allocs[name] = (self.off, self.off + nbytes, self.phase)
        self.off += nbytes
        if dt != BF16:
            a = a.bitcast(dt)
        if len(shape) == 3:
            a = a.rearrange("p (a b) -> p a b", a=shape[1])
        elif len(shape) == 4:
            a = a.rearrange("p (a b c) -> p a b c", a=shape[1], b=shape[2])
        return a

    def banks(self, n=1):
        r = self.ring
        if n == 1:
            b = r[self.bank_ptr % len(r)]
            self.bank_ptr += 1
            return b, [f"bank{b}"]
        assert n == 2
        for _ in range(len(r) + 1):
            b = r[self.bank_ptr % len(r)]
            if b % 2 == 0 and r[(self.bank_ptr + 1) % len(r)] == b + 1:
                self.bank_ptr += 2
                return b, [f"bank{b}", f"bank{b + 1}"]
            self.bank_ptr += 1
        raise AssertionError("no adjacent PSUM bank pair in ring")

    def pf32(self, b, ncols=512):
        return self.psum[:, b * 512:b * 512 + ncols]

    def pbf16(self, b):
        return self.psum[:, b * 512:(b + 1) * 512].bitcast(BF16)

    def barrier(self):
        S = self.S
        toks = [(k, v) for k, v in S.cnt.items() if v > 0]
        for e in S.ENGS:
            S.wait_all(e, toks)


def load_consts(C, aps):
    S = C.S
    K = {}
    K["ident"] = C.sb("ident", [128, 128], BF16)
    S.op("sp", DMA(K["ident"], aps["ident"]), writes=["ident"], dma_sem="c_ident")
    K["ident8"] = C.sb("ident8", [128, 128], BF16)
    S.op("sp", DMA(K["ident8"], aps["ident8"]), writes=["ident8"], dma_sem="c_ident8")
    K["gpre"] = C.sb("gpre", [128, 2, 8], F32)
    S.op("sp", DMA(K["gpre"], aps["gpreT"]), writes=["gpre"], dma_sem="c_gpre")
    K["mhalf"] = C.sb("mhalf", [128, 1], F32)
    S.op("pool", MEMSET(K["mhalf"], -0.5), writes=["mhalf"])
    K["junk"] = C.sb("junk", [128, 1024], BF16)
    return K


def merge(*lists):
    lists = [l for l in lists if l]
    out = []
    pos = [0] * len(lists)
    total = sum(len(l) for l in lists)
    for _ in range(total):
        best, bi = None, -1
        for i, l in enumerate(lists):
            if pos[i] < len(l):
                frac = (pos[i] + 0.5) / len(l)
                if best is None or frac < best:
                    best, bi = frac, i
        out.append(lists[bi][pos[bi]])
        pos[bi] += 1
    return out


def run(thunks):
    for t in thunks:
        t()


def norm_transpose_thunks(C, K, layer, xs_ap, xs_name, hT, hT_name, col0, st, st_name, hb, hb_name, after_hb=None):
    S = C.S
    ss, ms, rs = st[:, 0:1], st[:, 1:2], st[:, 2:3]
    th = []
    th.append(lambda: S.op("act", ACT(K["junk"], xs_ap, AF.Square, accum_out=ss), reads=[xs_name], writes=["junk", st_name + "a"]))
    th.append(lambda: S.op("dve", TS(ms, ss, 1.0 / D, EPS, ALU.mult, ALU.add), reads=[st_name + "a"], writes=[st_name + "b"]))
    th.append(lambda: S.op("pool", TT(rs, ms, K["mhalf"], ALU.pow), reads=[st_name + "b", "mhalf"], writes=[st_name + "c"]))

    def hb_():
        S.op("act", ACT(hb, xs_ap, AF.Copy, scale=rs), reads=[xs_name, st_name + "c"], writes=[hb_name])
        if after_hb is not None:
            after_hb()
    th.append(hb_)

    def tr_():
        b, bn = C.banks(1)
        pt = C.pbf16(b)
        S.op("pe", [TR(pt[:, k * 128:(k + 1) * 128], hb[:, k * 128:(k + 1) * 128], K["ident"]) for k in range(8)],
             reads=[hb_name, "ident"], writes=bn)
        S.op("dve", [TS(hT[:, k, col0:col0 + 128], pt[:, k * 128:(k + 1) * 128], K["gpre"][:, layer, k:k + 1], None, ALU.mult)
                     for k in range(8)],
             reads=bn + ["gpre"], writes=[hT_name])
    th.append(tr_)
    return th


def post_norm_thunks(C, K, layer, yb, ybn, st, st_name, tmp, tmp_name, xr_ap, xr_name, gpost=None, gpost_name=None):
    S = C.S
    y = C.pf32(yb, 1024)
    ss, ms, rs = st[:, 0:1], st[:, 1:2], st[:, 2:3]
    if gpost is None:
        gpost, gpost_name = K["gpost"], f"gpost{layer}"
    th = []
    th.append(lambda: S.op("act", ACT(K["junk"], y, AF.Square, accum_out=ss), reads=ybn, writes=["junk", st_name + "a"]))
    th.append(lambda: S.op("dve", TS(ms, ss, 1.0 / D, EPS, ALU.mult, ALU.add), reads=[st_name + "a"], writes=[st_name + "b"]))
    th.append(lambda: S.op("pool", TT(rs, ms, K["mhalf"], ALU.pow), reads=[st_name + "b", "mhalf"], writes=[st_name + "c"]))
    th.append(lambda: S.op("dve", STT(tmp, y, rs, gpost, ALU.mult, ALU.mult),
                           reads=ybn + [st_name + "c", gpost_name], writes=[tmp_name]))
    th.append(lambda: S.op("pool", TT(tmp, tmp, xr_ap, ALU.add), reads=[tmp_name, xr_name], writes=[tmp_name]))
    return th


def build_layer0(C, K, aps, x_in, x1_out, hoist=None, early=None):
    S = C.S
    sb = C.sb
    C.ring = list(range(8))
    C.bank_ptr = 0
    W0 = sb("W0", [128, 8, 4096], BF16)
    Wg = sb("Wg", [128, 16, 512], BF16)
    Wo = sb("Wo0", [128, 16, 1024], BF16)
    scl = sb("scl0", [128, 16], F32)
    band = sb("band", [128, 3, 4, 128], BF16)
    K["gpost"] = sb("gpost0", [128, 1024], F32)
    S.op("sp", DMA(K["gpost"], aps["gpost_bc"][:, 0, :]), writes=["gpost0"], dma_sem="c_gpost0")
    NXS = 2
    xs = [sb(f"l0xs{i}", [128, 1024], F32) for i in range(NXS)]
    xr = [sb(f"l0xr{i}", [128, 1024], F32) for i in range(2)]
    hb = [sb(f"l0hb{i}", [128, 1024], BF16) for i in range(2)]
    hT = [sb(f"l0hT{i}", [128, 8, 256], BF16) for i in range(2)]
    NA = 5
    at = [sb(f"l0a{i}", [128, 2048], BF16) for i in range(NA)]
    szT = [sb(f"l0sz{i}", [128, 16, 256], BF16) for i in range(2)]
    mixT = sb("l0mx", [128, 16, 256], BF16)
    tmp = [sb(f"l0tmp{i}", [128, 1024], F32) for i in range(2)]
    stat = sb("l0stat", [128, NT0, 2, 4], F32)

    w_in = aps["pool_w_in"].rearrange("(k p) n -> p k n", p=128)
    for n in range(4):
        for kh in range(2):
            S.op("pool", DMA(W0[:, kh * 4:(kh + 1) * 4, n * 512:(n + 1) * 512], w_in[:, kh * 4:(kh + 1) * 4, n * 512:(n + 1) * 512]),
                 writes=[f"W0a{n}_{kh}"], dma_sem=f"W0a{n}_{kh}")
    S.op("pool", DMA(band.rearrange("p a g t -> p (a g t)"), aps["band"]), writes=["band"], dma_sem="c_band")
    for n in range(4):
        for kh in range(2):
            c0 = 2048 + n * 512
            S.op("pool", DMA(W0[:, kh * 4:(kh + 1) * 4, c0:c0 + 512], w_in[:, kh * 4:(kh + 1) * 4, c0:c0 + 512]),
                 writes=[f"W0z{n}_{kh}"], dma_sem=f"W0z{n}_{kh}")
    wg_in = aps["pool_w_group"].rearrange("g (kc p) d -> p (g kc) d", p=128)
    for g in range(4):
        S.op("pool", DMA(Wg[:, g * 4:(g + 1) * 4, :], wg_in[:, g * 4:(g + 1) * 4, :]), writes=[f"Wg{g}"], dma_sem=f"Wg{g}")
    S.op("sp", DMA(scl, aps["pool_scaleT"]), writes=["scl0"], dma_sem="c_scl")
    wo_in = aps["pool_w_out"].rearrange("(k p) n -> p k n", p=128)
    for kq in range(4):
        S.op("pool", DMA(Wo[:, kq * 4:(kq + 1) * 4, :], wo_in[:, kq * 4:(kq + 1) * 4, :]), writes=[f"Wo0_{kq}"], dma_sem=f"Wo0_{kq}")
    Wo_names = [f"Wo0_{kq}" for kq in range(4)]

    def load_x(t):
        sl = t % NXS
        S.op("sp", DMA(xs[sl], x_in[t * 128:(t + 1) * 128, :]), writes=[f"l0xs{sl}"], dma_sem=f"l0xs{sl}")

    def front_tile(t, hTb, hTn, col0):
        nxt = (lambda: load_x(t + 2)) if t + 2 < NT0 else None
        return norm_transpose_thunks(C, K, 0, xs[t % NXS], f"l0xs{t % NXS}", hTb, hTn, col0,
                                     stat[:, t, 0, :], f"l0stat{t}", hb[t % 2], f"l0hb{t % 2}", after_hb=nxt)

    def proj_a_thunks(t, hTb, hTn, col0):
        a = at[t % NA]
        th = []
        for n in range(4):
            def f(n=n):
                b, bn = C.banks(1)
                pa = C.pf32(b)
                S.op("pe", [MM(pa, hTb[:, k, col0:col0 + 128], W0[:, k, n * 512:(n + 1) * 512], k == 0, k == 7) for k in range(8)],
                     reads=[hTn, f"W0a{n}_0", f"W0a{n}_1"], writes=bn)
                if n % 2 == 0:
                    S.op("act", ACT(a[:, n * 512:(n + 1) * 512], pa, AF.Copy), reads=bn, writes=[f"l0a{t % NA}_{n}"])
                else:
                    S.op("dve", CP(a[:, n * 512:(n + 1) * 512], pa), reads=bn, writes=[f"l0a{t % NA}_{n}"])
            th.append(f)
        return th

    def tiles_of(blk):
        return [1 + 2 * blk, 2 + 2 * blk]

    def stage_F(blk):
        p = blk % 2
        th = []
        for j, t in enumerate(tiles_of(blk)):
            th += front_tile(t, hT[p], f"l0hT{p}", j * 128)
        return th

    def stage_A(blk):
        p = blk % 2
        th = []
        for j, t in enumerate(tiles_of(blk)):
            th += proj_a_thunks(t, hT[p], f"l0hT{p}", j * 128)
        return th

    def stage_Z(blk):
        p = blk % 2
        th = []
        for d in range(16):
            def f(d=d):
                b, bn = C.banks(1)
                pz = C.pf32(b, 256)
                S.op("pe", [MM(pz, W0[:, k, 2048 + d * 128:2048 + (d + 1) * 128], hT[p][:, k, :], k == 0, k == 7) for k in range(8)],
                     reads=[f"l0hT{p}", f"W0z{d // 4}_0", f"W0z{d // 4}_1"], writes=bn)
                S.op("act", ACT(szT[p][:, d, :], pz, AF.Silu), reads=bn, writes=[f"l0sz{p}_{d}"])
            th.append(f)
        return th

    def stage_M(blk):
        th = []
        for j, t in enumerate(tiles_of(blk)):
            a_cur = at[t % NA]
            a_prev = at[(t - 1) % NA]
            cur_sel = 2 if t == 5 else 0
            for cq in range(4):
                def f(j=j, t=t, cq=cq, a_cur=a_cur, a_prev=a_prev, cur_sel=cur_sel):
                    b, bn = C.banks(1)
                    pm = C.pf32(b)
                    fns = []
                    for ci in range(4):
                        c = cq * 4 + ci
                        fns.append(MM(pm[:, ci * 128:(ci + 1) * 128], a_cur[:, c * 128:(c + 1) * 128], band[:, cur_sel, cq, :], True, False))
                        fns.append(MM(pm[:, ci * 128:ci * 128 + 16], a_prev[:, c * 128:(c + 1) * 128], band[:, 1, cq, 0:16], False, True))
                    S.op("pe", fns, reads=[f"l0a{t % NA}_{cq}", f"l0a{(t - 1) % NA}_{cq}", "band"], writes=bn)
                    src = pm.rearrange("p (c t) -> p c t", c=4)
                    dst = mixT[:, cq * 4:(cq + 1) * 4, j * 128:(j + 1) * 128]
                    if cq % 2 == 0:
                        S.op("dve", CP(dst, src), reads=bn, writes=[f"l0mx_{cq}_{j}"])
                    else:
                        S.op("act", ACT(dst, src, AF.Copy), reads=bn, writes=[f"l0mx_{cq}_{j}"])
                th.append(f)
        return th

    def stage_G(blk):
        p = blk % 2
        th = []
        for d in range(16):
            def f(d=d):
                g = d // 4
                b, bn = C.banks(1)
                pg = C.pf32(b, 256)
                S.op("pe", [MM(pg, Wg[:, g * 4 + kc, (d % 4) * 128:(d % 4 + 1) * 128], mixT[:, g * 4 + kc, :], kc == 0, kc == 3) for kc in range(4)],
                     reads=[f"l0mx_{g}_0", f"l0mx_{g}_1", f"Wg{g}"], writes=bn)
                S.op("dve", STT(szT[p][:, d, :], pg, scl[:, d:d + 1], szT[p][:, d, :], ALU.mult, ALU.mult),
                     reads=bn + [f"l0sz{p}_{d}", "scl0"], writes=[f"l0sz{p}_{d}"])
            th.append(f)
        return th

    out_toks = []

    def stage_Y(blk):
        p = blk % 2
        th = []
        for j, t in enumerate(tiles_of(blk)):
            hold = {}

            def mmy(j=j, t=t, hold=hold):
                S.op("sp", DMA(xr[j], x_in[t * 128:(t + 1) * 128, :]), writes=[f"l0xr{j}"], dma_sem=f"l0xr{j}")
                yb, ybn = C.banks(2)
                hold["y"] = (yb, ybn)
                for n in range(2):
                    py = C.pf32(yb + n)
                    S.op("pe", [MM(py, szT[p][:, kd, j * 128:(j + 1) * 128], Wo[:, kd, n * 512:(n + 1) * 512], kd == 0, kd == 15) for kd in range(16)],
                         reads=[f"l0sz{p}_{d}" for d in range(16)] + Wo_names, writes=[ybn[n]])
                yb, ybn = hold["y"]
                tm = tmp[t % 2]
                run(post_norm_thunks(C, K, 0, yb, ybn, stat[:, t, 1, :], f"l0stat{t}y", tm, f"l0tmp{t % 2}", xr[j], f"l0xr{j}"))
                tok = S.op("sp", DMA(x1_out[(t - 1) * 128:t * 128, :], tm), reads=[f"l0tmp{t % 2}"], writes=[f"x1d{t - 1}"],
                           dma_sem=f"l0out{t % 2}")
                out_toks.append(tok)
            th.append(mmy)
        return th

    load_x(0)
    load_x(1)
    run(front_tile(0, hT[1], "l0hT1", 0))
    run(proj_a_thunks(0, hT[1], "l0hT1", 0))
    NB = 10
    run(stage_F(0))
    run(stage_A(0))
    for blk in range(NB):
        nxt = blk + 1 < NB
        run(merge(stage_Z(blk), stage_F(blk + 1) if nxt else []))
        if not nxt and hoist is not None:
            run(hoist())
        run(merge(stage_M(blk), stage_A(blk + 1) if nxt else []))
        run(merge(stage_G(blk), stage_Y(blk - 1) if blk >= 1 else []))
    run(merge(stage_Y(NB - 1), early() if early is not None else []))
    return out_toks


def make_layer1(C, K, aps, x1_in, out):
    S = C.S
    sb = C.sb
    W1 = sb("W1", [128, 8, 4096], BF16)
    Wo = sb("Wo1", [128, 8, 1024], BF16)
    E = sb("Etab", [128, 16, 640], BF16)
    kT = sb("kT", [128, 8, 1024], BF16)
    vx = sb("vx", [128, 8, 16, 65], BF16)
    kval = sb("kval", [128, NT1], F32)
    gpost = sb("gpost1", [128, 1024], F32)
    NXS = 2
    xs = [sb(f"l1xs{i}", [128, 1024], F32) for i in range(NXS)]
    xr = sb("l1xr", [128, 1024], F32)
    hb = sb("l1hb", [128, 1024], BF16)
    hT = [sb(f"l1hT{i}", [128, 8, 256], BF16) for i in range(2)]
    stat = sb("l1stat", [128, NT1, 2, 4], F32)
    qE = [sb(f"l1qE{i}", [128, 8, 256], BF16) for i in range(2)]
    qO = [sb(f"l1qO{i}", [128, 8, 256], BF16) for i in range(2)]
    sz = [sb(f"l1sz{i}", [128, 2, 1024], BF16) for i in range(2)]
    zt = [sb(f"l1zt{i}", [128, 512], BF16) for i in range(2)]
    SKEW = 3
    NPE, NPT, NST = 2, SKEW + 2, 2
    pt_ = [sb(f"l1pt{i}", [128, 640], BF16) for i in range(NPT)]
    rden = sb("rden", [128, 4, 4], F32)
    gt = sb("l1g", [128, 1024], BF16)
    gT = sb("l1gT", [128, 8, 128], BF16)
    tmp = sb("l1tmp", [128, 1024], F32)

    w_in = aps["att_w_in"].rearrange("(k p) n -> p k n", p=128)

    def hoist():
        th = []
        for n in [2, 3, 4, 5, 0, 1, 6, 7]:
            for kh in range(2):
                th.append(lambda n=n, kh=kh: S.op(
                    "pool", DMA(W1[:, kh * 4:(kh + 1) * 4, n * 512:(n + 1) * 512], w_in[:, kh * 4:(kh + 1) * 4, n * 512:(n + 1) * 512]),
                    writes=[f"W1_{n}_{kh}"], dma_sem=f"W1_{n}_{kh}"))
        return th

    wo_in = aps["att_w_out"].rearrange("(k p) n -> p k n", p=128)
    bg = aps["biasG"]

    def prologue():
        S.op("sp", DMA(gpost, aps["gpost_bc"][:, 1, :]), writes=["gpost1"], dma_sem="c_gpost1")
        for kh in range(2):
            S.op("pool", DMA(Wo[:, kh * 4:(kh + 1) * 4, :], wo_in[:, kh * 4:(kh + 1) * 4, :]), writes=[f"Wo1_{kh}"], dma_sem=f"Wo1_{kh}")
        S.op("sp", DMA(kval, aps["kvalid"]), writes=["kval"], dma_sem="c_kval")

    def etab_thunks():
        th = []
        for hq4 in range(4):
            th.append(lambda hq4=hq4: S.op("pool", DMA(E[:, hq4 * 4:(hq4 + 1) * 4, :], bg[:, hq4 * 4:(hq4 + 1) * 4, :]),
                                           writes=[f"Etab{hq4}"], dma_sem=f"Etab{hq4}"))

        def edges():
            S.op("pool", MEMSET(E[0:64, :, 64:128], -30000.0), writes=[f"Etab{q}" for q in range(4)])
            S.op("pool", MEMSET(E[64:128, :, 512:576], -30000.0), writes=[f"Etab{q}" for q in range(4)])
        th.append(edges)
        for i in range(2):
            th.append(lambda i=i: S.op("pool", MEMSET(qE[i], 0.0), writes=[f"l1qE{i}_{hp}" for hp in range(8)]))
            th.append(lambda i=i: S.op("pool", MEMSET(qO[i], 0.0), writes=[f"l1qO{i}_{hp}" for hp in range(8)]))
        return th

    Wn = lambda n: [f"W1_{n}_0", f"W1_{n}_1"]

    def load_x(u):
        sl = u % NXS
        S.op("sp", DMA(xs[sl], x1_in[u * 128:(u + 1) * 128, :]), reads=[f"x1d{u}"], writes=[f"l1xs{sl}"], dma_sem=f"l1xs{sl}")

    out_toks = []
    OBS = [7, 7]

    def finish_tile(u, j, p):
        b, bn = C.banks(1)
        ptr = C.pbf16(b)
        S.op("pe", [TR(ptr[:, k * 128:(k + 1) * 128], gt[:, k * 128:(k + 1) * 128], K["ident"]) for k in range(8)],
             reads=[f"l1g_{hg}" for hg in range(4)] + ["ident"], writes=bn)
        S.op("act", ACT(gT.rearrange("p k t -> p (k t)"), ptr, AF.Copy), reads=bn, writes=["l1gT"])
        S.op("sp", DMA(xr, x1_in[u * 128:(u + 1) * 128, :]), reads=[f"x1d{u}"], writes=["l1xr"], dma_sem="l1xr")
        yb, ybn = C.banks(2)
        for n in range(2):
            py = C.pf32(yb + n)
            S.op("pe", [MM(py, gT[:, k, :], Wo[:, k, n * 512:(n + 1) * 512], k == 0, k == 7) for k in range(8)],
                 reads=["l1gT", "Wo1_0", "Wo1_1"], writes=[ybn[n]])
        run(post_norm_thunks(C, K, 1, yb, ybn, stat[:, u, 1, :], f"l1stat{u}y", tmp, "l1tmp", xr, "l1xr", gpost, "gpost1"))
        tok = S.op("sp", DMA(out[(u - 4) * 128:(u - 3) * 128, :], tmp), reads=["l1tmp"], dma_sem="l1out")
        out_toks.append(tok)

    pending = []

    def emit_pv(unit):
        u, j, p, h, T0, pti = unit
        ptb = pt_[pti]
        hq, hg = h % 4, h // 4
        OB = OBS[hg % 2]
        obn = [f"bank{OB}"]
        po = C.pf32(OB)
        S.op("pe", [MM(po[:, hq * 65:hq * 65 + 65], ptb[:, jt * 128:(jt + 1) * 128], vx[:, (T0 + jt) % 8, h, :], jt == 0, jt == 4)
                    for jt in range(5)],
             reads=[f"l1pt{pti}"] + [f"vx{(T0 + jt) % 8}" for jt in range(5)], writes=obn)
        if hq == 3:
            pov = po[:, 0:260].rearrange("p (h c) -> p h c", h=4)
            S.op("dve", RECIP(rden[:, hg, :], pov[:, :, 64]), reads=obn, writes=[f"rden{hg}"])
            S.op("dve", [STT(gt[:, (hg * 4 + q4) * 64:(hg * 4 + q4 + 1) * 64], pov[:, q4, 0:64], rden[:, hg, q4:q4 + 1],
                             sz[p][:, j, (hg * 4 + q4) * 64:(hg * 4 + q4 + 1) * 64], ALU.mult, ALU.mult) for q4 in range(4)],
                 reads=obn + [f"rden{hg}", f"l1sz{p}_{j}"], writes=[f"l1g_{hg}"])
        if h == 15:
            pending.append([2, (u, j, p)])
        for pf in list(pending):
            pf[0] -= 1
            if pf[0] < 0:
                pending.remove(pf)
                finish_tile(*pf[1])

    def stage_F(blk):
        p = blk % 2
        th = []
        for j, u in enumerate([2 * blk, 2 * blk + 1]):
            nxt = (lambda u=u: load_x(u + 2)) if u + 2 < NT1 else None
            th += norm_transpose_thunks(C, K, 1, xs[u % NXS], f"l1xs{u % NXS}", hT[p], f"l1hT{p}", j * 128,
                                        stat[:, u, 0, :], f"l1stat{u}", hb, "l1hb", after_hb=nxt)
        return th

    def stage_P(blk):
        p = blk % 2
        own = blk >= 2
        tiles = [2 * blk, 2 * blk + 1]
        th = []
        ring0 = (tiles[0] % 8) * 128
        for hp in range(8):
            def fk(hp=hp):
                b, bn = C.banks(1)
                pk = C.pf32(b, 256)
                S.op("pe", [MM(pk, W1[:, k, 1024 + hp * 128:1024 + (hp + 1) * 128], hT[p][:, k, :], k == 0, k == 7) for k in range(8)],
                     reads=[f"l1hT{p}"] + Wn(2 + hp // 4), writes=bn)
                dst = kT[:, hp, ring0:ring0 + 256]
                wn = [f"kT{tiles[0] % 8}_{hp}", f"kT{tiles[1] % 8}_{hp}"]
                if hp % 2 == 0:
                    S.op("act", ACT(dst, pk, AF.Copy), reads=bn, writes=wn)
                else:
                    S.op("dve", CP(dst, pk), reads=bn, writes=wn)
            th.append(fk)
        if own:
            for hp in range(8):
                def fq(hp=hp):
                    b, bn = C.banks(1)
                    pq = C.pf32(b, 256)
                    S.op("pe", [MM(pq, W1[:, k, hp * 128:(hp + 1) * 128], hT[p][:, k, :], k == 0, k == 7) for k in range(8)],
                         reads=[f"l1hT{p}"] + Wn(hp // 4), writes=bn)
                    S.op("dve", CP(qE[p][0:64, hp, :], pq[0:64, :]), reads=bn, writes=[f"l1qE{p}_{hp}"])
                    S.op("dve", CP(qO[p][64:128, hp, :], pq[64:128, :]), reads=bn, writes=[f"l1qO{p}_{hp}"])
                th.append(fq)
        for j, u in enumerate(tiles):
            rs_ = u % 8
            for n in range(2):
                def fv(j=j, u=u, n=n, rs_=rs_):
                    b, bn = C.banks(1)
                    pv = C.pf32(b)
                    S.op("pe", [MM(pv, hT[p][:, k, j * 128:(j + 1) * 128], W1[:, k, 2048 + n * 512:2048 + (n + 1) * 512], k == 0, k == 7) for k in range(8)],
                         reads=[f"l1hT{p}"] + Wn(4 + n), writes=bn)
                    src = pv.rearrange("p (h c) -> p h c", h=8)
                    dst = vx[:, rs_, n * 8:(n + 1) * 8, 0:64]
                    if n == 0:
                        S.op("dve", CP(dst, src), reads=bn, writes=[f"vx{rs_}"])
                    else:
                        S.op("act", ACT(dst, src, AF.Copy), reads=bn, writes=[f"vx{rs_}"])
                        S.op("pool", TS(vx[:, rs_, :, 64], kval[:, u:u + 1].to_broadcast([128, 16]), 2.0, None, ALU.mult), reads=["kval"], writes=[f"vx{rs_}"])
                th.append(fv)
            if own:
                for n in range(2):
                    def fz(j=j, n=n):
                        b, bn = C.banks(1)
                        pz = C.pf32(b)
                        S.op("pe", [MM(pz, hT[p][:, k, j * 128:(j + 1) * 128], W1[:, k, 3072 + n * 512:3072 + (n + 1) * 512], k == 0, k == 7) for k in range(8)],
                             reads=[f"l1hT{p}"] + Wn(6 + n), writes=bn)
                        zi = (2 * j + n) % 2
                        S.op("act", ACT(zt[zi], pz, AF.Tanh, scale=0.5), reads=bn, writes=[f"l1zt{zi}"])
                        S.op("dve", STT(sz[p][:, j, n * 512:(n + 1) * 512], zt[zi], 1.0, pz, ALU.add, ALU.mult),
                             reads=bn + [f"l1zt{zi}"], writes=[f"l1sz{p}_{j}"])
                    th.append(fz)
        return th

    units = []
    ucount = [0]

    def stage_ATT(blk):
        p = blk % 2
        th = []
        for j, u in enumerate([2 * blk, 2 * blk + 1]):
            T0 = u - 4
            for h in range(16):
                def f(j=j, u=u, T0=T0, h=h):
                    hp = h // 2
                    qsrc = (qE if h % 2 == 0 else qO)[p]
                    qn = f"l1q{'E' if h % 2 == 0 else 'O'}{p}_{hp}"
                    n = ucount[0]
                    ucount[0] += 1
                    sl, pti = n % NST, n % NPT
                    c0 = sl * 1024
                    fns = []
                    fns.append(MM(C.psum[:, c0:c0 + 512], K["ident8"], E[:, h, 0:512], True, False))
                    fns.append(MM(C.psum[:, c0 + 512:c0 + 640], K["ident8"], E[:, h, 512:640], True, False))
                    for jt in range(5):
                        rk = ((T0 + jt) % 8) * 128
                        dst = C.psum[:, c0 + jt * 128:c0 + (jt + 1) * 128]
                        fns.append(MM(dst, kT[:, hp, rk:rk + 128], qsrc[:, hp, j * 128:(j + 1) * 128], False, jt in (3, 4)))
                    S.op("pe", fns, reads=[qn, f"Etab{h // 4}", "ident8"] + [f"kT{(T0 + jt) % 8}_{hp}" for jt in range(5)], writes=[f"sT{sl}", f"bank{2 * sl}", f"bank{2 * sl + 1}"])
                    S.op("act", ACT(pt_[pti], C.psum[:, c0:c0 + 640], AF.Exp, scale=0.125),
                         reads=[f"sT{sl}", f"bank{2 * sl}", f"bank{2 * sl + 1}"], writes=[f"l1pt{pti}"])
                    units.append((u, j, p, h, T0, pti))
                    if len(units) > SKEW:
                        emit_pv(units.pop(0))
                th.append(f)

        def flush():
            while units:
                emit_pv(units.pop(0))
            while pending:
                finish_tile(*pending.pop(0)[1])
        th.append(flush)
        return th

    def early():
        def ld():
            load_x(0)
            load_x(1)
        return [ld] + stage_F(0)

    def body():
        C.ring = [4, 5, 6]
        C.bank_ptr = 0
        prologue()
        NB = 10
        et = etab_thunks()
        run(merge(stage_P(0), stage_F(1), et[:4]))
        run(merge(stage_P(1), stage_F(2), et[4:]))
        run(merge(stage_P(2), stage_F(3)))
        for blk in range(2, NB):
            run(merge(stage_ATT(blk),
                      stage_P(blk + 1) if blk + 1 < NB else [],
                      stage_F(blk + 2) if blk + 2 < NB else []))
        return out_toks

    return hoist, body, early


def build_program(mode="fused"):
    nc = bass.Bass("TRN2", target_bir_lowering=False)
    aps = {}

    def din(name, shape, dt=F32):
        aps[name] = nc.dram_tensor(name, list(shape), dt, kind="ExternalInput").ap()

    din("ident", [128, 128], BF16)
    din("ident8", [128, 128], BF16)
    din("gpreT", [128, 2, 8])
    din("gpost_bc", [128, 2, 1024])
    if mode in ("fused", "l0"):
        din("xin", [NT0 * 128, D])
        din("pool_w_in", [D, 4096])
        din("pool_w_group", [4, 512, 512])
        din("pool_scaleT", [128, 16])
        din("pool_w_out", [2048, D])
        din("band", [128, 3 * 4 * 128])
    if mode in ("fused", "l1"):
        din("att_w_in", [D, 4096])
        din("att_w_out", [D, D])
        din("biasG", [128, 16, 640])
        din("kvalid", [128, NT1])
    if mode == "l1":
        din("x1", [NT1 * 128, D])
        x1 = aps["x1"]
    elif mode == "l0":
        x1 = nc.dram_tensor("x1", [NT1 * 128, D], F32, kind="ExternalOutput").ap()
    else:
        x1 = nc.dram_tensor("x1_scratch", [NT1 * 128, D], F32, kind="Internal").ap()
    if mode in ("fused", "l1"):
        out = nc.dram_tensor("out", [SEG, D], F32, kind="ExternalOutput").ap()

    with ExitStack() as es:
        C = Ctx(nc, es)
        K = load_consts(C, aps)
        base = C.off
        toks = []
        hoist = body = early = None
        if mode in ("fused", "l1"):
            C.phase = 1
            hoist, body, early = make_layer1(C, K, aps, x1, out)
            C.off = base
        if mode in ("fused", "l0"):
            C.phase = 0
            toks = build_layer0(C, K, aps, aps["xin"], x1, hoist=hoist, early=early)
        elif hoist is not None:
            run(hoist())
            run(early())
        if body is not None:
            toks = body()
        C.S.wait_all("sp", toks)
        C.S.emit()
    return nc


def _band_consts(seg):
    band = np.zeros((128, 3, 4, 128), np.float32)
    tp = np.arange(128)[:, None]
    t = np.arange(128)[None, :]
    for g, w in enumerate(WINDOWS):
        win = ((tp <= t) & (tp > t - w)).astype(np.float32)
        band[:, 0, g, :] = win / w - np.eye(128, dtype=np.float32)
        band[:, 1, g, :] = ((tp - 128) > (t - w)).astype(np.float32) / w
        if seg == 0:
            cnt = np.minimum(t + 1, w).astype(np.float32)
            band[:, 2, g, :] = win / cnt - np.eye(128, dtype=np.float32)
        else:
            band[:, 2, g, :] = band[:, 0, g, :]
    return band.reshape(128, -1)


def _bias_gather(rel_bias):
    k = np.arange(128)[:, None, None]
    jt = np.arange(5)[None, :, None]
    q = np.arange(128)[None, None, :]
    idx = np.clip(q - k + 512 - 128 * jt, -256, 256) + 256
    g = rel_bias[:, idx]
    return np.ascontiguousarray(g.transpose(1, 0, 2, 3).reshape(128, 16, 640))


def make_in_maps(inputs, mode="fused", x1_full=None):
    x = np.asarray(inputs["x"], np.float32)
    norm_pre = np.asarray(inputs["norm_pre"], np.float32)
    norm_post = np.asarray(inputs["norm_post"], np.float32)
    ident = np.eye(128, dtype=np.float32).astype(ml_dtypes.bfloat16)
    gpreT = np.ascontiguousarray(norm_pre.reshape(2, 8, 128).transpose(2, 0, 1))
    gpost_bc = np.ascontiguousarray(np.broadcast_to(norm_post[None], (128, 2, 1024)))
    ident8 = (8.0 * np.eye(128, dtype=np.float32)).astype(ml_dtypes.bfloat16)
    common = {"ident": ident, "ident8": ident8, "gpreT": gpreT, "gpost_bc": gpost_bc}
    if mode in ("fused", "l0"):
        common.update({
            "pool_w_in": np.ascontiguousarray(inputs["pool_w_in"][0], np.float32),
            "pool_w_group": np.ascontiguousarray(inputs["pool_w_group"][0], np.float32),
            "pool_scaleT": np.ascontiguousarray(np.asarray(inputs["pool_scale"][0], np.float32).reshape(16, 128).T),
            "pool_w_out": np.ascontiguousarray(inputs["pool_w_out"][0], np.float32),
        })
    if mode in ("fused", "l1"):
        common.update({
            "att_w_in": np.ascontiguousarray(inputs["att_w_in"][0], np.float32),
            "att_w_out": np.ascontiguousarray(inputs["att_w_out"][0], np.float32),
            "biasG": _bias_gather(np.asarray(inputs["att_rel_bias"][0], np.float32)),
        })
    maps = []
    for c in range(NCORES):
        b, seg = c // 4, c % 4
        s = seg * SEG
        m = dict(common)
        if mode in ("fused", "l0"):
            xin = np.zeros((NT0 * 128, D), np.float32)
            lo = s - (HALO + 128)
            src_lo = max(lo, 0)
            xin[src_lo - lo:] = x[b, src_lo:s + SEG]
            m["xin"] = xin
            m["band"] = _band_consts(seg)
        if mode in ("fused", "l1"):
            pos = s - HALO + np.arange(NT1 * 128)
            m["kvalid"] = np.ascontiguousarray((pos >= 0).astype(np.float32).reshape(NT1, 128).T)
        if mode == "l1":
            x1c = np.zeros((NT1 * 128, D), np.float32)
            lo = s - HALO
            src_lo = max(lo, 0)
            x1c[src_lo - lo:] = x1_full[b, src_lo:s + SEG]
            m["x1"] = x1c
        maps.append(m)
    return maps


_NC_CACHE = {}


def _get_nc(mode):
    if mode not in _NC_CACHE:
        _NC_CACHE[mode] = build_program(mode)
    return _NC_CACHE[mode]


def kernel(x, norm_pre, norm_post, pool_w_in, pool_w_group, pool_scale, pool_w_out,
           att_w_in, att_rel_bias, att_w_out):
    inputs = dict(x=x, norm_pre=norm_pre, norm_post=norm_post, pool_w_in=pool_w_in,
                  pool_w_group=pool_w_group, pool_scale=pool_scale, pool_w_out=pool_w_out,
                  att_w_in=att_w_in, att_rel_bias=att_rel_bias, att_w_out=att_w_out)
    inputs = {k: np.asarray(v) for k, v in inputs.items()}
    nc = _get_nc("fused")
    maps = make_in_maps(inputs, "fused")
    res = run_bass_kernel_spmd(nc, maps, core_ids=list(range(NCORES)))
    out = np.empty((2, SEQ, D), np.float32)
    for c in range(NCORES):
        b, seg = c // 4, c % 4
        out[b, seg * SEG:(seg + 1) * SEG] = res.results[c]["out"]
    return out
```

```python
from contextlib import ExitStack

import ml_dtypes
import numpy as np

import concourse.bass as bass
import concourse.mybir as mybir
from concourse.bass_utils import run_bass_kernel_spmd

F32 = mybir.dt.float32
BF16 = mybir.dt.bfloat16
AF = mybir.ActivationFunctionType
ALU = mybir.AluOpType

D = 1024
SEQ = 8192
NCORES = 8
SEG = 2048
HALO = 512
NT0 = 21
NT1 = 20
EPS = 1e-6
WINDOWS = (2, 4, 8, 16)


class _Buf:
    __slots__ = ("w", "rs")

    def __init__(self):
        self.w = None
        self.rs = []


class Sched:
    ENGS = ("pe", "act", "dve", "pool", "sp")

    def __init__(self, nc, es):
        self.nc = nc
        self.es = es
        self.ops = {e: [] for e in self.ENGS}
        self.sems = {}
        self.cnt = {}
        self.waited = {e: {} for e in self.ENGS}
        self.bufs = {}
        self.name_alloc = {}
        self.allocs = {}
        self.alloc_names = {}
        for e in ("pe", "act", "dve", "pool"):
            self._sem("eng_" + e)

    def _sem(self, key):
        if key not in self.sems:
            self.sems[key] = self.es.enter_context(self.nc.semaphore(key))
            self.cnt[key] = 0
        return self.sems[key]

    def B(self, name):
        b = self.bufs.get(name)
        if b is None:
            b = self.bufs[name] = _Buf()
        return b

    def _alias_deps(self, name):
        if name in self.name_alloc:
            return []
        best = None
        for an in self.allocs:
            if name.startswith(an) and (best is None or len(an) > len(best)):
                best = an
        self.name_alloc[name] = best
        if best is None:
            return []
        self.alloc_names.setdefault(best, set()).add(name)
        s0, e0, ph = self.allocs[best]
        deps = []
        for an, (s1, e1, ph1) in self.allocs.items():
            if ph1 < ph and ph1 >= 0 and s1 < e0 and s0 < e1:
                deps += [self.B(n) for n in sorted(self.alloc_names.get(an, ()))]
        return deps

    def _need(self, eng, tok, waits):
        if tok is None:
            return
        key, val = tok
        if self.waited[eng].get(key, 0) >= val:
            return
        self.waited[eng][key] = val
        waits[key] = max(waits.get(key, 0), val)

    def op(self, eng, fns, reads=(), writes=(), dma_sem=None):
        if callable(fns):
            fns = [fns]
        waits = {}
        rb = [self.B(n) for n in reads]
        wb = [self.B(n) for n in writes]
        xb = []
        for n in list(reads) + list(writes):
            xb += self._alias_deps(n)
        for b in rb:
            self._need(eng, b.w, waits)
        for b in wb + xb:
            self._need(eng, b.w, waits)
            for t in b.rs:
                self._need(eng, t, waits)
        if dma_sem is not None:
            key = "dma_" + dma_sem
            self._sem(key)
            self.cnt[key] += 16
            inc = 16
        else:
            key = "eng_" + eng
            self.cnt[key] += 1
            inc = 1
        tok = (key, self.cnt[key])
        for b in rb:
            b.rs.append(tok)
        for b in wb:
            b.w = tok
            b.rs = []
        self.ops[eng].append((sorted(waits.items()), fns, key, inc))
        return tok

    def wait_all(self, eng, toks):
        waits = {}
        for t in toks:
            self._need(eng, t, waits)
        self.ops[eng].append((sorted(waits.items()), [], None, 0))

    def emit(self):
        sems = self.sems

        def run(e, lst):
            for waits, fns, key, inc in lst:
                for k, v in waits:
                    e.wait_ge(sems[k], v)
                ins = None
                for f in fns:
                    ins = f(e)
                if fns and key is not None:
                    ins.then_inc(sems[key], inc)

        with self.nc.Block() as block:
            @block.tensor
            def _(e):
                run(e, self.ops["pe"])

            @block.scalar
            def _(e):
                run(e, self.ops["act"])

            @block.vector
            def _(e):
                run(e, self.ops["dve"])

            @block.gpsimd
            def _(e):
                run(e, self.ops["pool"])

            @block.sync
            def _(e):
                run(e, self.ops["sp"])


def MM(out, lhsT, rhs, start=True, stop=True):
    return lambda e: e.matmul(out, lhsT=lhsT, rhs=rhs, start=start, stop=stop)


def TR(out, in_, ident):
    return lambda e: e.transpose(out=out, in_=in_, identity=ident)


def ACT(out, in_, func, **kw):
    return lambda e: e.activation(out=out, in_=in_, func=func, **kw)


def CP(out, in_):
    return lambda e: e.tensor_copy(out=out, in_=in_)


def TS(out, in0, s1, s2, op0, op1=None):
    if op1 is None:
        return lambda e: e.tensor_scalar(out=out, in0=in0, scalar1=s1, scalar2=None, op0=op0)
    return lambda e: e.tensor_scalar(out=out, in0=in0, scalar1=s1, scalar2=s2, op0=op0, op1=op1)


def STT(out, in0, scalar, in1, op0, op1):
    return lambda e: e.scalar_tensor_tensor(out=out, in0=in0, scalar=scalar, in1=in1, op0=op0, op1=op1)


def TT(out, in0, in1, op):
    return lambda e: e.tensor_tensor(out=out, in0=in0, in1=in1, op=op)


def DMA(out, in_):
    return lambda e: e.dma_start(out=out, in_=in_)


def MEMSET(ap, v):
    return lambda e: e.memset(ap, v)


def RECIP(out, in_):
    return lambda e: e.reciprocal(out=out, in_=in_)


class Ctx:
    ARENA_BYTES = 207 * 1024

    def __init__(self, nc, es):
        self.nc = nc
        self.es = es
        self.S = Sched(nc, es)
        self.psum = es.enter_context(nc.psum_tensor("psum_all", [128, 4096], F32))
        self.arena = es.enter_context(nc.sbuf_tensor("arena", [128, self.ARENA_BYTES // 2], BF16))
        self.off = 0
        self.phase = -1
        self.bank_ptr = 0
        self.ring = list(range(8))

    def sb(self, name, shape, dt):
        esz = 4 if dt == F32 else 2
        n = 1
        for s in shape[1:]:
            n *= s
        nbytes = (n * esz + 63) // 64 * 64
        assert self.off + nbytes <= self.ARENA_BYTES, (name, self.off, nbytes)
        a = self.arena[:, self.off // 2:self.off // 2 + n * esz // 2]
        self.S.allocs[name] = (self.off, self.off + nbytes, self.phase)
        self.off += nbytes
        if dt != BF16:
            a = a.bitcast(dt)
        if len(shape) == 3:
            a = a.rearrange("p (a b) -> p a b", a=shape[1])
        elif len(shape) == 4:
            a = a.rearrange("p (a b c) -> p a b c", a=shape[1], b=shape[2])
        return a

    def banks(self, n=1):
        r = self.ring
        if n == 1:
            b = r[self.bank_ptr % len(r)]
            self.bank_ptr += 1
            return b, [f"bank{b}"]
        assert n == 2
        for _ in range(len(r) + 1):
            b = r[self.bank_ptr % len(r)]
            if b % 2 == 0 and r[(self.bank_ptr + 1) % len(r)] == b + 1:
                self.bank_ptr += 2
                return b, [f"bank{b}", f"bank{b + 1}"]
            self.bank_ptr += 1
        raise AssertionError("no adjacent PSUM bank pair in ring")

    def pf32(self, b, ncols=512):
        return self.psum[:, b * 512:b * 512 + ncols]

    def pbf16(self, b):
        return self.psum[:, b * 512:(b + 1) * 512].bitcast(BF16)

    def barrier(self):
        S = self.S
        toks = [(k, v) for k, v in S.cnt.items() if v > 0]
        for e in S.ENGS:
            S.wait_all(e, toks)


def load_consts(C, aps):
    S = C.S
    K = {}
    K["ident"] = C.sb("ident", [128, 128], BF16)
    S.op("sp", DMA(K["ident"], aps["ident"]), writes=["ident"], dma_sem="c_ident")
    K["ident8"] = C.sb("ident8", [128, 128], BF16)
    S.op("sp", DMA(K["ident8"], aps["ident8"]), writes=["ident8"], dma_sem="c_ident8")
    K["gpre"] = C.sb("gpre", [128, 2, 8], F32)
    S.op("sp", DMA(K["gpre"], aps["gpreT"]), writes=["gpre"], dma_sem="c_gpre")
    K["mhalf"] = C.sb("mhalf", [128, 1], F32)
    S.op("pool", MEMSET(K["mhalf"], -0.5), writes=["mhalf"])
    K["junk"] = C.sb("junk", [128, 1024], BF16)
    return K


def merge(*lists):
    lists = [l for l in lists if l]
    out = []
    pos = [0] * len(lists)
    total = sum(len(l) for l in lists)
    for _ in range(total):
        best, bi = None, -1
        for i, l in enumerate(lists):
            if pos[i] < len(l):
                frac = (pos[i] + 0.5) / len(l)
                if best is None or frac < best:
                    best, bi = frac, i
        out.append(lists[bi][pos[bi]])
        pos[bi] += 1
    return out


def run(thunks):
    for t in thunks:
        t()


def norm_transpose_thunks(C, K, layer, xs_ap, xs_name, hT, hT_name, col0, st, st_name, hb, hb_name, after_hb=None):
    S = C.S
    ss, ms, rs = st[:, 0:1], st[:, 1:2], st[:, 2:3]
    th = []
    th.append(lambda: S.op("act", ACT(K["junk"], xs_ap, AF.Square, accum_out=ss), reads=[xs_name], writes=["junk", st_name + "a"]))
    th.append(lambda: S.op("dve", TS(ms, ss, 1.0 / D, EPS, ALU.mult, ALU.add), reads=[st_name + "a"], writes=[st_name + "b"]))
    th.append(lambda: S.op("pool", TT(rs, ms, K["mhalf"], ALU.pow), reads=[st_name + "b", "mhalf"], writes=[st_name + "c"]))

    def hb_():
        S.op("act", ACT(hb, xs_ap, AF.Copy, scale=rs), reads=[xs_name, st_name + "c"], writes=[hb_name])
        if after_hb is not None:
            after_hb()
    th.append(hb_)

    def tr_():
        b, bn = C.banks(1)
        pt = C.pbf16(b)
        S.op("pe", [TR(pt[:, k * 128:(k + 1) * 128], hb[:, k * 128:(k + 1) * 128], K["ident"]) for k in range(8)],
             reads=[hb_name, "ident"], writes=bn)
        S.op("dve", [TS(hT[:, k, col0:col0 + 128], pt[:, k * 128:(k + 1) * 128], K["gpre"][:, layer, k:k + 1], None, ALU.mult)
                     for k in range(8)],
             reads=bn + ["gpre"], writes=[hT_name])
    th.append(tr_)
    return th


def post_norm_thunks(C, K, layer, yb, ybn, st, st_name, tmp, tmp_name, xr_ap, xr_name, gpost=None, gpost_name=None):
    S = C.S
    y = C.pf32(yb, 1024)
    ss, ms, rs = st[:, 0:1], st[:, 1:2], st[:, 2:3]
    if gpost is None:
        gpost, gpost_name = K["gpost"], f"gpost{layer}"
    th = []
    th.append(lambda: S.op("act", ACT(K["junk"], y, AF.Square, accum_out=ss), reads=ybn, writes=["junk", st_name + "a"]))
    th.append(lambda: S.op("dve", TS(ms, ss, 1.0 / D, EPS, ALU.mult, ALU.add), reads=[st_name + "a"], writes=[st_name + "b"]))
    th.append(lambda: S.op("pool", TT(rs, ms, K["mhalf"], ALU.pow), reads=[st_name + "b", "mhalf"], writes=[st_name + "c"]))
    th.append(lambda: S.op("dve", STT(tmp, y, rs, gpost, ALU.mult, ALU.mult),
                           reads=ybn + [st_name + "c", gpost_name], writes=[tmp_name]))
    th.append(lambda: S.op("pool", TT(tmp, tmp, xr_ap, ALU.add), reads=[tmp_name, xr_name], writes=[tmp_name]))
    return th


def build_layer0(C, K, aps, x_in, x1_out, hoist=None, early=None):
    S = C.S
    sb = C.sb
    C.ring = list(range(8))
    C.bank_ptr = 0
    W0 = sb("W0", [128, 8, 4096], BF16)
    Wg = sb("Wg", [128, 16, 512], BF16)
    Wo = sb("Wo0", [128, 16, 1024], BF16)
    scl = sb("scl0", [128, 16], F32)
    band = sb("band", [128, 3, 4, 128], BF16)
    K["gpost"] = sb("gpost0", [128, 1024], F32)
    S.op("sp", DMA(K["gpost"], aps["gpost_bc"][:, 0, :]), writes=["gpost0"], dma_sem="c_gpost0")
    NXS = 2
    xs = [sb(f"l0xs{i}", [128, 1024], F32) for i in range(NXS)]
    xr = [sb(f"l0xr{i}", [128, 1024], F32) for i in range(2)]
    hb = [sb(f"l0hb{i}", [128, 1024], BF16) for i in range(2)]
    hT = [sb(f"l0hT{i}", [128, 8, 256], BF16) for i in range(2)]
    NA = 5
    at = [sb(f"l0a{i}", [128, 2048], BF16) for i in range(NA)]
    szT = [sb(f"l0sz{i}", [128, 16, 256], BF16) for i in range(2)]
    mixT = sb("l0mx", [128, 16, 256], BF16)
    tmp = [sb(f"l0tmp{i}", [128, 1024], F32) for i in range(2)]
    stat = sb("l0stat", [128, NT0, 2, 4], F32)

    w_in = aps["pool_w_in"].rearrange("(k p) n -> p k n", p=128)
    for n in range(4):
        for kh in range(2):
            S.op("pool", DMA(W0[:, kh * 4:(kh + 1) * 4, n * 512:(n + 1) * 512], w_in[:, kh * 4:(kh + 1) * 4, n * 512:(n + 1) * 512]),
                 writes=[f"W0a{n}_{kh}"], dma_sem=f"W0a{n}_{kh}")
    S.op("pool", DMA(band.rearrange("p a g t -> p (a g t)"), aps["band"]), writes=["band"], dma_sem="c_band")
    for n in range(4):
        for kh in range(2):
            c0 = 2048 + n * 512
            S.op("pool", DMA(W0[:, kh * 4:(kh + 1) * 4, c0:c0 + 512], w_in[:, kh * 4:(kh + 1) * 4, c0:c0 + 512]),
                 writes=[f"W0z{n}_{kh}"], dma_sem=f"W0z{n}_{kh}")
    wg_in = aps["pool_w_group"].rearrange("g (kc p) d -> p (g kc) d", p=128)
    for g in range(4):
        S.op("pool", DMA(Wg[:, g * 4:(g + 1) * 4, :], wg_in[:, g * 4:(g + 1) * 4, :]), writes=[f"Wg{g}"], dma_sem=f"Wg{g}")
    S.op("sp", DMA(scl, aps["pool_scaleT"]), writes=["scl0"], dma_sem="c_scl")
    wo_in = aps["pool_w_out"].rearrange("(k p) n -> p k n", p=128)
    for kq in range(4):
        S.op("pool", DMA(Wo[:, kq * 4:(kq + 1) * 4, :], wo_in[:, kq * 4:(kq + 1) * 4, :]), writes=[f"Wo0_{kq}"], dma_sem=f"Wo0_{kq}")
    Wo_names = [f"Wo0_{kq}" for kq in range(4)]

    def load_x(t):
        sl = t % NXS
        S.op("sp", DMA(xs[sl], x_in[t * 128:(t + 1) * 128, :]), writes=[f"l0xs{sl}"], dma_sem=f"l0xs{sl}")

    def front_tile(t, hTb, hTn, col0):
        nxt = (lambda: load_x(t + 2)) if t + 2 < NT0 else None
        return norm_transpose_thunks(C, K, 0, xs[t % NXS], f"l0xs{t % NXS}", hTb, hTn, col0,
                                     stat[:, t, 0, :], f"l0stat{t}", hb[t % 2], f"l0hb{t % 2}", after_hb=nxt)

    def proj_a_thunks(t, hTb, hTn, col0):
        a = at[t % NA]
        th = []
        for n in range(4):
            def f(n=n):
                b, bn = C.banks(1)
                pa = C.pf32(b)
                S.op("pe", [MM(pa, hTb[:, k, col0:col0 + 128], W0[:, k, n * 512:(n + 1) * 512], k == 0, k == 7) for k in range(8)],
                     reads=[hTn, f"W0a{n}_0", f"W0a{n}_1"], writes=bn)
                if n % 2 == 0:
                    S.op("act", ACT(a[:, n * 512:(n + 1) * 512], pa, AF.Copy), reads=bn, writes=[f"l0a{t % NA}_{n}"])
                else:
                    S.op("dve", CP(a[:, n * 512:(n + 1) * 512], pa), reads=bn, writes=[f"l0a{t % NA}_{n}"])
            th.append(f)
        return th

    def tiles_of(blk):
        return [1 + 2 * blk, 2 + 2 * blk]

    def stage_F(blk):
        p = blk % 2
        th = []
        for j, t in enumerate(tiles_of(blk)):
            th += front_tile(t, hT[p], f"l0hT{p}", j * 128)
        return th

    def stage_A(blk):
        p = blk % 2
        th = []
        for j, t in enumerate(tiles_of(blk)):
            th += proj_a_thunks(t, hT[p], f"l0hT{p}", j * 128)
        return th

    def stage_Z(blk):
        p = blk % 2
        th = []
        for d in range(16):
            def f(d=d):
                b, bn = C.banks(1)
                pz = C.pf32(b, 256)
                S.op("pe", [MM(pz, W0[:, k, 2048 + d * 128:2048 + (d + 1) * 128], hT[p][:, k, :], k == 0, k == 7) for k in range(8)],
                     reads=[f"l0hT{p}", f"W0z{d // 4}_0", f"W0z{d // 4}_1"], writes=bn)
                S.op("act", ACT(szT[p][:, d, :], pz, AF.Silu), reads=bn, writes=[f"l0sz{p}_{d}"])
            th.append(f)
        return th

    def stage_M(blk):
        th = []
        for j, t in enumerate(tiles_of(blk)):
            a_cur = at[t % NA]
            a_prev = at[(t - 1) % NA]
            cur_sel = 2 if t == 5 else 0
            for cq in range(4):
                def f(j=j, t=t, cq=cq, a_cur=a_cur, a_prev=a_prev, cur_sel=cur_sel):
                    b, bn = C.banks(1)
                    pm = C.pf32(b)
                    fns = []
                    for ci in range(4):
                        c = cq * 4 + ci
                        fns.append(MM(pm[:, ci * 128:(ci + 1) * 128], a_cur[:, c * 128:(c + 1) * 128], band[:, cur_sel, cq, :], True, False))
                        fns.append(MM(pm[:, ci * 128:ci * 128 + 16], a_prev[:, c * 128:(c + 1) * 128], band[:, 1, cq, 0:16], False, True))
                    S.op("pe", fns, reads=[f"l0a{t % NA}_{cq}", f"l0a{(t - 1) % NA}_{cq}", "band"], writes=bn)
                    src = pm.rearrange("p (c t) -> p c t", c=4)
                    dst = mixT[:, cq * 4:(cq + 1) * 4, j * 128:(j + 1) * 128]
                    if cq % 2 == 0:
                        S.op("dve", CP(dst, src), reads=bn, writes=[f"l0mx_{cq}_{j}"])
                    else:
                        S.op("act", ACT(dst, src, AF.Copy), reads=bn, writes=[f"l0mx_{cq}_{j}"])
                th.append(f)
        return th

    def stage_G(blk):
        p = blk % 2
        th = []
        for d in range(16):
            def f(d=d):
                g = d // 4
                b, bn = C.banks(1)
                pg = C.pf32(b, 256)
                S.op("pe", [MM(pg, Wg[:, g * 4 + kc, (d % 4) * 128:(d % 4 + 1) * 128], mixT[:, g * 4 + kc, :], kc == 0, kc == 3) for kc in range(4)],
                     reads=[f"l0mx_{g}_0", f"l0mx_{g}_1", f"Wg{g}"], writes=bn)
                S.op("dve", STT(szT[p][:, d, :], pg, scl[:, d:d + 1], szT[p][:, d, :], ALU.mult, ALU.mult),
                     reads=bn + [f"l0sz{p}_{d}", "scl0"], writes=[f"l0sz{p}_{d}"])
            th.append(f)
        return th

    out_toks = []

    def stage_Y(blk):
        p = blk % 2
        th = []
        for j, t in enumerate(tiles_of(blk)):
            hold = {}

            def mmy(j=j, t=t, hold=hold):
                S.op("sp", DMA(xr[j], x_in[t * 128:(t + 1) * 128, :]), writes=[f"l0xr{j}"], dma_sem=f"l0xr{j}")
                yb, ybn = C.banks(2)
                hold["y"] = (yb, ybn)
                for n in range(2):
                    py = C.pf32(yb + n)
                    S.op("pe", [MM(py, szT[p][:, kd, j * 128:(j + 1) * 128], Wo[:, kd, n * 512:(n + 1) * 512], kd == 0, kd == 15) for kd in range(16)],
                         reads=[f"l0sz{p}_{d}" for d in range(16)] + Wo_names, writes=[ybn[n]])
                yb, ybn = hold["y"]
                tm = tmp[t % 2]
                run(post_norm_thunks(C, K, 0, yb, ybn, stat[:, t, 1, :], f"l0stat{t}y", tm, f"l0tmp{t % 2}", xr[j], f"l0xr{j}"))
                tok = S.op("sp", DMA(x1_out[(t - 1) * 128:t * 128, :], tm), reads=[f"l0tmp{t % 2}"], writes=[f"x1d{t - 1}"],
                           dma_sem=f"l0out{t % 2}")
                out_toks.append(tok)
            th.append(mmy)
        return th

    load_x(0)
    load_x(1)
    run(front_tile(0, hT[1], "l0hT1", 0))
    run(proj_a_thunks(0, hT[1], "l0hT1", 0))
    NB = 10
    run(stage_F(0))
    run(stage_A(0))
    for blk in range(NB):
        nxt = blk + 1 < NB
        run(merge(stage_Z(blk), stage_F(blk + 1) if nxt else []))
        if not nxt and hoist is not None:
            run(hoist())
        run(merge(stage_M(blk), stage_A(blk + 1) if nxt else []))
        run(merge(stage_G(blk), stage_Y(blk - 1) if blk >= 1 else []))
    run(merge(stage_Y(NB - 1), early() if early is not None else []))
    return out_toks


def make_layer1(C, K, aps, x1_in, out):
    S = C.S
    sb = C.sb
    W1 = sb("W1", [128, 8, 4096], BF16)
    Wo = sb("Wo1", [128, 8, 1024], BF16)
    E = sb("Etab", [128, 16, 640], BF16)
    kT = sb("kT", [128, 8, 1024], BF16)
    vx = sb("vx", [128, 8, 16, 65], BF16)
    kval = sb("kval", [128, NT1], F32)
    gpost = sb("gpost1", [128, 1024], F32)
    NXS = 2
    xs = [sb(f"l1xs{i}", [128, 1024], F32) for i in range(NXS)]
    xr = sb("l1xr", [128, 1024], F32)
    hb = sb("l1hb", [128, 1024], BF16)
    hT = [sb(f"l1hT{i}", [128, 8, 256], BF16) for i in range(2)]
    stat = sb("l1stat", [128, NT1, 2, 4], F32)
    qE = [sb(f"l1qE{i}", [128, 8, 256], BF16) for i in range(2)]
    qO = [sb(f"l1qO{i}", [128, 8, 256], BF16) for i in range(2)]
    sz = [sb(f"l1sz{i}", [128, 2, 1024], BF16) for i in range(2)]
    zt = [sb(f"l1zt{i}", [128, 512], BF16) for i in range(2)]
    SKEW = 3
    NPE, NPT, NST = 2, SKEW + 2, 2
    pt_ = [sb(f"l1pt{i}", [128, 640], BF16) for i in range(NPT)]
    rden = sb("rden", [128, 4, 4], F32)
    ocp = [sb(f"l1ocp{i}", [128, 4, 65], F32) for i in range(2)]
    gt = sb("l1g", [128, 1024], BF16)
    gT = sb("l1gT", [128, 8, 128], BF16)
    tmp = sb("l1tmp", [128, 1024], F32)

    w_in = aps["att_w_in"].rearrange("(k p) n -> p k n", p=128)

    def hoist():
        th = []
        for n in [2, 3, 4, 5, 0, 1, 6, 7]:
            for kh in range(2):
                th.append(lambda n=n, kh=kh: S.op(
                    "pool", DMA(W1[:, kh * 4:(kh + 1) * 4, n * 512:(n + 1) * 512], w_in[:, kh * 4:(kh + 1) * 4, n * 512:(n + 1) * 512]),
                    writes=[f"W1_{n}_{kh}"], dma_sem=f"W1_{n}_{kh}"))
        return th

    wo_in = aps["att_w_out"].rearrange("(k p) n -> p k n", p=128)
    bg = aps["biasG"]

    def prologue():
        S.op("sp", DMA(gpost, aps["gpost_bc"][:, 1, :]), writes=["gpost1"], dma_sem="c_gpost1")
        for kh in range(2):
            S.op("pool", DMA(Wo[:, kh * 4:(kh + 1) * 4, :], wo_in[:, kh * 4:(kh + 1) * 4, :]), writes=[f"Wo1_{kh}"], dma_sem=f"Wo1_{kh}")
        S.op("sp", DMA(kval, aps["kvalid"]), writes=["kval"], dma_sem="c_kval")

    def etab_thunks():
        th = []
        for hq4 in range(4):
            th.append(lambda hq4=hq4: S.op("pool", DMA(E[:, hq4 * 4:(hq4 + 1) * 4, :], bg[:, hq4 * 4:(hq4 + 1) * 4, :]),
                                           writes=[f"Etab{hq4}"], dma_sem=f"Etab{hq4}"))

        def edges():
            S.op("pool", MEMSET(E[0:64, :, 64:128], -30000.0), writes=[f"Etab{q}" for q in range(4)])
            S.op("pool", MEMSET(E[64:128, :, 512:576], -30000.0), writes=[f"Etab{q}" for q in range(4)])
        th.append(edges)
        for i in range(2):
            th.append(lambda i=i: S.op("pool", MEMSET(qE[i], 0.0), writes=[f"l1qE{i}_{hp}" for hp in range(8)]))
            th.append(lambda i=i: S.op("pool", MEMSET(qO[i], 0.0), writes=[f"l1qO{i}_{hp}" for hp in range(8)]))
        return th

    Wn = lambda n: [f"W1_{n}_0", f"W1_{n}_1"]

    def load_x(u):
        sl = u % NXS
        S.op("sp", DMA(xs[sl], x1_in[u * 128:(u + 1) * 128, :]), reads=[f"x1d{u}"], writes=[f"l1xs{sl}"], dma_sem=f"l1xs{sl}")

    out_toks = []
    OBS = [7, 7]

    def finish_tile(u, j, p):
        b, bn = C.banks(1)
        ptr = C.pbf16(b)
        S.op("pe", [TR(ptr[:, k * 128:(k + 1) * 128], gt[:, k * 128:(k + 1) * 128], K["ident"]) for k in range(8)],
             reads=[f"l1g_{hg}" for hg in range(4)] + ["ident"], writes=bn)
        S.op("act", ACT(gT.rearrange("p k t -> p (k t)"), ptr, AF.Copy), reads=bn, writes=["l1gT"])
        S.op("sp", DMA(xr, x1_in[u * 128:(u + 1) * 128, :]), reads=[f"x1d{u}"], writes=["l1xr"], dma_sem="l1xr")
        yb, ybn = C.banks(2)
        for n in range(2):
            py = C.pf32(yb + n)
            S.op("pe", [MM(py, gT[:, k, :], Wo[:, k, n * 512:(n + 1) * 512], k == 0, k == 7) for k in range(8)],
                 reads=["l1gT", "Wo1_0", "Wo1_1"], writes=[ybn[n]])
        run(post_norm_thunks(C, K, 1, yb, ybn, stat[:, u, 1, :], f"l1stat{u}y", tmp, "l1tmp", xr, "l1xr", gpost, "gpost1"))
        tok = S.op("sp", DMA(out[(u - 4) * 128:(u - 3) * 128, :], tmp), reads=["l1tmp"], dma_sem="l1out")
        out_toks.append(tok)

    pending = []

    def emit_pv(unit):
        u, j, p, h, T0, pti = unit
        ptb = pt_[pti]
        hq, hg = h % 4, h // 4
        OB = OBS[hg % 2]
        obn = [f"bank{OB}"]
        po = C.pf32(OB)
        S.op("pe", [MM(po[:, hq * 65:hq * 65 + 65], ptb[:, jt * 128:(jt + 1) * 128], vx[:, (T0 + jt) % 8, h, :], jt == 0, jt == 4)
                    for jt in range(5)],
             reads=[f"l1pt{pti}"] + [f"vx{(T0 + jt) % 8}" for jt in range(5)], writes=obn)
        if hq == 3:
            oc = ocp[hg % 2]
            S.op("dve", CP(oc.rearrange("p h c -> p (h c)"), po[:, 0:260]), reads=obn, writes=[f"l1ocp{hg % 2}"])
            S.op("dve", RECIP(rden[:, hg, :], oc[:, :, 64]), reads=[f"l1ocp{hg % 2}"], writes=[f"rden{hg}"])
            S.op("dve", [STT(gt[:, (hg * 4 + q4) * 64:(hg * 4 + q4 + 1) * 64], oc[:, q4, 0:64], rden[:, hg, q4:q4 + 1],
                             sz[p][:, j, (hg * 4 + q4) * 64:(hg * 4 + q4 + 1) * 64], ALU.mult, ALU.mult) for q4 in range(4)],
                 reads=[f"l1ocp{hg % 2}", f"rden{hg}", f"l1sz{p}_{j}"], writes=[f"l1g_{hg}"])
        if h == 15:
            pending.append([2, (u, j, p)])
        for pf in list(pending):
            pf[0] -= 1
            if pf[0] < 0:
                pending.remove(pf)
                finish_tile(*pf[1])

    def stage_F(blk):
        p = blk % 2
        th = []
        for j, u in enumerate([2 * blk, 2 * blk + 1]):
            nxt = (lambda u=u: load_x(u + 2)) if u + 2 < NT1 else None
            th += norm_transpose_thunks(C, K, 1, xs[u % NXS], f"l1xs{u % NXS}", hT[p], f"l1hT{p}", j * 128,
                                        stat[:, u, 0, :], f"l1stat{u}", hb, "l1hb", after_hb=nxt)
        return th

    def stage_P(blk):
        p = blk % 2
        own = blk >= 2
        tiles = [2 * blk, 2 * blk + 1]
        th = []
        ring0 = (tiles[0] % 8) * 128
        for hp in range(8):
            def fk(hp=hp):
                b, bn = C.banks(1)
                pk = C.pf32(b, 256)
                S.op("pe", [MM(pk, W1[:, k, 1024 + hp * 128:1024 + (hp + 1) * 128], hT[p][:, k, :], k == 0, k == 7) for k in range(8)],
                     reads=[f"l1hT{p}"] + Wn(2 + hp // 4), writes=bn)
                dst = kT[:, hp, ring0:ring0 + 256]
                wn = [f"kT{tiles[0] % 8}_{hp}", f"kT{tiles[1] % 8}_{hp}"]
                if hp % 2 == 0:
                    S.op("act", ACT(dst, pk, AF.Copy), reads=bn, writes=wn)
                else:
                    S.op("dve", CP(dst, pk), reads=bn, writes=wn)
            th.append(fk)
        if own:
            for hp in range(8):
                def fq(hp=hp):
                    b, bn = C.banks(1)
                    pq = C.pf32(b, 256)
                    S.op("pe", [MM(pq, W1[:, k, hp * 128:(hp + 1) * 128], hT[p][:, k, :], k == 0, k == 7) for k in range(8)],
                         reads=[f"l1hT{p}"] + Wn(hp // 4), writes=bn)
                    S.op("act", ACT(qE[p][0:64, hp, :], pq[0:64, :], AF.Copy), reads=bn, writes=[f"l1qE{p}_{hp}"])
                    S.op("dve", CP(qO[p][64:128, hp, :], pq[64:128, :]), reads=bn, writes=[f"l1qO{p}_{hp}"])
                th.append(fq)
        for j, u in enumerate(tiles):
            rs_ = u % 8
            for n in range(2):
                def fv(j=j, u=u, n=n, rs_=rs_):
                    b, bn = C.banks(1)
                    pv = C.pf32(b)
                    S.op("pe", [MM(pv, hT[p][:, k, j * 128:(j + 1) * 128], W1[:, k, 2048 + n * 512:2048 + (n + 1) * 512], k == 0, k == 7) for k in range(8)],
                         reads=[f"l1hT{p}"] + Wn(4 + n), writes=bn)
                    src = pv.rearrange("p (h c) -> p h c", h=8)
                    dst = vx[:, rs_, n * 8:(n + 1) * 8, 0:64]
                    if n == 0:
                        S.op("dve", CP(dst, src), reads=bn, writes=[f"vx{rs_}"])
                    else:
                        S.op("act", ACT(dst, src, AF.Copy), reads=bn, writes=[f"vx{rs_}"])
                        S.op("pool", TS(vx[:, rs_, :, 64], kval[:, u:u + 1].to_broadcast([128, 16]), 2.0, None, ALU.mult), reads=["kval"], writes=[f"vx{rs_}"])
                th.append(fv)
            if own:
                for n in range(2):
                    def fz(j=j, n=n):
                        b, bn = C.banks(1)
                        pz = C.pf32(b)
                        S.op("pe", [MM(pz, hT[p][:, k, j * 128:(j + 1) * 128], W1[:, k, 3072 + n * 512:3072 + (n + 1) * 512], k == 0, k == 7) for k in range(8)],
                             reads=[f"l1hT{p}"] + Wn(6 + n), writes=bn)
                        zi = (2 * j + n) % 2
                        S.op("act", ACT(zt[zi], pz, AF.Tanh, scale=0.5), reads=bn, writes=[f"l1zt{zi}"])
                        S.op("dve", STT(sz[p][:, j, n * 512:(n + 1) * 512], zt[zi], 1.0, pz, ALU.add, ALU.mult),
                             reads=bn + [f"l1zt{zi}"], writes=[f"l1sz{p}_{j}"])
                    th.append(fz)
        return th

    units = []
    ucount = [0]

    def stage_ATT(blk):
        p = blk % 2
        th = []
        for j, u in enumerate([2 * blk, 2 * blk + 1]):
            T0 = u - 4
            for h in range(16):
                def f(j=j, u=u, T0=T0, h=h):
                    hp = h // 2
                    qsrc = (qE if h % 2 == 0 else qO)[p]
                    qn = f"l1q{'E' if h % 2 == 0 else 'O'}{p}_{hp}"
                    n = ucount[0]
                    ucount[0] += 1
                    sl, pti = n % NST, n % NPT
                    c0 = sl * 1024
                    fns = []
                    fns.append(MM(C.psum[:, c0:c0 + 512], K["ident8"], E[:, h, 0:512], True, False))
                    fns.append(MM(C.psum[:, c0 + 512:c0 + 640], K["ident8"], E[:, h, 512:640], True, False))
                    for jt in range(5):
                        rk = ((T0 + jt) % 8) * 128
                        dst = C.psum[:, c0 + jt * 128:c0 + (jt + 1) * 128]
                        fns.append(MM(dst, kT[:, hp, rk:rk + 128], qsrc[:, hp, j * 128:(j + 1) * 128], False, jt in (3, 4)))
                    S.op("pe", fns, reads=[qn, f"Etab{h // 4}", "ident8"] + [f"kT{(T0 + jt) % 8}_{hp}" for jt in range(5)], writes=[f"sT{sl}", f"bank{2 * sl}", f"bank{2 * sl + 1}"])
                    S.op("act", ACT(pt_[pti], C.psum[:, c0:c0 + 640], AF.Exp, scale=0.125),
                         reads=[f"sT{sl}", f"bank{2 * sl}", f"bank{2 * sl + 1}"], writes=[f"l1pt{pti}"])
                    units.append((u, j, p, h, T0, pti))
                    if len(units) > SKEW:
                        emit_pv(units.pop(0))
                th.append(f)

        def flush():
            while units:
                emit_pv(units.pop(0))
            while pending:
                finish_tile(*pending.pop(0)[1])
        th.append(flush)
        return th

    def early():
        def ld():
            load_x(0)
            load_x(1)
        return [ld] + stage_F(0)

    def body():
        C.ring = [4, 5, 6]
        C.bank_ptr = 0
        prologue()
        NB = 10
        et = etab_thunks()
        run(merge(stage_P(0), stage_F(1), et[:4]))
        run(merge(stage_P(1), stage_F(2), et[4:]))
        run(merge(stage_P(2), stage_F(3)))
        for blk in range(2, NB):
            run(merge(stage_ATT(blk),
                      stage_P(blk + 1) if blk + 1 < NB else [],
                      stage_F(blk + 2) if blk + 2 < NB else []))
        return out_toks

    return hoist, body, early


def build_program(mode="fused"):
    nc = bass.Bass("TRN2", target_bir_lowering=False)
    aps = {}

    def din(name, shape, dt=F32):
        aps[name] = nc.dram_tensor(name, list(shape), dt, kind="ExternalInput").ap()

    din("ident", [128, 128], BF16)
    din("ident8", [128, 128], BF16)
    din("gpreT", [128, 2, 8])
    din("gpost_bc", [128, 2, 1024])
    if mode in ("fused", "l0"):
        din("xin", [NT0 * 128, D])
        din("pool_w_in", [D, 4096])
        din("pool_w_group", [4, 512, 512])
        din("pool_scaleT", [128, 16])
        din("pool_w_out", [2048, D])
        din("band", [128, 3 * 4 * 128])
    if mode in ("fused", "l1"):
        din("att_w_in", [D, 4096])
        din("att_w_out", [D, D])
        din("biasG", [128, 16, 640])
        din("kvalid", [128, NT1])
    if mode == "l1":
        din("x1", [NT1 * 128, D])
        x1 = aps["x1"]
    elif mode == "l0":
        x1 = nc.dram_tensor("x1", [NT1 * 128, D], F32, kind="ExternalOutput").ap()
    else:
        x1 = nc.dram_tensor("x1_scratch", [NT1 * 128, D], F32, kind="Internal").ap()
    if mode in ("fused", "l1"):
        out = nc.dram_tensor("out", [SEG, D], F32, kind="ExternalOutput").ap()

    with ExitStack() as es:
        C = Ctx(nc, es)
        K = load_consts(C, aps)
        base = C.off
        toks = []
        hoist = body = early = None
        if mode in ("fused", "l1"):
            C.phase = 1
            hoist, body, early = make_layer1(C, K, aps, x1, out)
            C.off = base
        if mode in ("fused", "l0"):
            C.phase = 0
            toks = build_layer0(C, K, aps, aps["xin"], x1, hoist=hoist, early=early)
        elif hoist is not None:
            run(hoist())
            run(early())
        if body is not None:
            toks = body()
        C.S.wait_all("sp", toks)
        C.S.emit()
    return nc


def _band_consts(seg):
    band = np.zeros((128, 3, 4, 128), np.float32)
    tp = np.arange(128)[:, None]
    t = np.arange(128)[None, :]
    for g, w in enumerate(WINDOWS):
        win = ((tp <= t) & (tp > t - w)).astype(np.float32)
        band[:, 0, g, :] = win / w - np.eye(128, dtype=np.float32)
        band[:, 1, g, :] = ((tp - 128) > (t - w)).astype(np.float32) / w
        if seg == 0:
            cnt = np.minimum(t + 1, w).astype(np.float32)
            band[:, 2, g, :] = win / cnt - np.eye(128, dtype=np.float32)
        else:
            band[:, 2, g, :] = band[:, 0, g, :]
    return band.reshape(128, -1)


def _bias_gather(rel_bias):
    k = np.arange(128)[:, None, None]
    jt = np.arange(5)[None, :, None]
    q = np.arange(128)[None, None, :]
    idx = np.clip(q - k + 512 - 128 * jt, -256, 256) + 256
    g = rel_bias[:, idx]
    return np.ascontiguousarray(g.transpose(1, 0, 2, 3).reshape(128, 16, 640))


def make_in_maps(inputs, mode="fused", x1_full=None):
    x = np.asarray(inputs["x"], np.float32)
    norm_pre = np.asarray(inputs["norm_pre"], np.float32)
    norm_post = np.asarray(inputs["norm_post"], np.float32)
    ident = np.eye(128, dtype=np.float32).astype(ml_dtypes.bfloat16)
    gpreT = np.ascontiguousarray(norm_pre.reshape(2, 8, 128).transpose(2, 0, 1))
    gpost_bc = np.ascontiguousarray(np.broadcast_to(norm_post[None], (128, 2, 1024)))
    ident8 = (8.0 * np.eye(128, dtype=np.float32)).astype(ml_dtypes.bfloat16)
    common = {"ident": ident, "ident8": ident8, "gpreT": gpreT, "gpost_bc": gpost_bc}
    if mode in ("fused", "l0"):
        common.update({
            "pool_w_in": np.ascontiguousarray(inputs["pool_w_in"][0], np.float32),
            "pool_w_group": np.ascontiguousarray(inputs["pool_w_group"][0], np.float32),
            "pool_scaleT": np.ascontiguousarray(np.asarray(inputs["pool_scale"][0], np.float32).reshape(16, 128).T),
            "pool_w_out": np.ascontiguousarray(inputs["pool_w_out"][0], np.float32),
        })
    if mode in ("fused", "l1"):
        common.update({
            "att_w_in": np.ascontiguousarray(inputs["att_w_in"][0], np.float32),
            "att_w_out": np.ascontiguousarray(inputs["att_w_out"][0], np.float32),
            "biasG": _bias_gather(np.asarray(inputs["att_rel_bias"][0], np.float32)),
        })
    maps = []
    for c in range(NCORES):
        b, seg = c // 4, c % 4
        s = seg * SEG
        m = dict(common)
        if mode in ("fused", "l0"):
            xin = np.zeros((NT0 * 128, D), np.float32)
            lo = s - (HALO + 128)
            src_lo = max(lo, 0)
            xin[src_lo - lo:] = x[b, src_lo:s + SEG]
            m["xin"] = xin
            m["band"] = _band_consts(seg)
        if mode in ("fused", "l1"):
            pos = s - HALO + np.arange(NT1 * 128)
            m["kvalid"] = np.ascontiguousarray((pos >= 0).astype(np.float32).reshape(NT1, 128).T)
        if mode == "l1":
            x1c = np.zeros((NT1 * 128, D), np.float32)
            lo = s - HALO
            src_lo = max(lo, 0)
            x1c[src_lo - lo:] = x1_full[b, src_lo:s + SEG]
            m["x1"] = x1c
        maps.append(m)
    return maps


_NC_CACHE = {}


def _get_nc(mode):
    if mode not in _NC_CACHE:
        _NC_CACHE[mode] = build_program(mode)
    return _NC_CACHE[mode]


def kernel(x, norm_pre, norm_post, pool_w_in, pool_w_group, pool_scale, pool_w_out,
           att_w_in, att_rel_bias, att_w_out):
    inputs = dict(x=x, norm_pre=norm_pre, norm_post=norm_post, pool_w_in=pool_w_in,
                  pool_w_group=pool_w_group, pool_scale=pool_scale, pool_w_out=pool_w_out,
                  att_w_in=att_w_in, att_rel_bias=att_rel_bias, att_w_out=att_w_out)
    inputs = {k: np.asarray(v) for k, v in inputs.items()}
    nc = _get_nc("fused")
    maps = make_in_maps(inputs, "fused")
    res = run_bass_kernel_spmd(nc, maps, core_ids=list(range(NCORES)))
    out = np.empty((2, SEQ, D), np.float32)
    for c in range(NCORES):
        b, seg = c // 4, c % 4
        out[b, seg * SEG:(seg + 1) * SEG] = res.results[c]["out"]
    return out
```

```python
from contextlib import ExitStack

import ml_dtypes
import numpy as np

import concourse.bass as bass
import concourse.mybir as mybir
from concourse.bass_utils import run_bass_kernel_spmd

F32 = mybir.dt.float32
BF16 = mybir.dt.bfloat16
AF = mybir.ActivationFunctionType
ALU = mybir.AluOpType

D = 1024
SEQ = 8192
NCORES = 8
SEG = 2048
HALO = 512
NT0 = 21
NT1 = 20
EPS = 1e-6
WINDOWS = (2, 4, 8, 16)


class _Buf:
    __slots__ = ("w", "rs")

    def __init__(self):
        self.w = None
        self.rs = []


class Sched:
    ENGS = ("pe", "act", "dve", "pool", "sp")

    def __init__(self, nc, es):
        self.nc = nc
        self.es = es
        self.ops = {e: [] for e in self.ENGS}
        self.sems = {}
        self.cnt = {}
        self.waited = {e: {} for e in self.ENGS}
        self.bufs = {}
        self.name_alloc = {}
        self.allocs = {}
        self.alloc_names = {}
        for e in ("pe", "act", "dve", "pool"):
            self._sem("eng_" + e)

    def _sem(self, key):
        if key not in self.sems:
            self.sems[key] = self.es.enter_context(self.nc.semaphore(key))
            self.cnt[key] = 0
        return self.sems[key]

    def B(self, name):
        b = self.bufs.get(name)
        if b is None:
            b = self.bufs[name] = _Buf()
        return b

    def _alias_deps(self, name):
        if name in self.name_alloc:
            return []
        best = None
        for an in self.allocs:
            if name.startswith(an) and (best is None or len(an) > len(best)):
                best = an
        self.name_alloc[name] = best
        if best is None:
            return []
        self.alloc_names.setdefault(best, set()).add(name)
        s0, e0, ph = self.allocs[best]
        deps = []
        for an, (s1, e1, ph1) in self.allocs.items():
            if ph1 < ph and ph1 >= 0 and s1 < e0 and s0 < e1:
                deps += [self.B(n) for n in sorted(self.alloc_names.get(an, ()))]
        return deps

    def _need(self, eng, tok, waits):
        if tok is None:
            return
        key, val = tok
        if self.waited[eng].get(key, 0) >= val:
            return
        self.waited[eng][key] = val
        waits[key] = max(waits.get(key, 0), val)

    def op(self, eng, fns, reads=(), writes=(), dma_sem=None):
        if callable(fns):
            fns = [fns]
        waits = {}
        rb = [self.B(n) for n in reads]
        wb = [self.B(n) for n in writes]
        xb = []
        for n in list(reads) + list(writes):
            xb += self._alias_deps(n)
        for b in rb:
            self._need(eng, b.w, waits)
        for b in wb + xb:
            self._need(eng, b.w, waits)
            for t in b.rs:
                self._need(eng, t, waits)
        if dma_sem is not None:
            key = "dma_" + dma_sem
            self._sem(key)
            self.cnt[key] += 16
            inc = 16
        else:
            key = "eng_" + eng
            self.cnt[key] += 1
            inc = 1
        tok = (key, self.cnt[key])
        for b in rb:
            b.rs.append(tok)
        for b in wb:
            b.w = tok
            b.rs = []
        self.ops[eng].append((sorted(waits.items()), fns, key, inc))
        return tok

    def wait_all(self, eng, toks):
        waits = {}
        for t in toks:
            self._need(eng, t, waits)
        self.ops[eng].append((sorted(waits.items()), [], None, 0))

    def emit(self):
        sems = self.sems

        def run(e, lst):
            for waits, fns, key, inc in lst:
                for k, v in waits:
                    e.wait_ge(sems[k], v)
                ins = None
                for f in fns:
                    ins = f(e)
                if fns and key is not None:
                    ins.then_inc(sems[key], inc)

        with self.nc.Block() as block:
            @block.tensor
            def _(e):
                run(e, self.ops["pe"])

            @block.scalar
            def _(e):
                run(e, self.ops["act"])

            @block.vector
            def _(e):
                run(e, self.ops["dve"])

            @block.gpsimd
            def _(e):
                run(e, self.ops["pool"])

            @block.sync
            def _(e):
                run(e, self.ops["sp"])


def MM(out, lhsT, rhs, start=True, stop=True):
    return lambda e: e.matmul(out, lhsT=lhsT, rhs=rhs, start=start, stop=stop)


def TR(out, in_, ident):
    return lambda e: e.transpose(out=out, in_=in_, identity=ident)


def ACT(out, in_, func, **kw):
    return lambda e: e.activation(out=out, in_=in_, func=func, **kw)


def CP(out, in_):
    return lambda e: e.tensor_copy(out=out, in_=in_)


def TS(out, in0, s1, s2, op0, op1=None):
    if op1 is None:
        return lambda e: e.tensor_scalar(out=out, in0=in0, scalar1=s1, scalar2=None, op0=op0)
    return lambda e: e.tensor_scalar(out=out, in0=in0, scalar1=s1, scalar2=s2, op0=op0, op1=op1)


def STT(out, in0, scalar, in1, op0, op1):
    return lambda e: e.scalar_tensor_tensor(out=out, in0=in0, scalar=scalar, in1=in1, op0=op0, op1=op1)


def TT(out, in0, in1, op):
    return lambda e: e.tensor_tensor(out=out, in0=in0, in1=in1, op=op)


def DMA(out, in_):
    return lambda e: e.dma_start(out=out, in_=in_)


def MEMSET(ap, v):
    return lambda e: e.memset(ap, v)


def RECIP(out, in_):
    return lambda e: e.reciprocal(out=out, in_=in_)


class Ctx:
    ARENA_BYTES = 207 * 1024

    def __init__(self, nc, es):
        self.nc = nc
        self.es = es
        self.S = Sched(nc, es)
        self.psum = es.enter_context(nc.psum_tensor("psum_all", [128, 4096], F32))
        self.arena = es.enter_context(nc.sbuf_tensor("arena", [128, self.ARENA_BYTES // 2], BF16))
        self.off = 0
        self.phase = -1
        self.bank_ptr = 0
        self.ring = list(range(8))

    def sb(self, name, shape, dt):
        esz = 4 if dt == F32 else 2
        n = 1
        for s in shape[1:]:
            n *= s
        nbytes = (n * esz + 63) // 64 * 64
        assert self.off + nbytes <= self.ARENA_BYTES, (name, self.off, nbytes)
        a = self.arena[:, self.off // 2:self.off // 2 + n * esz // 2]
        self.S.allocs[name] = (self.off, self.off + nbytes, self.phase)
        self.off += nbytes
        if dt != BF16:
            a = a.bitcast(dt)
        if len(shape) == 3:
            a = a.rearrange("p (a b) -> p a b", a=shape[1])
        elif len(shape) == 4:
            a = a.rearrange("p (a b c) -> p a b c", a=shape[1], b=shape[2])
        return a

    def banks(self, n=1):
        r = self.ring
        if n == 1:
            b = r[self.bank_ptr % len(r)]
            self.bank_ptr += 1
            return b, [f"bank{b}"]
        assert n == 2
        for _ in range(len(r) + 1):
            b = r[self.bank_ptr % len(r)]
            if b % 2 == 0 and r[(self.bank_ptr + 1) % len(r)] == b + 1:
                self.bank_ptr += 2
                return b, [f"bank{b}", f"bank{b + 1}"]
            self.bank_ptr += 1
        raise AssertionError("no adjacent PSUM bank pair in ring")

    def pf32(self, b, ncols=512):
        return self.psum[:, b * 512:b * 512 + ncols]

    def pbf16(self, b):
        return self.psum[:, b * 512:(b + 1) * 512].bitcast(BF16)

    def barrier(self):
        S = self.S
        toks = [(k, v) for k, v in S.cnt.items() if v > 0]
        for e in S.ENGS:
            S.wait_all(e, toks)


def load_consts(C, aps):
    S = C.S
    K = {}
    K["ident"] = C.sb("ident", [128, 128], BF16)
    S.op("sp", DMA(K["ident"], aps["ident"]), writes=["ident"], dma_sem="c_ident")
    K["ident8"] = C.sb("ident8", [128, 128], BF16)
    S.op("sp", DMA(K["ident8"], aps["ident8"]), writes=["ident8"], dma_sem="c_ident8")
    K["gpre"] = C.sb("gpre", [128, 2, 8], F32)
    S.op("sp", DMA(K["gpre"], aps["gpreT"]), writes=["gpre"], dma_sem="c_gpre")
    K["mhalf"] = C.sb("mhalf", [128, 1], F32)
    S.op("pool", MEMSET(K["mhalf"], -0.5), writes=["mhalf"])
    K["junk"] = C.sb("junk", [128, 1024], BF16)
    return K


def merge(*lists):
    lists = [l for l in lists if l]
    out = []
    pos = [0] * len(lists)
    total = sum(len(l) for l in lists)
    for _ in range(total):
        best, bi = None, -1
        for i, l in enumerate(lists):
            if pos[i] < len(l):
                frac = (pos[i] + 0.5) / len(l)
                if best is None or frac < best:
                    best, bi = frac, i
        out.append(lists[bi][pos[bi]])
        pos[bi] += 1
    return out


def run(thunks):
    for t in thunks:
        t()


def norm_transpose_thunks(C, K, layer, xs_ap, xs_name, hT, hT_name, col0, st, st_name, hb, hb_name, after_hb=None):
    S = C.S
    ss, ms, rs = st[:, 0:1], st[:, 1:2], st[:, 2:3]
    th = []
    th.append(lambda: S.op("act", ACT(K["junk"], xs_ap, AF.Square, accum_out=ss), reads=[xs_name], writes=["junk", st_name + "a"]))
    th.append(lambda: S.op("dve", TS(ms, ss, 1.0 / D, EPS, ALU.mult, ALU.add), reads=[st_name + "a"], writes=[st_name + "b"]))
    th.append(lambda: S.op("pool", TT(rs, ms, K["mhalf"], ALU.pow), reads=[st_name + "b", "mhalf"], writes=[st_name + "c"]))

    def hb_():
        S.op("act", ACT(hb, xs_ap, AF.Copy, scale=rs), reads=[xs_name, st_name + "c"], writes=[hb_name])
        if after_hb is not None:
            after_hb()
    th.append(hb_)

    def tr_():
        b, bn = C.banks(1)
        pt = C.pbf16(b)
        S.op("pe", [TR(pt[:, k * 128:(k + 1) * 128], hb[:, k * 128:(k + 1) * 128], K["ident"]) for k in range(8)],
             reads=[hb_name, "ident"], writes=bn)
        gb = K["gpre"][:, layer, :].unsqueeze(2).to_broadcast([128, 8, 128])
        S.op("dve", TT(hT[:, :, col0:col0 + 128], pt.rearrange("p (k t) -> p k t", k=8), gb, ALU.mult),
             reads=bn + ["gpre"], writes=[hT_name])
    th.append(tr_)
    return th


def post_norm_thunks(C, K, layer, yb, ybn, st, st_name, tmp, tmp_name, xr_ap, xr_name, gpost=None, gpost_name=None):
    S = C.S
    y = C.pf32(yb, 1024)
    ss, ms, rs = st[:, 0:1], st[:, 1:2], st[:, 2:3]
    if gpost is None:
        gpost, gpost_name = K["gpost"], f"gpost{layer}"
    th = []
    th.append(lambda: S.op("act", ACT(K["junk"], y, AF.Square, accum_out=ss), reads=ybn, writes=["junk", st_name + "a"]))
    th.append(lambda: S.op("dve", TS(ms, ss, 1.0 / D, EPS, ALU.mult, ALU.add), reads=[st_name + "a"], writes=[st_name + "b"]))
    th.append(lambda: S.op("pool", TT(rs, ms, K["mhalf"], ALU.pow), reads=[st_name + "b", "mhalf"], writes=[st_name + "c"]))
    th.append(lambda: S.op("dve", STT(tmp, y, rs, gpost, ALU.mult, ALU.mult),
                           reads=ybn + [st_name + "c", gpost_name], writes=[tmp_name]))
    th.append(lambda: S.op("pool", TT(tmp, tmp, xr_ap, ALU.add), reads=[tmp_name, xr_name], writes=[tmp_name]))
    return th


def build_layer0(C, K, aps, x_in, x1_out, hoist=None, early=None):
    S = C.S
    sb = C.sb
    C.ring = list(range(8))
    C.bank_ptr = 0
    W0 = sb("W0", [128, 8, 4096], BF16)
    Wg = sb("Wg", [128, 16, 512], BF16)
    Wo = sb("Wo0", [128, 16, 1024], BF16)
    scl = sb("scl0", [128, 16], F32)
    band = sb("band", [128, 3, 4, 128], BF16)
    K["gpost"] = sb("gpost0", [128, 1024], F32)
    S.op("sp", DMA(K["gpost"], aps["gpost_bc"][:, 0, :]), writes=["gpost0"], dma_sem="c_gpost0")
    NXS = 2
    xs = [sb(f"l0xs{i}", [128, 1024], F32) for i in range(NXS)]
    xr = [sb(f"l0xr{i}", [128, 1024], F32) for i in range(2)]
    hb = [sb(f"l0hb{i}", [128, 1024], BF16) for i in range(2)]
    hT = [sb(f"l0hT{i}", [128, 8, 256], BF16) for i in range(2)]
    NA = 5
    at = [sb(f"l0a{i}", [128, 2048], BF16) for i in range(NA)]
    szT = [sb(f"l0sz{i}", [128, 16, 256], BF16) for i in range(2)]
    mixT = sb("l0mx", [128, 16, 256], BF16)
    tmp = [sb(f"l0tmp{i}", [128, 1024], F32) for i in range(2)]
    stat = sb("l0stat", [128, NT0, 2, 4], F32)

    w_in = aps["pool_w_in"].rearrange("(k p) n -> p k n", p=128)
    for n in range(4):
        for kh in range(2):
            S.op("pool", DMA(W0[:, kh * 4:(kh + 1) * 4, n * 512:(n + 1) * 512], w_in[:, kh * 4:(kh + 1) * 4, n * 512:(n + 1) * 512]),
                 writes=[f"W0a{n}_{kh}"], dma_sem=f"W0a{n}_{kh}")
    S.op("pool", DMA(band.rearrange("p a g t -> p (a g t)"), aps["band"]), writes=["band"], dma_sem="c_band")
    for n in range(4):
        for kh in range(2):
            c0 = 2048 + n * 512
            S.op("pool", DMA(W0[:, kh * 4:(kh + 1) * 4, c0:c0 + 512], w_in[:, kh * 4:(kh + 1) * 4, c0:c0 + 512]),
                 writes=[f"W0z{n}_{kh}"], dma_sem=f"W0z{n}_{kh}")
    wg_in = aps["pool_w_group"].rearrange("g (kc p) d -> p (g kc) d", p=128)
    for g in range(4):
        S.op("pool", DMA(Wg[:, g * 4:(g + 1) * 4, :], wg_in[:, g * 4:(g + 1) * 4, :]), writes=[f"Wg{g}"], dma_sem=f"Wg{g}")
    S.op("sp", DMA(scl, aps["pool_scaleT"]), writes=["scl0"], dma_sem="c_scl")
    wo_in = aps["pool_w_out"].rearrange("(k p) n -> p k n", p=128)
    for kq in range(4):
        S.op("pool", DMA(Wo[:, kq * 4:(kq + 1) * 4, :], wo_in[:, kq * 4:(kq + 1) * 4, :]), writes=[f"Wo0_{kq}"], dma_sem=f"Wo0_{kq}")
    Wo_names = [f"Wo0_{kq}" for kq in range(4)]

    def load_x(t):
        sl = t % NXS
        S.op("sp", DMA(xs[sl], x_in[t * 128:(t + 1) * 128, :]), writes=[f"l0xs{sl}"], dma_sem=f"l0xs{sl}")

    def front_tile(t, hTb, hTn, col0):
        nxt = (lambda: load_x(t + 2)) if t + 2 < NT0 else None
        return norm_transpose_thunks(C, K, 0, xs[t % NXS], f"l0xs{t % NXS}", hTb, hTn, col0,
                                     stat[:, t, 0, :], f"l0stat{t}", hb[t % 2], f"l0hb{t % 2}", after_hb=nxt)

    def proj_a_thunks(t, hTb, hTn, col0):
        a = at[t % NA]
        th = []
        for n in range(4):
            def f(n=n):
                b, bn = C.banks(1)
                pa = C.pf32(b)
                S.op("pe", [MM(pa, hTb[:, k, col0:col0 + 128], W0[:, k, n * 512:(n + 1) * 512], k == 0, k == 7) for k in range(8)],
                     reads=[hTn, f"W0a{n}_0", f"W0a{n}_1"], writes=bn)
                if n % 2 == 0:
                    S.op("act", ACT(a[:, n * 512:(n + 1) * 512], pa, AF.Copy), reads=bn, writes=[f"l0a{t % NA}_{n}"])
                else:
                    S.op("dve", CP(a[:, n * 512:(n + 1) * 512], pa), reads=bn, writes=[f"l0a{t % NA}_{n}"])
            th.append(f)
        return th

    def tiles_of(blk):
        return [1 + 2 * blk, 2 + 2 * blk]

    def stage_F(blk):
        p = blk % 2
        th = []
        for j, t in enumerate(tiles_of(blk)):
            th += front_tile(t, hT[p], f"l0hT{p}", j * 128)
        return th

    def stage_A(blk):
        p = blk % 2
        th = []
        for j, t in enumerate(tiles_of(blk)):
            th += proj_a_thunks(t, hT[p], f"l0hT{p}", j * 128)
        return th

    def stage_Z(blk):
        p = blk % 2
        th = []
        for d in range(16):
            def f(d=d):
                b, bn = C.banks(1)
                pz = C.pf32(b, 256)
                S.op("pe", [MM(pz, W0[:, k, 2048 + d * 128:2048 + (d + 1) * 128], hT[p][:, k, :], k == 0, k == 7) for k in range(8)],
                     reads=[f"l0hT{p}", f"W0z{d // 4}_0", f"W0z{d // 4}_1"], writes=bn)
                S.op("act", ACT(szT[p][:, d, :], pz, AF.Silu), reads=bn, writes=[f"l0sz{p}_{d}"])
            th.append(f)
        return th

    def stage_M(blk):
        th = []
        for j, t in enumerate(tiles_of(blk)):
            a_cur = at[t % NA]
            a_prev = at[(t - 1) % NA]
            cur_sel = 2 if t == 5 else 0
            for cq in range(4):
                def f(j=j, t=t, cq=cq, a_cur=a_cur, a_prev=a_prev, cur_sel=cur_sel):
                    b, bn = C.banks(1)
                    pm = C.pf32(b)
                    fns = []
                    for ci in range(4):
                        c = cq * 4 + ci
                        fns.append(MM(pm[:, ci * 128:(ci + 1) * 128], a_cur[:, c * 128:(c + 1) * 128], band[:, cur_sel, cq, :], True, False))
                        fns.append(MM(pm[:, ci * 128:ci * 128 + 16], a_prev[:, c * 128:(c + 1) * 128], band[:, 1, cq, 0:16], False, True))
                    S.op("pe", fns, reads=[f"l0a{t % NA}_{cq}", f"l0a{(t - 1) % NA}_{cq}", "band"], writes=bn)
                    src = pm.rearrange("p (c t) -> p c t", c=4)
                    dst = mixT[:, cq * 4:(cq + 1) * 4, j * 128:(j + 1) * 128]
                    if cq % 2 == 0:
                        S.op("dve", CP(dst, src), reads=bn, writes=[f"l0mx_{cq}_{j}"])
                    else:
                        S.op("act", ACT(dst, src, AF.Copy), reads=bn, writes=[f"l0mx_{cq}_{j}"])
                th.append(f)
        return th

    def stage_G(blk):
        p = blk % 2
        th = []
        for d in range(16):
            def f(d=d):
                g = d // 4
                b, bn = C.banks(1)
                pg = C.pf32(b, 256)
                S.op("pe", [MM(pg, Wg[:, g * 4 + kc, (d % 4) * 128:(d % 4 + 1) * 128], mixT[:, g * 4 + kc, :], kc == 0, kc == 3) for kc in range(4)],
                     reads=[f"l0mx_{g}_0", f"l0mx_{g}_1", f"Wg{g}"], writes=bn)
                S.op("dve", STT(szT[p][:, d, :], pg, scl[:, d:d + 1], szT[p][:, d, :], ALU.mult, ALU.mult),
                     reads=bn + [f"l0sz{p}_{d}", "scl0"], writes=[f"l0sz{p}_{d}"])
            th.append(f)
        return th

    out_toks = []

    def stage_Y(blk):
        p = blk % 2
        th = []
        for j, t in enumerate(tiles_of(blk)):
            hold = {}

            def mmy(j=j, t=t, hold=hold):
                S.op("sp", DMA(xr[j], x_in[t * 128:(t + 1) * 128, :]), writes=[f"l0xr{j}"], dma_sem=f"l0xr{j}")
                yb, ybn = C.banks(2)
                hold["y"] = (yb, ybn)
                for n in range(2):
                    py = C.pf32(yb + n)
                    S.op("pe", [MM(py, szT[p][:, kd, j * 128:(j + 1) * 128], Wo[:, kd, n * 512:(n + 1) * 512], kd == 0, kd == 15) for kd in range(16)],
                         reads=[f"l0sz{p}_{d}" for d in range(16)] + Wo_names, writes=[ybn[n]])
                yb, ybn = hold["y"]
                tm = tmp[t % 2]
                run(post_norm_thunks(C, K, 0, yb, ybn, stat[:, t, 1, :], f"l0stat{t}y", tm, f"l0tmp{t % 2}", xr[j], f"l0xr{j}"))
                tok = S.op("sp", DMA(x1_out[(t - 1) * 128:t * 128, :], tm), reads=[f"l0tmp{t % 2}"], writes=[f"x1d{t - 1}"],
                           dma_sem=f"l0out{t % 2}")
                out_toks.append(tok)
            th.append(mmy)
        return th

    load_x(0)
    load_x(1)
    run(front_tile(0, hT[1], "l0hT1", 0))
    run(proj_a_thunks(0, hT[1], "l0hT1", 0))
    NB = 10
    run(stage_F(0))
    run(stage_A(0))
    for blk in range(NB):
        nxt = blk + 1 < NB
        run(merge(stage_Z(blk), stage_F(blk + 1) if nxt else []))
        if not nxt and hoist is not None:
            run(hoist())
        run(merge(stage_M(blk), stage_A(blk + 1) if nxt else []))
        run(merge(stage_G(blk), stage_Y(blk - 1) if blk >= 1 else []))
    run(merge(stage_Y(NB - 1), early() if early is not None else []))
    return out_toks


def make_layer1(C, K, aps, x1_in, out):
    S = C.S
    sb = C.sb
    W1 = sb("W1", [128, 8, 4096], BF16)
    Wo = sb("Wo1", [128, 8, 1024], BF16)
    E = sb("Etab", [128, 16, 640], BF16)
    kT = sb("kT", [128, 8, 1024], BF16)
    vx = sb("vx", [128, 8, 16, 65], BF16)
    kval = sb("kval", [128, NT1], F32)
    gpost = sb("gpost1", [128, 1024], F32)
    NXS = 2
    xs = [sb(f"l1xs{i}", [128, 1024], F32) for i in range(NXS)]
    xr = sb("l1xr", [128, 1024], F32)
    hb = sb("l1hb", [128, 1024], BF16)
    hT = [sb(f"l1hT{i}", [128, 8, 256], BF16) for i in range(2)]
    stat = sb("l1stat", [128, NT1, 2, 4], F32)
    qE = [sb(f"l1qE{i}", [128, 8, 256], BF16) for i in range(2)]
    qO = [sb(f"l1qO{i}", [128, 8, 256], BF16) for i in range(2)]
    sz = [sb(f"l1sz{i}", [128, 2, 1024], BF16) for i in range(2)]
    zt = [sb(f"l1zt{i}", [128, 512], BF16) for i in range(2)]
    SKEW = 3
    NPE, NPT, NST = 2, SKEW + 2, 2
    pt_ = [sb(f"l1pt{i}", [128, 640], BF16) for i in range(NPT)]
    rden = sb("rden", [128, 4, 4], F32)
    ocp = [sb(f"l1ocp{i}", [128, 4, 65], F32) for i in range(2)]
    gt = sb("l1g", [128, 1024], BF16)
    gT = sb("l1gT", [128, 8, 128], BF16)
    tmp = sb("l1tmp", [128, 1024], F32)

    w_in = aps["att_w_in"].rearrange("(k p) n -> p k n", p=128)

    def hoist():
        th = []
        for n in [2, 3, 4, 5, 0, 1, 6, 7]:
            for kh in range(2):
                th.append(lambda n=n, kh=kh: S.op(
                    "pool", DMA(W1[:, kh * 4:(kh + 1) * 4, n * 512:(n + 1) * 512], w_in[:, kh * 4:(kh + 1) * 4, n * 512:(n + 1) * 512]),
                    writes=[f"W1_{n}_{kh}"], dma_sem=f"W1_{n}_{kh}"))
        return th

    wo_in = aps["att_w_out"].rearrange("(k p) n -> p k n", p=128)
    bg = aps["biasG"]

    def prologue():
        S.op("sp", DMA(gpost, aps["gpost_bc"][:, 1, :]), writes=["gpost1"], dma_sem="c_gpost1")
        for kh in range(2):
            S.op("pool", DMA(Wo[:, kh * 4:(kh + 1) * 4, :], wo_in[:, kh * 4:(kh + 1) * 4, :]), writes=[f"Wo1_{kh}"], dma_sem=f"Wo1_{kh}")
        S.op("sp", DMA(kval, aps["kvalid"]), writes=["kval"], dma_sem="c_kval")

    def etab_thunks():
        th = []
        for hq4 in range(4):
            th.append(lambda hq4=hq4: S.op("pool", DMA(E[:, hq4 * 4:(hq4 + 1) * 4, :], bg[:, hq4 * 4:(hq4 + 1) * 4, :]),
                                           writes=[f"Etab{hq4}"], dma_sem=f"Etab{hq4}"))

        def edges():
            S.op("pool", MEMSET(E[0:64, :, 64:128], -30000.0), writes=[f"Etab{q}" for q in range(4)])
            S.op("pool", MEMSET(E[64:128, :, 512:576], -30000.0), writes=[f"Etab{q}" for q in range(4)])
        th.append(edges)
        for i in range(2):
            th.append(lambda i=i: S.op("pool", MEMSET(qE[i], 0.0), writes=[f"l1qE{i}_{hp}" for hp in range(8)]))
            th.append(lambda i=i: S.op("pool", MEMSET(qO[i], 0.0), writes=[f"l1qO{i}_{hp}" for hp in range(8)]))
        return th

    Wn = lambda n: [f"W1_{n}_0", f"W1_{n}_1"]

    def load_x(u):
        sl = u % NXS
        S.op("sp", DMA(xs[sl], x1_in[u * 128:(u + 1) * 128, :]), reads=[f"x1d{u}"], writes=[f"l1xs{sl}"], dma_sem=f"l1xs{sl}")

    out_toks = []
    OBS = [7, 7]

    def finish_tile(u, j, p):
        b, bn = C.banks(1)
        ptr = C.pbf16(b)
        S.op("pe", [TR(ptr[:, k * 128:(k + 1) * 128], gt[:, k * 128:(k + 1) * 128], K["ident"]) for k in range(8)],
             reads=[f"l1g_{hg}" for hg in range(4)] + ["ident"], writes=bn)
        S.op("act", ACT(gT.rearrange("p k t -> p (k t)"), ptr, AF.Copy), reads=bn, writes=["l1gT"])
        S.op("sp", DMA(xr, x1_in[u * 128:(u + 1) * 128, :]), reads=[f"x1d{u}"], writes=["l1xr"], dma_sem="l1xr")
        yb, ybn = C.banks(2)
        for n in range(2):
            py = C.pf32(yb + n)
            S.op("pe", [MM(py, gT[:, k, :], Wo[:, k, n * 512:(n + 1) * 512], k == 0, k == 7) for k in range(8)],
                 reads=["l1gT", "Wo1_0", "Wo1_1"], writes=[ybn[n]])
        run(post_norm_thunks(C, K, 1, yb, ybn, stat[:, u, 1, :], f"l1stat{u}y", tmp, "l1tmp", xr, "l1xr", gpost, "gpost1"))
        tok = S.op("sp", DMA(out[(u - 4) * 128:(u - 3) * 128, :], tmp), reads=["l1tmp"], dma_sem="l1out")
        out_toks.append(tok)

    pending = []

    def emit_pv(unit):
        u, j, p, h, T0, pti = unit
        ptb = pt_[pti]
        hq, hg = h % 4, h // 4
        OB = OBS[hg % 2]
        obn = [f"bank{OB}"]
        po = C.pf32(OB)
        S.op("pe", [MM(po[:, hq * 65:hq * 65 + 65], ptb[:, jt * 128:(jt + 1) * 128], vx[:, (T0 + jt) % 8, h, :], jt == 0, jt == 4)
                    for jt in range(5)],
             reads=[f"l1pt{pti}"] + [f"vx{(T0 + jt) % 8}" for jt in range(5)], writes=obn)
        if hq == 3:
            oc = ocp[hg % 2]
            S.op("dve", CP(oc.rearrange("p h c -> p (h c)"), po[:, 0:260]), reads=obn, writes=[f"l1ocp{hg % 2}"])
            S.op("dve", RECIP(rden[:, hg, :], oc[:, :, 64]), reads=[f"l1ocp{hg % 2}"], writes=[f"rden{hg}"])
            S.op("dve", [STT(gt[:, (hg * 4 + q4) * 64:(hg * 4 + q4 + 1) * 64], oc[:, q4, 0:64], rden[:, hg, q4:q4 + 1],
                             sz[p][:, j, (hg * 4 + q4) * 64:(hg * 4 + q4 + 1) * 64], ALU.mult, ALU.mult) for q4 in range(4)],
                 reads=[f"l1ocp{hg % 2}", f"rden{hg}", f"l1sz{p}_{j}"], writes=[f"l1g_{hg}"])
        if h == 15:
            pending.append([2, (u, j, p)])
        for pf in list(pending):
            pf[0] -= 1
            if pf[0] < 0:
                pending.remove(pf)
                finish_tile(*pf[1])

    def stage_F(blk):
        p = blk % 2
        th = []
        for j, u in enumerate([2 * blk, 2 * blk + 1]):
            nxt = (lambda u=u: load_x(u + 2)) if u + 2 < NT1 else None
            th += norm_transpose_thunks(C, K, 1, xs[u % NXS], f"l1xs{u % NXS}", hT[p], f"l1hT{p}", j * 128,
                                        stat[:, u, 0, :], f"l1stat{u}", hb, "l1hb", after_hb=nxt)
        return th

    def stage_P(blk):
        p = blk % 2
        own = blk >= 2
        tiles = [2 * blk, 2 * blk + 1]
        th = []
        ring0 = (tiles[0] % 8) * 128
        for hp in range(8):
            def fk(hp=hp):
                b, bn = C.banks(1)
                pk = C.pf32(b, 256)
                S.op("pe", [MM(pk, W1[:, k, 1024 + hp * 128:1024 + (hp + 1) * 128], hT[p][:, k, :], k == 0, k == 7) for k in range(8)],
                     reads=[f"l1hT{p}"] + Wn(2 + hp // 4), writes=bn)
                dst = kT[:, hp, ring0:ring0 + 256]
                wn = [f"kT{tiles[0] % 8}_{hp}", f"kT{tiles[1] % 8}_{hp}"]
                if hp % 2 == 0:
                    S.op("act", ACT(dst, pk, AF.Copy), reads=bn, writes=wn)
                else:
                    S.op("dve", CP(dst, pk), reads=bn, writes=wn)
            th.append(fk)
        if own:
            for hp in range(8):
                def fq(hp=hp):
                    b, bn = C.banks(1)
                    pq = C.pf32(b, 256)
                    S.op("pe", [MM(pq, W1[:, k, hp * 128:(hp + 1) * 128], hT[p][:, k, :], k == 0, k == 7) for k in range(8)],
                         reads=[f"l1hT{p}"] + Wn(hp // 4), writes=bn)
                    S.op("act", ACT(qE[p][0:64, hp, :], pq[0:64, :], AF.Copy), reads=bn, writes=[f"l1qE{p}_{hp}"])
                    S.op("dve", CP(qO[p][64:128, hp, :], pq[64:128, :]), reads=bn, writes=[f"l1qO{p}_{hp}"])
                th.append(fq)
        for j, u in enumerate(tiles):
            rs_ = u % 8
            for n in range(2):
                def fv(j=j, u=u, n=n, rs_=rs_):
                    b, bn = C.banks(1)
                    pv = C.pf32(b)
                    S.op("pe", [MM(pv, hT[p][:, k, j * 128:(j + 1) * 128], W1[:, k, 2048 + n * 512:2048 + (n + 1) * 512], k == 0, k == 7) for k in range(8)],
                         reads=[f"l1hT{p}"] + Wn(4 + n), writes=bn)
                    src = pv.rearrange("p (h c) -> p h c", h=8)
                    dst = vx[:, rs_, n * 8:(n + 1) * 8, 0:64]
                    if n == 0:
                        S.op("dve", CP(dst, src), reads=bn, writes=[f"vx{rs_}"])
                    else:
                        S.op("act", ACT(dst, src, AF.Copy), reads=bn, writes=[f"vx{rs_}"])
                        S.op("pool", TS(vx[:, rs_, :, 64], kval[:, u:u + 1].to_broadcast([128, 16]), 2.0, None, ALU.mult), reads=["kval"], writes=[f"vx{rs_}"])
                th.append(fv)
            if own:
                for n in range(2):
                    def fz(j=j, n=n):
                        b, bn = C.banks(1)
                        pz = C.pf32(b)
                        S.op("pe", [MM(pz, hT[p][:, k, j * 128:(j + 1) * 128], W1[:, k, 3072 + n * 512:3072 + (n + 1) * 512], k == 0, k == 7) for k in range(8)],
                             reads=[f"l1hT{p}"] + Wn(6 + n), writes=bn)
                        zi = (2 * j + n) % 2
                        S.op("act", ACT(zt[zi], pz, AF.Tanh, scale=0.5), reads=bn, writes=[f"l1zt{zi}"])
                        S.op("dve", STT(sz[p][:, j, n * 512:(n + 1) * 512], zt[zi], 1.0, pz, ALU.add, ALU.mult),
                             reads=bn + [f"l1zt{zi}"], writes=[f"l1sz{p}_{j}"])
                    th.append(fz)
        return th

    units = []
    ucount = [0]

    def stage_ATT(blk):
        p = blk % 2
        th = []
        for j, u in enumerate([2 * blk, 2 * blk + 1]):
            T0 = u - 4
            for h in range(16):
                def f(j=j, u=u, T0=T0, h=h):
                    hp = h // 2
                    qsrc = (qE if h % 2 == 0 else qO)[p]
                    qn = f"l1q{'E' if h % 2 == 0 else 'O'}{p}_{hp}"
                    n = ucount[0]
                    ucount[0] += 1
                    sl, pti = n % NST, n % NPT
                    c0 = sl * 1024
                    fns = []
                    fns.append(MM(C.psum[:, c0:c0 + 512], K["ident8"], E[:, h, 0:512], True, False))
                    fns.append(MM(C.psum[:, c0 + 512:c0 + 640], K["ident8"], E[:, h, 512:640], True, False))
                    for jt in range(5):
                        rk = ((T0 + jt) % 8) * 128
                        dst = C.psum[:, c0 + jt * 128:c0 + (jt + 1) * 128]
                        fns.append(MM(dst, kT[:, hp, rk:rk + 128], qsrc[:, hp, j * 128:(j + 1) * 128], False, jt in (3, 4)))
                    S.op("pe", fns, reads=[qn, f"Etab{h // 4}", "ident8"] + [f"kT{(T0 + jt) % 8}_{hp}" for jt in range(5)], writes=[f"sT{sl}", f"bank{2 * sl}", f"bank{2 * sl + 1}"])
                    S.op("act", ACT(pt_[pti], C.psum[:, c0:c0 + 640], AF.Exp, scale=0.125),
                         reads=[f"sT{sl}", f"bank{2 * sl}", f"bank{2 * sl + 1}"], writes=[f"l1pt{pti}"])
                    units.append((u, j, p, h, T0, pti))
                    if len(units) > SKEW:
                        emit_pv(units.pop(0))
                th.append(f)

        def flush():
            while units:
                emit_pv(units.pop(0))
            while pending:
                finish_tile(*pending.pop(0)[1])
        th.append(flush)
        return th

    def early():
        def ld():
            load_x(0)
            load_x(1)
        return [ld] + stage_F(0)

    def body():
        C.ring = [4, 5, 6]
        C.bank_ptr = 0
        prologue()
        NB = 10
        et = etab_thunks()
        run(merge(stage_P(0), stage_F(1), et[:4]))
        run(merge(stage_P(1), stage_F(2), et[4:]))
        run(merge(stage_P(2), stage_F(3)))
        for blk in range(2, NB):
            run(merge(stage_ATT(blk),
                      stage_P(blk + 1) if blk + 1 < NB else [],
                      stage_F(blk + 2) if blk + 2 < NB else []))
        return out_toks

    return hoist, body, early


def build_program(mode="fused"):
    nc = bass.Bass("TRN2", target_bir_lowering=False)
    aps = {}

    def din(name, shape, dt=F32):
        aps[name] = nc.dram_tensor(name, list(shape), dt, kind="ExternalInput").ap()

    din("ident", [128, 128], BF16)
    din("ident8", [128, 128], BF16)
    din("gpreT", [128, 2, 8])
    din("gpost_bc", [128, 2, 1024])
    if mode in ("fused", "l0"):
        din("xin", [NT0 * 128, D])
        din("pool_w_in", [D, 4096])
        din("pool_w_group", [4, 512, 512])
        din("pool_scaleT", [128, 16])
        din("pool_w_out", [2048, D])
        din("band", [128, 3 * 4 * 128])
    if mode in ("fused", "l1"):
        din("att_w_in", [D, 4096])
        din("att_w_out", [D, D])
        din("biasG", [128, 16, 640])
        din("kvalid", [128, NT1])
    if mode == "l1":
        din("x1", [NT1 * 128, D])
        x1 = aps["x1"]
    elif mode == "l0":
        x1 = nc.dram_tensor("x1", [NT1 * 128, D], F32, kind="ExternalOutput").ap()
    else:
        x1 = nc.dram_tensor("x1_scratch", [NT1 * 128, D], F32, kind="Internal").ap()
    if mode in ("fused", "l1"):
        out = nc.dram_tensor("out", [SEG, D], F32, kind="ExternalOutput").ap()

    with ExitStack() as es:
        C = Ctx(nc, es)
        K = load_consts(C, aps)
        base = C.off
        toks = []
        hoist = body = early = None
        if mode in ("fused", "l1"):
            C.phase = 1
            hoist, body, early = make_layer1(C, K, aps, x1, out)
            C.off = base
        if mode in ("fused", "l0"):
            C.phase = 0
            toks = build_layer0(C, K, aps, aps["xin"], x1, hoist=hoist, early=early)
        elif hoist is not None:
            run(hoist())
            run(early())
        if body is not None:
            toks = body()
        C.S.wait_all("sp", toks)
        C.S.emit()
    return nc


def _band_consts(seg):
    band = np.zeros((128, 3, 4, 128), np.float32)
    tp = np.arange(128)[:, None]
    t = np.arange(128)[None, :]
    for g, w in enumerate(WINDOWS):
        win = ((tp <= t) & (tp > t - w)).astype(np.float32)
        band[:, 0, g, :] = win / w - np.eye(128, dtype=np.float32)
        band[:, 1, g, :] = ((tp - 128) > (t - w)).astype(np.float32) / w
        if seg == 0:
            cnt = np.minimum(t + 1, w).astype(np.float32)
            band[:, 2, g, :] = win / cnt - np.eye(128, dtype=np.float32)
        else:
            band[:, 2, g, :] = band[:, 0, g, :]
    return band.reshape(128, -1)


def _bias_gather(rel_bias):
    k = np.arange(128)[:, None, None]
    jt = np.arange(5)[None, :, None]
    q = np.arange(128)[None, None, :]
    idx = np.clip(q - k + 512 - 128 * jt, -256, 256) + 256
    g = rel_bias[:, idx]
    return np.ascontiguousarray(g.transpose(1, 0, 2, 3).reshape(128, 16, 640))


def make_in_maps(inputs, mode="fused", x1_full=None):
    x = np.asarray(inputs["x"], np.float32)
    norm_pre = np.asarray(inputs["norm_pre"], np.float32)
    norm_post = np.asarray(inputs["norm_post"], np.float32)
    ident = np.eye(128, dtype=np.float32).astype(ml_dtypes.bfloat16)
    gpreT = np.ascontiguousarray(norm_pre.reshape(2, 8, 128).transpose(2, 0, 1))
    gpost_bc = np.ascontiguousarray(np.broadcast_to(norm_post[None], (128, 2, 1024)))
    ident8 = (8.0 * np.eye(128, dtype=np.float32)).astype(ml_dtypes.bfloat16)
    common = {"ident": ident, "ident8": ident8, "gpreT": gpreT, "gpost_bc": gpost_bc}
    if mode in ("fused", "l0"):
        common.update({
            "pool_w_in": np.ascontiguousarray(inputs["pool_w_in"][0], np.float32),
            "pool_w_group": np.ascontiguousarray(inputs["pool_w_group"][0], np.float32),
            "pool_scaleT": np.ascontiguousarray(np.asarray(inputs["pool_scale"][0], np.float32).reshape(16, 128).T),
            "pool_w_out": np.ascontiguousarray(inputs["pool_w_out"][0], np.float32),
        })
    if mode in ("fused", "l1"):
        common.update({
            "att_w_in": np.ascontiguousarray(inputs["att_w_in"][0], np.float32),
            "att_w_out": np.ascontiguousarray(inputs["att_w_out"][0], np.float32),
            "biasG": _bias_gather(np.asarray(inputs["att_rel_bias"][0], np.float32)),
        })
    maps = []
    for c in range(NCORES):
        b, seg = c // 4, c % 4
        s = seg * SEG
        m = dict(common)
        if mode in ("fused", "l0"):
            xin = np.zeros((NT0 * 128, D), np.float32)
            lo = s - (HALO + 128)
            src_lo = max(lo, 0)
            xin[src_lo - lo:] = x[b, src_lo:s + SEG]
            m["xin"] = xin
            m["band"] = _band_consts(seg)
        if mode in ("fused", "l1"):
            pos = s - HALO + np.arange(NT1 * 128)
            m["kvalid"] = np.ascontiguousarray((pos >= 0).astype(np.float32).reshape(NT1, 128).T)
        if mode == "l1":
            x1c = np.zeros((NT1 * 128, D), np.float32)
            lo = s - HALO
            src_lo = max(lo, 0)
            x1c[src_lo - lo:] = x1_full[b, src_lo:s + SEG]
            m["x1"] = x1c
        maps.append(m)
    return maps


_NC_CACHE = {}


def _get_nc(mode):
    if mode not in _NC_CACHE:
        _NC_CACHE[mode] = build_program(mode)
    return _NC_CACHE[mode]


def kernel(x, norm_pre, norm_post, pool_w_in, pool_w_group, pool_scale, pool_w_out,
           att_w_in, att_rel_bias, att_w_out):
    inputs = dict(x=x, norm_pre=norm_pre, norm_post=norm_post, pool_w_in=pool_w_in,
                  pool_w_group=pool_w_group, pool_scale=pool_scale, pool_w_out=pool_w_out,
                  att_w_in=att_w_in, att_rel_bias=att_rel_bias, att_w_out=att_w_out)
    inputs = {k: np.asarray(v) for k, v in inputs.items()}
    nc = _get_nc("fused")
    maps = make_in_maps(inputs, "fused")
    res = run_bass_kernel_spmd(nc, maps, core_ids=list(range(NCORES)))
    out = np.empty((2, SEQ, D), np.float32)
    for c in range(NCORES):
        b, seg = c // 4, c % 4
        out[b, seg * SEG:(seg + 1) * SEG] = res.results[c]["out"]
    return out
```

```python
from contextlib import ExitStack

import ml_dtypes
import numpy as np

import concourse.bass as bass
import concourse.mybir as mybir
from concourse.bass_utils import run_bass_kernel_spmd

F32 = mybir.dt.float32
BF16 = mybir.dt.bfloat16
AF = mybir.ActivationFunctionType
ALU = mybir.AluOpType

D = 1024
SEQ = 8192
NCORES = 8
SEG = 2048
HALO = 512
NT0 = 21
NT1 = 20
EPS = 1e-6
WINDOWS = (2, 4, 8, 16)


class _Buf:
    __slots__ = ("w", "rs")

    def __init__(self):
        self.w = None
        self.rs = []


class Sched:
    ENGS = ("pe", "act", "dve", "pool", "sp")

    def __init__(self, nc, es):
        self.nc = nc
        self.es = es
        self.ops = {e: [] for e in self.ENGS}
        self.sems = {}
        self.cnt = {}
        self.waited = {e: {} for e in self.ENGS}
        self.bufs = {}
        self.name_alloc = {}
        self.allocs = {}
        self.alloc_names = {}
        for e in ("pe", "act", "dve", "pool"):
            self._sem("eng_" + e)

    def _sem(self, key):
        if key not in self.sems:
            self.sems[key] = self.es.enter_context(self.nc.semaphore(key))
            self.cnt[key] = 0
        return self.sems[key]

    def B(self, name):
        b = self.bufs.get(name)
        if b is None:
            b = self.bufs[name] = _Buf()
        return b

    def _alias_deps(self, name):
        if name in self.name_alloc:
            return []
        best = None
        for an in self.allocs:
            if name.startswith(an) and (best is None or len(an) > len(best)):
                best = an
        self.name_alloc[name] = best
        if best is None:
            return []
        self.alloc_names.setdefault(best, set()).add(name)
        s0, e0, ph = self.allocs[best]
        deps = []
        for an, (s1, e1, ph1) in self.allocs.items():
            if ph1 < ph and ph1 >= 0 and s1 < e0 and s0 < e1:
                deps += [self.B(n) for n in sorted(self.alloc_names.get(an, ()))]
        return deps

    def _need(self, eng, tok, waits):
        if tok is None:
            return
        key, val = tok
        if self.waited[eng].get(key, 0) >= val:
            return
        self.waited[eng][key] = val
        waits[key] = max(waits.get(key, 0), val)

    def op(self, eng, fns, reads=(), writes=(), dma_sem=None):
        if callable(fns):
            fns = [fns]
        waits = {}
        rb = [self.B(n) for n in reads]
        wb = [self.B(n) for n in writes]
        xb = []
        for n in list(reads) + list(writes):
            xb += self._alias_deps(n)
        for b in rb:
            self._need(eng, b.w, waits)
        for b in wb + xb:
            self._need(eng, b.w, waits)
            for t in b.rs:
                self._need(eng, t, waits)
        if dma_sem is not None:
            key = "dma_" + dma_sem
            self._sem(key)
            self.cnt[key] += 16
            inc = 16
        else:
            key = "eng_" + eng
            self.cnt[key] += 1
            inc = 1
        tok = (key, self.cnt[key])
        for b in rb:
            b.rs.append(tok)
        for b in wb:
            b.w = tok
            b.rs = []
        self.ops[eng].append((sorted(waits.items()), fns, key, inc))
        return tok

    def wait_all(self, eng, toks):
        waits = {}
        for t in toks:
            self._need(eng, t, waits)
        self.ops[eng].append((sorted(waits.items()), [], None, 0))

    def emit(self):
        sems = self.sems

        def run(e, lst):
            for waits, fns, key, inc in lst:
                for k, v in waits:
                    e.wait_ge(sems[k], v)
                ins = None
                for f in fns:
                    ins = f(e)
                if fns and key is not None:
                    ins.then_inc(sems[key], inc)

        with self.nc.Block() as block:
            @block.tensor
            def _(e):
                run(e, self.ops["pe"])

            @block.scalar
            def _(e):
                run(e, self.ops["act"])

            @block.vector
            def _(e):
                run(e, self.ops["dve"])

            @block.gpsimd
            def _(e):
                run(e, self.ops["pool"])

            @block.sync
            def _(e):
                run(e, self.ops["sp"])


def MM(out, lhsT, rhs, start=True, stop=True):
    return lambda e: e.matmul(out, lhsT=lhsT, rhs=rhs, start=start, stop=stop)


def TR(out, in_, ident):
    return lambda e: e.transpose(out=out, in_=in_, identity=ident)


def ACT(out, in_, func, **kw):
    return lambda e: e.activation(out=out, in_=in_, func=func, **kw)


def CP(out, in_):
    return lambda e: e.tensor_copy(out=out, in_=in_)


def TS(out, in0, s1, s2, op0, op1=None):
    if op1 is None:
        return lambda e: e.tensor_scalar(out=out, in0=in0, scalar1=s1, scalar2=None, op0=op0)
    return lambda e: e.tensor_scalar(out=out, in0=in0, scalar1=s1, scalar2=s2, op0=op0, op1=op1)


def STT(out, in0, scalar, in1, op0, op1):
    return lambda e: e.scalar_tensor_tensor(out=out, in0=in0, scalar=scalar, in1=in1, op0=op0, op1=op1)


def TT(out, in0, in1, op):
    return lambda e: e.tensor_tensor(out=out, in0=in0, in1=in1, op=op)


def DMA(out, in_):
    return lambda e: e.dma_start(out=out, in_=in_)


def MEMSET(ap, v):
    return lambda e: e.memset(ap, v)


def RECIP(out, in_):
    return lambda e: e.reciprocal(out=out, in_=in_)


class Ctx:
    ARENA_BYTES = 207 * 1024

    def __init__(self, nc, es):
        self.nc = nc
        self.es = es
        self.S = Sched(nc, es)
        self.psum = es.enter_context(nc.psum_tensor("psum_all", [128, 4096], F32))
        self.arena = es.enter_context(nc.sbuf_tensor("arena", [128, self.ARENA_BYTES // 2], BF16))
        self.off = 0
        self.phase = -1
        self.bank_ptr = 0
        self.ring = list(range(8))

    def sb(self, name, shape, dt):
        esz = 4 if dt == F32 else 2
        n = 1
        for s in shape[1:]:
            n *= s
        nbytes = (n * esz + 63) // 64 * 64
        assert self.off + nbytes <= self.ARENA_BYTES, (name, self.off, nbytes)
        a = self.arena[:, self.off // 2:self.off // 2 + n * esz // 2]
        self.S.allocs[name] = (self.off, self.off + nbytes, self.phase)
        self.off += nbytes
        if dt != BF16:
            a = a.bitcast(dt)
        if len(shape) == 3:
            a = a.rearrange("p (a b) -> p a b", a=shape[1])
        elif len(shape) == 4:
            a = a.rearrange("p (a b c) -> p a b c", a=shape[1], b=shape[2])
        return a

    def banks(self, n=1):
        r = self.ring
        if n == 1:
            b = r[self.bank_ptr % len(r)]
            self.bank_ptr += 1
            return b, [f"bank{b}"]
        assert n == 2
        for _ in range(len(r) + 1):
            b = r[self.bank_ptr % len(r)]
            if b % 2 == 0 and r[(self.bank_ptr + 1) % len(r)] == b + 1:
                self.bank_ptr += 2
                return b, [f"bank{b}", f"bank{b + 1}"]
            self.bank_ptr += 1
        raise AssertionError("no adjacent PSUM bank pair in ring")

    def pf32(self, b, ncols=512):
        return self.psum[:, b * 512:b * 512 + ncols]

    def pbf16(self, b):
        return self.psum[:, b * 512:(b + 1) * 512].bitcast(BF16)

    def barrier(self):
        S = self.S
        toks = [(k, v) for k, v in S.cnt.items() if v > 0]
        for e in S.ENGS:
            S.wait_all(e, toks)


def load_consts(C, aps):
    S = C.S
    K = {}
    K["ident"] = C.sb("ident", [128, 128], BF16)
    S.op("sp", DMA(K["ident"], aps["ident"]), writes=["ident"], dma_sem="c_ident")
    K["ident8"] = C.sb("ident8", [128, 128], BF16)
    S.op("sp", DMA(K["ident8"], aps["ident8"]), writes=["ident8"], dma_sem="c_ident8")
    K["gpre"] = C.sb("gpre", [128, 2, 8], F32)
    S.op("sp", DMA(K["gpre"], aps["gpreT"]), writes=["gpre"], dma_sem="c_gpre")
    K["mhalf"] = C.sb("mhalf", [128, 1], F32)
    S.op("pool", MEMSET(K["mhalf"], -0.5), writes=["mhalf"])
    K["junk"] = C.sb("junk", [128, 1024], BF16)
    return K


def merge(*lists):
    lists = [l for l in lists if l]
    out = []
    pos = [0] * len(lists)
    total = sum(len(l) for l in lists)
    for _ in range(total):
        best, bi = None, -1
        for i, l in enumerate(lists):
            if pos[i] < len(l):
                frac = (pos[i] + 0.5) / len(l)
                if best is None or frac < best:
                    best, bi = frac, i
        out.append(lists[bi][pos[bi]])
        pos[bi] += 1
    return out


def run(thunks):
    for t in thunks:
        t()


def norm_transpose_thunks(C, K, layer, xs_ap, xs_name, hT, hT_name, col0, st, st_name, hb, hb_name, after_hb=None):
    S = C.S
    ss, ms, rs = st[:, 0:1], st[:, 1:2], st[:, 2:3]
    th = []
    th.append(lambda: S.op("act", ACT(K["junk"], xs_ap, AF.Square, accum_out=ss), reads=[xs_name], writes=["junk", st_name + "a"]))
    th.append(lambda: S.op("dve", TS(ms, ss, 1.0 / D, EPS, ALU.mult, ALU.add), reads=[st_name + "a"], writes=[st_name + "b"]))
    th.append(lambda: S.op("pool", TT(rs, ms, K["mhalf"], ALU.pow), reads=[st_name + "b", "mhalf"], writes=[st_name + "c"]))

    def hb_():
        S.op("act", ACT(hb, xs_ap, AF.Copy, scale=rs), reads=[xs_name, st_name + "c"], writes=[hb_name])
        if after_hb is not None:
            after_hb()
    th.append(hb_)

    def tr_():
        b, bn = C.banks(1)
        pt = C.pbf16(b)
        S.op("pe", [TR(pt[:, k * 128:(k + 1) * 128], hb[:, k * 128:(k + 1) * 128], K["ident"]) for k in range(8)],
             reads=[hb_name, "ident"], writes=bn)
        gb = K["gpre"][:, layer, :].unsqueeze(2).to_broadcast([128, 8, 128])
        S.op("dve", TT(hT[:, :, col0:col0 + 128], pt.rearrange("p (k t) -> p k t", k=8), gb, ALU.mult),
             reads=bn + ["gpre"], writes=[hT_name])
    th.append(tr_)
    return th


def post_norm_thunks(C, K, layer, yb, ybn, st, st_name, tmp, tmp_name, xr_ap, xr_name, gpost=None, gpost_name=None):
    S = C.S
    y = C.pf32(yb, 1024)
    ss, ms, rs = st[:, 0:1], st[:, 1:2], st[:, 2:3]
    if gpost is None:
        gpost, gpost_name = K["gpost"], f"gpost{layer}"
    th = []
    th.append(lambda: S.op("act", ACT(K["junk"], y, AF.Square, accum_out=ss), reads=ybn, writes=["junk", st_name + "a"]))
    th.append(lambda: S.op("dve", TS(ms, ss, 1.0 / D, EPS, ALU.mult, ALU.add), reads=[st_name + "a"], writes=[st_name + "b"]))
    th.append(lambda: S.op("pool", TT(rs, ms, K["mhalf"], ALU.pow), reads=[st_name + "b", "mhalf"], writes=[st_name + "c"]))
    th.append(lambda: S.op("dve", STT(tmp, y, rs, gpost, ALU.mult, ALU.mult),
                           reads=ybn + [st_name + "c", gpost_name], writes=[tmp_name]))
    th.append(lambda: S.op("pool", TT(tmp, tmp, xr_ap, ALU.add), reads=[tmp_name, xr_name], writes=[tmp_name]))
    return th


def build_layer0(C, K, aps, x_in, x1_out, hoist=None, early=None):
    S = C.S
    sb = C.sb
    C.ring = list(range(8))
    C.bank_ptr = 0
    W0 = sb("W0", [128, 8, 4096], BF16)
    Wg = sb("Wg", [128, 16, 512], BF16)
    Wo = sb("Wo0", [128, 16, 1024], BF16)
    scl = sb("scl0", [128, 16], F32)
    band = sb("band", [128, 3, 4, 128], BF16)
    K["gpost"] = sb("gpost0", [128, 1024], F32)
    S.op("sp", DMA(K["gpost"], aps["gpost_bc"][:, 0, :]), writes=["gpost0"], dma_sem="c_gpost0")
    NXS = 2
    xs = [sb(f"l0xs{i}", [128, 1024], F32) for i in range(NXS)]
    xr = [sb(f"l0xr{i}", [128, 1024], F32) for i in range(2)]
    hb = [sb(f"l0hb{i}", [128, 1024], BF16) for i in range(2)]
    hT = [sb(f"l0hT{i}", [128, 8, 256], BF16) for i in range(2)]
    NA = 5
    at = [sb(f"l0a{i}", [128, 2048], BF16) for i in range(NA)]
    szT = [sb(f"l0sz{i}", [128, 16, 256], BF16) for i in range(2)]
    mixT = sb("l0mx", [128, 16, 256], BF16)
    tmp = [sb(f"l0tmp{i}", [128, 1024], F32) for i in range(2)]
    stat = sb("l0stat", [128, NT0, 2, 4], F32)

    w_in = aps["pool_w_in"].rearrange("(k p) n -> p k n", p=128)
    for n in range(4):
        for kh in range(2):
            S.op("pool", DMA(W0[:, kh * 4:(kh + 1) * 4, n * 512:(n + 1) * 512], w_in[:, kh * 4:(kh + 1) * 4, n * 512:(n + 1) * 512]),
                 writes=[f"W0a{n}_{kh}"], dma_sem=f"W0a{n}_{kh}")
    S.op("pool", DMA(band.rearrange("p a g t -> p (a g t)"), aps["band"]), writes=["band"], dma_sem="c_band")
    for n in range(4):
        for kh in range(2):
            c0 = 2048 + n * 512
            S.op("pool", DMA(W0[:, kh * 4:(kh + 1) * 4, c0:c0 + 512], w_in[:, kh * 4:(kh + 1) * 4, c0:c0 + 512]),
                 writes=[f"W0z{n}_{kh}"], dma_sem=f"W0z{n}_{kh}")
    wg_in = aps["pool_w_group"].rearrange("g (kc p) d -> p (g kc) d", p=128)
    for g in range(4):
        S.op("pool", DMA(Wg[:, g * 4:(g + 1) * 4, :], wg_in[:, g * 4:(g + 1) * 4, :]), writes=[f"Wg{g}"], dma_sem=f"Wg{g}")
    S.op("sp", DMA(scl, aps["pool_scaleT"]), writes=["scl0"], dma_sem="c_scl")
    wo_in = aps["pool_w_out"].rearrange("(k p) n -> p k n", p=128)
    for kq in range(4):
        S.op("pool", DMA(Wo[:, kq * 4:(kq + 1) * 4, :], wo_in[:, kq * 4:(kq + 1) * 4, :]), writes=[f"Wo0_{kq}"], dma_sem=f"Wo0_{kq}")
    Wo_names = [f"Wo0_{kq}" for kq in range(4)]

    def load_x(t):
        sl = t % NXS
        S.op("sp", DMA(xs[sl], x_in[t * 128:(t + 1) * 128, :]), writes=[f"l0xs{sl}"], dma_sem=f"l0xs{sl}")

    def front_tile(t, hTb, hTn, col0):
        nxt = (lambda: load_x(t + 2)) if t + 2 < NT0 else None
        return norm_transpose_thunks(C, K, 0, xs[t % NXS], f"l0xs{t % NXS}", hTb, hTn, col0,
                                     stat[:, t, 0, :], f"l0stat{t}", hb[t % 2], f"l0hb{t % 2}", after_hb=nxt)

    def proj_a_thunks(t, hTb, hTn, col0):
        a = at[t % NA]
        th = []
        for n in range(4):
            def f(n=n):
                b, bn = C.banks(1)
                pa = C.pf32(b)
                S.op("pe", [MM(pa, hTb[:, k, col0:col0 + 128], W0[:, k, n * 512:(n + 1) * 512], k == 0, k == 7) for k in range(8)],
                     reads=[hTn, f"W0a{n}_0", f"W0a{n}_1"], writes=bn)
                if n % 2 == 0:
                    S.op("act", ACT(a[:, n * 512:(n + 1) * 512], pa, AF.Copy), reads=bn, writes=[f"l0a{t % NA}_{n}"])
                else:
                    S.op("dve", CP(a[:, n * 512:(n + 1) * 512], pa), reads=bn, writes=[f"l0a{t % NA}_{n}"])
            th.append(f)
        return th

    def tiles_of(blk):
        return [1 + 2 * blk, 2 + 2 * blk]

    def stage_F(blk):
        p = blk % 2
        th = []
        for j, t in enumerate(tiles_of(blk)):
            th += front_tile(t, hT[p], f"l0hT{p}", j * 128)
        return th

    def stage_A(blk):
        p = blk % 2
        th = []
        for j, t in enumerate(tiles_of(blk)):
            th += proj_a_thunks(t, hT[p], f"l0hT{p}", j * 128)
        return th

    def stage_Z(blk):
        p = blk % 2
        th = []
        for d in range(16):
            def f(d=d):
                b, bn = C.banks(1)
                pz = C.pf32(b, 256)
                S.op("pe", [MM(pz, W0[:, k, 2048 + d * 128:2048 + (d + 1) * 128], hT[p][:, k, :], k == 0, k == 7) for k in range(8)],
                     reads=[f"l0hT{p}", f"W0z{d // 4}_0", f"W0z{d // 4}_1"], writes=bn)
                S.op("act", ACT(szT[p][:, d, :], pz, AF.Silu), reads=bn, writes=[f"l0sz{p}_{d}"])
            th.append(f)
        return th

    def stage_M(blk):
        th = []
        for j, t in enumerate(tiles_of(blk)):
            a_cur = at[t % NA]
            a_prev = at[(t - 1) % NA]
            cur_sel = 2 if t == 5 else 0
            for cq in range(4):
                def f(j=j, t=t, cq=cq, a_cur=a_cur, a_prev=a_prev, cur_sel=cur_sel):
                    b, bn = C.banks(1)
                    pm = C.pf32(b)
                    fns = []
                    for ci in range(4):
                        c = cq * 4 + ci
                        fns.append(MM(pm[:, ci * 128:(ci + 1) * 128], a_cur[:, c * 128:(c + 1) * 128], band[:, cur_sel, cq, :], True, False))
                        fns.append(MM(pm[:, ci * 128:ci * 128 + 16], a_prev[:, c * 128:(c + 1) * 128], band[:, 1, cq, 0:16], False, True))
                    S.op("pe", fns, reads=[f"l0a{t % NA}_{cq}", f"l0a{(t - 1) % NA}_{cq}", "band"], writes=bn)
                    src = pm.rearrange("p (c t) -> p c t", c=4)
                    dst = mixT[:, cq * 4:(cq + 1) * 4, j * 128:(j + 1) * 128]
                    if cq % 2 == 0:
                        S.op("dve", CP(dst, src), reads=bn, writes=[f"l0mx_{cq}_{j}"])
                    else:
                        S.op("act", ACT(dst, src, AF.Copy), reads=bn, writes=[f"l0mx_{cq}_{j}"])
                th.append(f)
        return th

    def stage_G(blk):
        p = blk % 2
        th = []
        for d in range(16):
            def f(d=d):
                g = d // 4
                b, bn = C.banks(1)
                pg = C.pf32(b, 256)
                S.op("pe", [MM(pg, Wg[:, g * 4 + kc, (d % 4) * 128:(d % 4 + 1) * 128], mixT[:, g * 4 + kc, :], kc == 0, kc == 3) for kc in range(4)],
                     reads=[f"l0mx_{g}_0", f"l0mx_{g}_1", f"Wg{g}"], writes=bn)
                S.op("dve", STT(szT[p][:, d, :], pg, scl[:, d:d + 1], szT[p][:, d, :], ALU.mult, ALU.mult),
                     reads=bn + [f"l0sz{p}_{d}", "scl0"], writes=[f"l0sz{p}_{d}"])
            th.append(f)
        return th

    out_toks = []

    def stage_Y(blk):
        p = blk % 2
        th = []
        for j, t in enumerate(tiles_of(blk)):
            hold = {}

            def mmy(j=j, t=t, hold=hold):
                S.op("sp", DMA(xr[j], x_in[t * 128:(t + 1) * 128, :]), writes=[f"l0xr{j}"], dma_sem=f"l0xr{j}")
                yb, ybn = C.banks(2)
                hold["y"] = (yb, ybn)
                for n in range(2):
                    py = C.pf32(yb + n)
                    S.op("pe", [MM(py, szT[p][:, kd, j * 128:(j + 1) * 128], Wo[:, kd, n * 512:(n + 1) * 512], kd == 0, kd == 15) for kd in range(16)],
                         reads=[f"l0sz{p}_{d}" for d in range(16)] + Wo_names, writes=[ybn[n]])
                yb, ybn = hold["y"]
                tm = tmp[t % 2]
                run(post_norm_thunks(C, K, 0, yb, ybn, stat[:, t, 1, :], f"l0stat{t}y", tm, f"l0tmp{t % 2}", xr[j], f"l0xr{j}"))
                tok = S.op("sp", DMA(x1_out[(t - 1) * 128:t * 128, :], tm), reads=[f"l0tmp{t % 2}"], writes=[f"x1d{t - 1}"],
                           dma_sem=f"l0out{t % 2}")
                out_toks.append(tok)
            th.append(mmy)
        return th

    load_x(0)
    load_x(1)
    run(front_tile(0, hT[1], "l0hT1", 0))
    run(proj_a_thunks(0, hT[1], "l0hT1", 0))
    NB = 10
    run(stage_F(0))
    run(stage_A(0))
    run(stage_F(1))
    run(stage_A(1))
    for blk in range(NB):
        nxt = blk + 1 < NB
        pre = blk == 0
        run(merge(stage_Z(blk), stage_F(blk + 1) if nxt and not pre else []))
        if not nxt and hoist is not None:
            run(hoist())
        run(merge(stage_M(blk), stage_A(blk + 1) if nxt and not pre else []))
        run(merge(stage_G(blk), stage_Y(blk - 1) if blk >= 1 else []))
    run(merge(stage_Y(NB - 1), early() if early is not None else []))
    return out_toks


def make_layer1(C, K, aps, x1_in, out):
    S = C.S
    sb = C.sb
    W1 = sb("W1", [128, 8, 4096], BF16)
    Wo = sb("Wo1", [128, 8, 1024], BF16)
    E = sb("Etab", [128, 16, 640], BF16)
    kT = sb("kT", [128, 8, 1024], BF16)
    vx = sb("vx", [128, 8, 16, 65], BF16)
    kval = sb("kval", [128, NT1], F32)
    gpost = sb("gpost1", [128, 1024], F32)
    NXS = 2
    xs = [sb(f"l1xs{i}", [128, 1024], F32) for i in range(NXS)]
    xr = sb("l1xr", [128, 1024], F32)
    hb = sb("l1hb", [128, 1024], BF16)
    hT = [sb(f"l1hT{i}", [128, 8, 256], BF16) for i in range(2)]
    stat = sb("l1stat", [128, NT1, 2, 4], F32)
    qE = [sb(f"l1qE{i}", [128, 8, 256], BF16) for i in range(2)]
    qO = [sb(f"l1qO{i}", [128, 8, 256], BF16) for i in range(2)]
    sz = [sb(f"l1sz{i}", [128, 2, 1024], BF16) for i in range(2)]
    zt = [sb(f"l1zt{i}", [128, 512], BF16) for i in range(2)]
    SKEW = 3
    NPE, NPT, NST = 2, SKEW + 2, 2
    pt_ = [sb(f"l1pt{i}", [128, 640], BF16) for i in range(NPT)]
    rden = sb("rden", [128, 4, 4], F32)
    ocp = [sb(f"l1ocp{i}", [128, 4, 65], F32) for i in range(2)]
    gt = sb("l1g", [128, 1024], BF16)
    gT = sb("l1gT", [128, 8, 128], BF16)
    tmp = sb("l1tmp", [128, 1024], F32)

    w_in = aps["att_w_in"].rearrange("(k p) n -> p k n", p=128)

    def hoist():
        th = []
        for n in [2, 3, 4, 5, 0, 1, 6, 7]:
            for kh in range(2):
                th.append(lambda n=n, kh=kh: S.op(
                    "pool", DMA(W1[:, kh * 4:(kh + 1) * 4, n * 512:(n + 1) * 512], w_in[:, kh * 4:(kh + 1) * 4, n * 512:(n + 1) * 512]),
                    writes=[f"W1_{n}_{kh}"], dma_sem=f"W1_{n}_{kh}"))
        return th

    wo_in = aps["att_w_out"].rearrange("(k p) n -> p k n", p=128)
    bg = aps["biasG"]

    def prologue():
        S.op("sp", DMA(gpost, aps["gpost_bc"][:, 1, :]), writes=["gpost1"], dma_sem="c_gpost1")
        for kh in range(2):
            S.op("pool", DMA(Wo[:, kh * 4:(kh + 1) * 4, :], wo_in[:, kh * 4:(kh + 1) * 4, :]), writes=[f"Wo1_{kh}"], dma_sem=f"Wo1_{kh}")
        S.op("sp", DMA(kval, aps["kvalid"]), writes=["kval"], dma_sem="c_kval")

    def etab_thunks():
        th = []
        for hq4 in range(4):
            th.append(lambda hq4=hq4: S.op("pool", DMA(E[:, hq4 * 4:(hq4 + 1) * 4, :], bg[:, hq4 * 4:(hq4 + 1) * 4, :]),
                                           writes=[f"Etab{hq4}"], dma_sem=f"Etab{hq4}"))

        def edges():
            S.op("pool", MEMSET(E[0:64, :, 64:128], -30000.0), writes=[f"Etab{q}" for q in range(4)])
            S.op("pool", MEMSET(E[64:128, :, 512:576], -30000.0), writes=[f"Etab{q}" for q in range(4)])
        th.append(edges)
        for i in range(2):
            th.append(lambda i=i: S.op("pool", MEMSET(qE[i], 0.0), writes=[f"l1qE{i}_{hp}" for hp in range(8)]))
            th.append(lambda i=i: S.op("pool", MEMSET(qO[i], 0.0), writes=[f"l1qO{i}_{hp}" for hp in range(8)]))
        return th

    Wn = lambda n: [f"W1_{n}_0", f"W1_{n}_1"]

    def load_x(u):
        sl = u % NXS
        S.op("sp", DMA(xs[sl], x1_in[u * 128:(u + 1) * 128, :]), reads=[f"x1d{u}"], writes=[f"l1xs{sl}"], dma_sem=f"l1xs{sl}")

    out_toks = []
    OBS = [7, 7]

    def finish_tile(u, j, p):
        b, bn = C.banks(1)
        ptr = C.pbf16(b)
        S.op("pe", [TR(ptr[:, k * 128:(k + 1) * 128], gt[:, k * 128:(k + 1) * 128], K["ident"]) for k in range(8)],
             reads=[f"l1g_{hg}" for hg in range(4)] + ["ident"], writes=bn)
        S.op("act", ACT(gT.rearrange("p k t -> p (k t)"), ptr, AF.Copy), reads=bn, writes=["l1gT"])
        S.op("sp", DMA(xr, x1_in[u * 128:(u + 1) * 128, :]), reads=[f"x1d{u}"], writes=["l1xr"], dma_sem="l1xr")
        yb, ybn = C.banks(2)
        for n in range(2):
            py = C.pf32(yb + n)
            S.op("pe", [MM(py, gT[:, k, :], Wo[:, k, n * 512:(n + 1) * 512], k == 0, k == 7) for k in range(8)],
                 reads=["l1gT", "Wo1_0", "Wo1_1"], writes=[ybn[n]])
        run(post_norm_thunks(C, K, 1, yb, ybn, stat[:, u, 1, :], f"l1stat{u}y", tmp, "l1tmp", xr, "l1xr", gpost, "gpost1"))
        tok = S.op("sp", DMA(out[(u - 4) * 128:(u - 3) * 128, :], tmp), reads=["l1tmp"], dma_sem="l1out")
        out_toks.append(tok)

    pending = []

    def emit_pv(unit):
        u, j, p, h, T0, pti = unit
        ptb = pt_[pti]
        hq, hg = h % 4, h // 4
        OB = OBS[hg % 2]
        obn = [f"bank{OB}"]
        po = C.pf32(OB)
        S.op("pe", [MM(po[:, hq * 65:hq * 65 + 65], ptb[:, jt * 128:(jt + 1) * 128], vx[:, (T0 + jt) % 8, h, :], jt == 0, jt == 4)
                    for jt in range(5)],
             reads=[f"l1pt{pti}"] + [f"vx{(T0 + jt) % 8}" for jt in range(5)], writes=obn)
        if hq == 3:
            oc = ocp[hg % 2]
            S.op("dve", CP(oc.rearrange("p h c -> p (h c)"), po[:, 0:260]), reads=obn, writes=[f"l1ocp{hg % 2}"])
            S.op("dve", RECIP(rden[:, hg, :], oc[:, :, 64]), reads=[f"l1ocp{hg % 2}"], writes=[f"rden{hg}"])
            S.op("dve", [STT(gt[:, (hg * 4 + q4) * 64:(hg * 4 + q4 + 1) * 64], oc[:, q4, 0:64], rden[:, hg, q4:q4 + 1],
                             sz[p][:, j, (hg * 4 + q4) * 64:(hg * 4 + q4 + 1) * 64], ALU.mult, ALU.mult) for q4 in range(4)],
                 reads=[f"l1ocp{hg % 2}", f"rden{hg}", f"l1sz{p}_{j}"], writes=[f"l1g_{hg}"])
        if h == 15:
            pending.append([2, (u, j, p)])
        for pf in list(pending):
            pf[0] -= 1
            if pf[0] < 0:
                pending.remove(pf)
                finish_tile(*pf[1])

    def stage_F(blk):
        p = blk % 2
        th = []
        for j, u in enumerate([2 * blk, 2 * blk + 1]):
            nxt = (lambda u=u: load_x(u + 2)) if u + 2 < NT1 else None
            th += norm_transpose_thunks(C, K, 1, xs[u % NXS], f"l1xs{u % NXS}", hT[p], f"l1hT{p}", j * 128,
                                        stat[:, u, 0, :], f"l1stat{u}", hb, "l1hb", after_hb=nxt)
        return th

    def stage_P(blk):
        p = blk % 2
        own = blk >= 2
        tiles = [2 * blk, 2 * blk + 1]
        th = []
        ring0 = (tiles[0] % 8) * 128
        for hp in range(8):
            def fk(hp=hp):
                b, bn = C.banks(1)
                pk = C.pf32(b, 256)
                S.op("pe", [MM(pk, W1[:, k, 1024 + hp * 128:1024 + (hp + 1) * 128], hT[p][:, k, :], k == 0, k == 7) for k in range(8)],
                     reads=[f"l1hT{p}"] + Wn(2 + hp // 4), writes=bn)
                dst = kT[:, hp, ring0:ring0 + 256]
                wn = [f"kT{tiles[0] % 8}_{hp}", f"kT{tiles[1] % 8}_{hp}"]
                if hp % 2 == 0:
                    S.op("act", ACT(dst, pk, AF.Copy), reads=bn, writes=wn)
                else:
                    S.op("dve", CP(dst, pk), reads=bn, writes=wn)
            th.append(fk)
        if own:
            for hp in range(8):
                def fq(hp=hp):
                    b, bn = C.banks(1)
                    pq = C.pf32(b, 256)
                    S.op("pe", [MM(pq, W1[:, k, hp * 128:(hp + 1) * 128], hT[p][:, k, :], k == 0, k == 7) for k in range(8)],
                         reads=[f"l1hT{p}"] + Wn(hp // 4), writes=bn)
                    S.op("act", ACT(qE[p][0:64, hp, :], pq[0:64, :], AF.Copy), reads=bn, writes=[f"l1qE{p}_{hp}"])
                    S.op("dve", CP(qO[p][64:128, hp, :], pq[64:128, :]), reads=bn, writes=[f"l1qO{p}_{hp}"])
                th.append(fq)
        for j, u in enumerate(tiles):
            rs_ = u % 8
            for n in range(2):
                def fv(j=j, u=u, n=n, rs_=rs_):
                    b, bn = C.banks(1)
                    pv = C.pf32(b)
                    S.op("pe", [MM(pv, hT[p][:, k, j * 128:(j + 1) * 128], W1[:, k, 2048 + n * 512:2048 + (n + 1) * 512], k == 0, k == 7) for k in range(8)],
                         reads=[f"l1hT{p}"] + Wn(4 + n), writes=bn)
                    src = pv.rearrange("p (h c) -> p h c", h=8)
                    dst = vx[:, rs_, n * 8:(n + 1) * 8, 0:64]
                    if n == 0:
                        S.op("dve", CP(dst, src), reads=bn, writes=[f"vx{rs_}"])
                    else:
                        S.op("act", ACT(dst, src, AF.Copy), reads=bn, writes=[f"vx{rs_}"])
                        S.op("pool", TS(vx[:, rs_, :, 64], kval[:, u:u + 1].to_broadcast([128, 16]), 2.0, None, ALU.mult), reads=["kval"], writes=[f"vx{rs_}"])
                th.append(fv)
            if own:
                for n in range(2):
                    def fz(j=j, n=n):
                        b, bn = C.banks(1)
                        pz = C.pf32(b)
                        S.op("pe", [MM(pz, hT[p][:, k, j * 128:(j + 1) * 128], W1[:, k, 3072 + n * 512:3072 + (n + 1) * 512], k == 0, k == 7) for k in range(8)],
                             reads=[f"l1hT{p}"] + Wn(6 + n), writes=bn)
                        zi = (2 * j + n) % 2
                        S.op("act", ACT(zt[zi], pz, AF.Tanh, scale=0.5), reads=bn, writes=[f"l1zt{zi}"])
                        S.op("dve", STT(sz[p][:, j, n * 512:(n + 1) * 512], zt[zi], 1.0, pz, ALU.add, ALU.mult),
                             reads=bn + [f"l1zt{zi}"], writes=[f"l1sz{p}_{j}"])
                    th.append(fz)
        return th

    units = []
    ucount = [0]

    def stage_ATT(blk):
        p = blk % 2
        th = []
        for j, u in enumerate([2 * blk, 2 * blk + 1]):
            T0 = u - 4
            for h in range(16):
                def f(j=j, u=u, T0=T0, h=h):
                    hp = h // 2
                    qsrc = (qE if h % 2 == 0 else qO)[p]
                    qn = f"l1q{'E' if h % 2 == 0 else 'O'}{p}_{hp}"
                    n = ucount[0]
                    ucount[0] += 1
                    sl, pti = n % NST, n % NPT
                    c0 = sl * 1024
                    fns = []
                    fns.append(MM(C.psum[:, c0:c0 + 512], K["ident8"], E[:, h, 0:512], True, False))
                    fns.append(MM(C.psum[:, c0 + 512:c0 + 640], K["ident8"], E[:, h, 512:640], True, False))
                    for jt in range(5):
                        rk = ((T0 + jt) % 8) * 128
                        dst = C.psum[:, c0 + jt * 128:c0 + (jt + 1) * 128]
                        fns.append(MM(dst, kT[:, hp, rk:rk + 128], qsrc[:, hp, j * 128:(j + 1) * 128], False, jt in (3, 4)))
                    S.op("pe", fns, reads=[qn, f"Etab{h // 4}", "ident8"] + [f"kT{(T0 + jt) % 8}_{hp}" for jt in range(5)], writes=[f"sT{sl}", f"bank{2 * sl}", f"bank{2 * sl + 1}"])
                    S.op("act", ACT(pt_[pti], C.psum[:, c0:c0 + 640], AF.Exp, scale=0.125),
                         reads=[f"sT{sl}", f"bank{2 * sl}", f"bank{2 * sl + 1}"], writes=[f"l1pt{pti}"])
                    units.append((u, j, p, h, T0, pti))
                    if len(units) > SKEW:
                        emit_pv(units.pop(0))
                th.append(f)

        def flush():
            while units:
                emit_pv(units.pop(0))
            while pending:
                finish_tile(*pending.pop(0)[1])
        th.append(flush)
        return th

    def early():
        def ld():
            load_x(0)
            load_x(1)
        return [ld] + stage_F(0)

    def body():
        C.ring = [4, 5, 6]
        C.bank_ptr = 0
        prologue()
        NB = 10
        et = etab_thunks()
        run(merge(stage_P(0), stage_F(1), et[:4]))
        run(merge(stage_P(1), stage_F(2), et[4:]))
        run(merge(stage_P(2), stage_F(3)))
        for blk in range(2, NB):
            run(merge(stage_ATT(blk),
                      stage_P(blk + 1) if blk + 1 < NB else [],
                      stage_F(blk + 2) if blk + 2 < NB else []))
        return out_toks

    return hoist, body, early


def build_program(mode="fused"):
    nc = bass.Bass("TRN2", target_bir_lowering=False)
    aps = {}

    def din(name, shape, dt=F32):
        aps[name] = nc.dram_tensor(name, list(shape), dt, kind="ExternalInput").ap()

    din("ident", [128, 128], BF16)
    din("ident8", [128, 128], BF16)
    din("gpreT", [128, 2, 8])
    din("gpost_bc", [128, 2, 1024])
    if mode in ("fused", "l0"):
        din("xin", [NT0 * 128, D])
        din("pool_w_in", [D, 4096])
        din("pool_w_group", [4, 512, 512])
        din("pool_scaleT", [128, 16])
        din("pool_w_out", [2048, D])
        din("band", [128, 3 * 4 * 128])
    if mode in ("fused", "l1"):
        din("att_w_in", [D, 4096])
        din("att_w_out", [D, D])
        din("biasG", [128, 16, 640])
        din("kvalid", [128, NT1])
    if mode == "l1":
        din("x1", [NT1 * 128, D])
        x1 = aps["x1"]
    elif mode == "l0":
        x1 = nc.dram_tensor("x1", [NT1 * 128, D], F32, kind="ExternalOutput").ap()
    else:
        x1 = nc.dram_tensor("x1_scratch", [NT1 * 128, D], F32, kind="Internal").ap()
    if mode in ("fused", "l1"):
        out = nc.dram_tensor("out", [SEG, D], F32, kind="ExternalOutput").ap()

    with ExitStack() as es:
        C = Ctx(nc, es)
        K = load_consts(C, aps)
        base = C.off
        toks = []
        hoist = body = early = None
        if mode in ("fused", "l1"):
            C.phase = 1
            hoist, body, early = make_layer1(C, K, aps, x1, out)
            C.off = base
        if mode in ("fused", "l0"):
            C.phase = 0
            toks = build_layer0(C, K, aps, aps["xin"], x1, hoist=hoist, early=early)
        elif hoist is not None:
            run(hoist())
            run(early())
        if body is not None:
            toks = body()
        C.S.wait_all("sp", toks)
        C.S.emit()
    return nc


def _band_consts(seg):
    band = np.zeros((128, 3, 4, 128), np.float32)
    tp = np.arange(128)[:, None]
    t = np.arange(128)[None, :]
    for g, w in enumerate(WINDOWS):
        win = ((tp <= t) & (tp > t - w)).astype(np.float32)
        band[:, 0, g, :] = win / w - np.eye(128, dtype=np.float32)
        band[:, 1, g, :] = ((tp - 128) > (t - w)).astype(np.float32) / w
        if seg == 0:
            cnt = np.minimum(t + 1, w).astype(np.float32)
            band[:, 2, g, :] = win / cnt - np.eye(128, dtype=np.float32)
        else:
            band[:, 2, g, :] = band[:, 0, g, :]
    return band.reshape(128, -1)


def _bias_gather(rel_bias):
    k = np.arange(128)[:, None, None]
    jt = np.arange(5)[None, :, None]
    q = np.arange(128)[None, None, :]
    idx = np.clip(q - k + 512 - 128 * jt, -256, 256) + 256
    g = rel_bias[:, idx]
    return np.ascontiguousarray(g.transpose(1, 0, 2, 3).reshape(128, 16, 640))


def make_in_maps(inputs, mode="fused", x1_full=None):
    x = np.asarray(inputs["x"], np.float32)
    norm_pre = np.asarray(inputs["norm_pre"], np.float32)
    norm_post = np.asarray(inputs["norm_post"], np.float32)
    ident = np.eye(128, dtype=np.float32).astype(ml_dtypes.bfloat16)
    gpreT = np.ascontiguousarray(norm_pre.reshape(2, 8, 128).transpose(2, 0, 1))
    gpost_bc = np.ascontiguousarray(np.broadcast_to(norm_post[None], (128, 2, 1024)))
    ident8 = (8.0 * np.eye(128, dtype=np.float32)).astype(ml_dtypes.bfloat16)
    common = {"ident": ident, "ident8": ident8, "gpreT": gpreT, "gpost_bc": gpost_bc}
    if mode in ("fused", "l0"):
        common.update({
            "pool_w_in": np.ascontiguousarray(inputs["pool_w_in"][0], np.float32),
            "pool_w_group": np.ascontiguousarray(inputs["pool_w_group"][0], np.float32),
            "pool_scaleT": np.ascontiguousarray(np.asarray(inputs["pool_scale"][0], np.float32).reshape(16, 128).T),
            "pool_w_out": np.ascontiguousarray(inputs["pool_w_out"][0], np.float32),
        })
    if mode in ("fused", "l1"):
        common.update({
            "att_w_in": np.ascontiguousarray(inputs["att_w_in"][0], np.float32),
            "att_w_out": np.ascontiguousarray(inputs["att_w_out"][0], np.float32),
            "biasG": _bias_gather(np.asarray(inputs["att_rel_bias"][0], np.float32)),
        })
    maps = []
    for c in range(NCORES):
        b, seg = c // 4, c % 4
        s = seg * SEG
        m = dict(common)
        if mode in ("fused", "l0"):
            xin = np.zeros((NT0 * 128, D), np.float32)
            lo = s - (HALO + 128)
            src_lo = max(lo, 0)
            xin[src_lo - lo:] = x[b, src_lo:s + SEG]
            m["xin"] = xin
            m["band"] = _band_consts(seg)
        if mode in ("fused", "l1"):
            pos = s - HALO + np.arange(NT1 * 128)
            m["kvalid"] = np.ascontiguousarray((pos >= 0).astype(np.float32).reshape(NT1, 128).T)
        if mode == "l1":
            x1c = np.zeros((NT1 * 128, D), np.float32)
            lo = s - HALO
            src_lo = max(lo, 0)
            x1c[src_lo - lo:] = x1_full[b, src_lo:s + SEG]
            m["x1"] = x1c
        maps.append(m)
    return maps


_NC_CACHE = {}


def _get_nc(mode):
    if mode not in _NC_CACHE:
        _NC_CACHE[mode] = build_program(mode)
    return _NC_CACHE[mode]


def kernel(x, norm_pre, norm_post, pool_w_in, pool_w_group, pool_scale, pool_w_out,
           att_w_in, att_rel_bias, att_w_out):
    inputs = dict(x=x, norm_pre=norm_pre, norm_post=norm_post, pool_w_in=pool_w_in,
                  pool_w_group=pool_w_group, pool_scale=pool_scale, pool_w_out=pool_w_out,
                  att_w_in=att_w_in, att_rel_bias=att_rel_bias, att_w_out=att_w_out)
    inputs = {k: np.asarray(v) for k, v in inputs.items()}
    nc = _get_nc("fused")
    maps = make_in_maps(inputs, "fused")
    res = run_bass_kernel_spmd(nc, maps, core_ids=list(range(NCORES)))
    out = np.empty((2, SEQ, D), np.float32)
    for c in range(NCORES):
        b, seg = c // 4, c % 4
        out[b, seg * SEG:(seg + 1) * SEG] = res.results[c]["out"]
    return out
```

```python
from contextlib import ExitStack

import ml_dtypes
import numpy as np

import concourse.bass as bass
import concourse.mybir as mybir
from concourse.bass_utils import run_bass_kernel_spmd

F32 = mybir.dt.float32
BF16 = mybir.dt.bfloat16
AF = mybir.ActivationFunctionType
ALU = mybir.AluOpType

D = 1024
SEQ = 8192
NCORES = 8
SEG = 2048
HALO = 512
NT0 = 21
NT1 = 20
EPS = 1e-6
WINDOWS = (2, 4, 8, 16)


class _Buf:
    __slots__ = ("w", "rs")

    def __init__(self):
        self.w = None
        self.rs = []


class Sched:
    ENGS = ("pe", "act", "dve", "pool", "sp")

    def __init__(self, nc, es):
        self.nc = nc
        self.es = es
        self.ops = {e: [] for e in self.ENGS}
        self.sems = {}
        self.cnt = {}
        self.waited = {e: {} for e in self.ENGS}
        self.bufs = {}
        self.name_alloc = {}
        self.allocs = {}
        self.alloc_names = {}
        for e in ("pe", "act", "dve", "pool"):
            self._sem("eng_" + e)

    def _sem(self, key):
        if key not in self.sems:
            self.sems[key] = self.es.enter_context(self.nc.semaphore(key))
            self.cnt[key] = 0
        return self.sems[key]

    def B(self, name):
        b = self.bufs.get(name)
        if b is None:
            b = self.bufs[name] = _Buf()
        return b

    def _alias_deps(self, name):
        if name in self.name_alloc:
            return []
        best = None
        for an in self.allocs:
            if name.startswith(an) and (best is None or len(an) > len(best)):
                best = an
        self.name_alloc[name] = best
        if best is None:
            return []
        self.alloc_names.setdefault(best, set()).add(name)
        s0, e0, ph = self.allocs[best]
        deps = []
        for an, (s1, e1, ph1) in self.allocs.items():
            if ph1 < ph and ph1 >= 0 and s1 < e0 and s0 < e1:
                deps += [self.B(n) for n in sorted(self.alloc_names.get(an, ()))]
        return deps

    def _need(self, eng, tok, waits):
        if tok is None:
            return
        key, val = tok
        if self.waited[eng].get(key, 0) >= val:
            return
        self.waited[eng][key] = val
        waits[key] = max(waits.get(key, 0), val)

    def op(self, eng, fns, reads=(), writes=(), dma_sem=None):
        if callable(fns):
            fns = [fns]
        waits = {}
        rb = [self.B(n) for n in reads]
        wb = [self.B(n) for n in writes]
        xb = []
        for n in list(reads) + list(writes):
            xb += self._alias_deps(n)
        for b in rb:
            self._need(eng, b.w, waits)
        for b in wb + xb:
            self._need(eng, b.w, waits)
            for t in b.rs:
                self._need(eng, t, waits)
        if dma_sem is not None:
            key = "dma_" + dma_sem
            self._sem(key)
            self.cnt[key] += 16
            inc = 16
        else:
            key = "eng_" + eng
            self.cnt[key] += 1
            inc = 1
        tok = (key, self.cnt[key])
        for b in rb:
            b.rs.append(tok)
        for b in wb:
            b.w = tok
            b.rs = []
        self.ops[eng].append((sorted(waits.items()), fns, key, inc))
        return tok

    def wait_all(self, eng, toks):
        waits = {}
        for t in toks:
            self._need(eng, t, waits)
        self.ops[eng].append((sorted(waits.items()), [], None, 0))

    def emit(self):
        sems = self.sems

        def run(e, lst):
            for waits, fns, key, inc in lst:
                for k, v in waits:
                    e.wait_ge(sems[k], v)
                ins = None
                for f in fns:
                    ins = f(e)
                if fns and key is not None:
                    ins.then_inc(sems[key], inc)

        with self.nc.Block() as block:
            @block.tensor
            def _(e):
                run(e, self.ops["pe"])

            @block.scalar
            def _(e):
                run(e, self.ops["act"])

            @block.vector
            def _(e):
                run(e, self.ops["dve"])

            @block.gpsimd
            def _(e):
                run(e, self.ops["pool"])

            @block.sync
            def _(e):
                run(e, self.ops["sp"])


def MM(out, lhsT, rhs, start=True, stop=True):
    return lambda e: e.matmul(out, lhsT=lhsT, rhs=rhs, start=start, stop=stop)


def TR(out, in_, ident):
    return lambda e: e.transpose(out=out, in_=in_, identity=ident)


def ACT(out, in_, func, **kw):
    return lambda e: e.activation(out=out, in_=in_, func=func, **kw)


def CP(out, in_):
    return lambda e: e.tensor_copy(out=out, in_=in_)


def TS(out, in0, s1, s2, op0, op1=None):
    if op1 is None:
        return lambda e: e.tensor_scalar(out=out, in0=in0, scalar1=s1, scalar2=None, op0=op0)
    return lambda e: e.tensor_scalar(out=out, in0=in0, scalar1=s1, scalar2=s2, op0=op0, op1=op1)


def STT(out, in0, scalar, in1, op0, op1):
    return lambda e: e.scalar_tensor_tensor(out=out, in0=in0, scalar=scalar, in1=in1, op0=op0, op1=op1)


def TT(out, in0, in1, op):
    return lambda e: e.tensor_tensor(out=out, in0=in0, in1=in1, op=op)


def DMA(out, in_):
    return lambda e: e.dma_start(out=out, in_=in_)


def MEMSET(ap, v):
    return lambda e: e.memset(ap, v)


def RECIP(out, in_):
    return lambda e: e.reciprocal(out=out, in_=in_)


class Ctx:
    ARENA_BYTES = 207 * 1024

    def __init__(self, nc, es):
        self.nc = nc
        self.es = es
        self.S = Sched(nc, es)
        self.psum = es.enter_context(nc.psum_tensor("psum_all", [128, 4096], F32))
        self.arena = es.enter_context(nc.sbuf_tensor("arena", [128, self.ARENA_BYTES // 2], BF16))
        self.off = 0
        self.phase = -1
        self.bank_ptr = 0
        self.ring = list(range(8))

    def sb(self, name, shape, dt):
        esz = 4 if dt == F32 else 2
        n = 1
        for s in shape[1:]:
            n *= s
        nbytes = (n * esz + 63) // 64 * 64
        assert self.off + nbytes <= self.ARENA_BYTES, (name, self.off, nbytes)
        a = self.arena[:, self.off // 2:self.off // 2 + n * esz // 2]
        self.S.allocs[name] = (self.off, self.off + nbytes, self.phase)
        self.off += nbytes
        if dt != BF16:
            a = a.bitcast(dt)
        if len(shape) == 3:
            a = a.rearrange("p (a b) -> p a b", a=shape[1])
        elif len(shape) == 4:
            a = a.rearrange("p (a b c) -> p a b c", a=shape[1], b=shape[2])
        return a

    def banks(self, n=1):
        r = self.ring
        if n == 1:
            b = r[self.bank_ptr % len(r)]
            self.bank_ptr += 1
            return b, [f"bank{b}"]
        assert n == 2
        for _ in range(len(r) + 1):
            b = r[self.bank_ptr % len(r)]
            if b % 2 == 0 and r[(self.bank_ptr + 1) % len(r)] == b + 1:
                self.bank_ptr += 2
                return b, [f"bank{b}", f"bank{b + 1}"]
            self.bank_ptr += 1
        raise AssertionError("no adjacent PSUM bank pair in ring")

    def pf32(self, b, ncols=512):
        return self.psum[:, b * 512:b * 512 + ncols]

    def pbf16(self, b):
        return self.psum[:, b * 512:(b + 1) * 512].bitcast(BF16)

    def barrier(self):
        S = self.S
        toks = [(k, v) for k, v in S.cnt.items() if v > 0]
        for e in S.ENGS:
            S.wait_all(e, toks)


def load_consts(C, aps):
    S = C.S
    K = {}
    K["ident"] = C.sb("ident", [128, 128], BF16)
    S.op("sp", DMA(K["ident"], aps["ident"]), writes=["ident"], dma_sem="c_ident")
    K["ident8"] = C.sb("ident8", [128, 128], BF16)
    S.op("sp", DMA(K["ident8"], aps["ident8"]), writes=["ident8"], dma_sem="c_ident8")
    K["gpre"] = C.sb("gpre", [128, 2, 8], F32)
    S.op("sp", DMA(K["gpre"], aps["gpreT"]), writes=["gpre"], dma_sem="c_gpre")
    K["mhalf"] = C.sb("mhalf", [128, 1], F32)
    S.op("pool", MEMSET(K["mhalf"], -0.5), writes=["mhalf"])
    K["junk"] = C.sb("junk", [128, 1024], BF16)
    return K


def merge(*lists):
    lists = [l for l in lists if l]
    out = []
    pos = [0] * len(lists)
    total = sum(len(l) for l in lists)
    for _ in range(total):
        best, bi = None, -1
        for i, l in enumerate(lists):
            if pos[i] < len(l):
                frac = (pos[i] + 0.5) / len(l)
                if best is None or frac < best:
                    best, bi = frac, i
        out.append(lists[bi][pos[bi]])
        pos[bi] += 1
    return out


def run(thunks):
    for t in thunks:
        t()


def norm_transpose_thunks(C, K, layer, xs_ap, xs_name, hT, hT_name, col0, st, st_name, hb, hb_name, after_hb=None,
                          rstd_on_pool=True):
    S = C.S
    ss, ms, rs = st[:, 0:1], st[:, 1:2], st[:, 2:3]
    th = []
    th.append(lambda: S.op("act", ACT(K["junk"], xs_ap, AF.Square, accum_out=ss), reads=[xs_name], writes=["junk", st_name + "a"]))
    th.append(lambda: S.op("dve", TS(ms, ss, 1.0 / D, EPS, ALU.mult, ALU.add), reads=[st_name + "a"], writes=[st_name + "b"]))
    if rstd_on_pool:
        th.append(lambda: S.op("pool", TT(rs, ms, K["mhalf"], ALU.pow), reads=[st_name + "b", "mhalf"], writes=[st_name + "c"]))
    else:
        sq = st[:, 3:4]
        th.append(lambda: S.op("act", ACT(sq, ms, AF.Sqrt), reads=[st_name + "b"], writes=[st_name + "s"]))
        th.append(lambda: S.op("dve", RECIP(rs, sq), reads=[st_name + "s"], writes=[st_name + "c"]))

    def hb_():
        S.op("act", ACT(hb, xs_ap, AF.Copy, scale=rs), reads=[xs_name, st_name + "c"], writes=[hb_name])
        if after_hb is not None:
            after_hb()
    th.append(hb_)

    def tr_():
        b, bn = C.banks(1)
        pt = C.pbf16(b)
        S.op("pe", [TR(pt[:, k * 128:(k + 1) * 128], hb[:, k * 128:(k + 1) * 128], K["ident"]) for k in range(8)],
             reads=[hb_name, "ident"], writes=bn)
        gb = K["gpre"][:, layer, :].unsqueeze(2).to_broadcast([128, 8, 128])
        S.op("dve", TT(hT[:, :, col0:col0 + 128], pt.rearrange("p (k t) -> p k t", k=8), gb, ALU.mult),
             reads=bn + ["gpre"], writes=[hT_name])
    th.append(tr_)
    return th


def post_norm_thunks(C, K, layer, yb, ybn, st, st_name, tmp, tmp_name, xr_ap, xr_name, gpost=None, gpost_name=None):
    S = C.S
    y = C.pf32(yb, 1024)
    ss, ms, rs = st[:, 0:1], st[:, 1:2], st[:, 2:3]
    if gpost is None:
        gpost, gpost_name = K["gpost"], f"gpost{layer}"
    th = []
    th.append(lambda: S.op("act", ACT(K["junk"], y, AF.Square, accum_out=ss), reads=ybn, writes=["junk", st_name + "a"]))
    th.append(lambda: S.op("dve", TS(ms, ss, 1.0 / D, EPS, ALU.mult, ALU.add), reads=[st_name + "a"], writes=[st_name + "b"]))
    th.append(lambda: S.op("pool", TT(rs, ms, K["mhalf"], ALU.pow), reads=[st_name + "b", "mhalf"], writes=[st_name + "c"]))
    th.append(lambda: S.op("dve", STT(tmp, y, rs, gpost, ALU.mult, ALU.mult),
                           reads=ybn + [st_name + "c", gpost_name], writes=[tmp_name]))
    th.append(lambda: S.op("pool", TT(tmp, tmp, xr_ap, ALU.add), reads=[tmp_name, xr_name], writes=[tmp_name]))
    return th


def build_layer0(C, K, aps, x_in, x1_out, hoist=None, early=None):
    S = C.S
    sb = C.sb
    C.ring = list(range(8))
    C.bank_ptr = 0
    W0 = sb("W0", [128, 8, 4096], BF16)
    Wg = sb("Wg", [128, 16, 512], BF16)
    Wo = sb("Wo0", [128, 16, 1024], BF16)
    scl = sb("scl0", [128, 16], F32)
    band = sb("band", [128, 3, 4, 128], BF16)
    K["gpost"] = sb("gpost0", [128, 1024], F32)
    S.op("sp", DMA(K["gpost"], aps["gpost_bc"][:, 0, :]), writes=["gpost0"], dma_sem="c_gpost0")
    NXS = 2
    xs = [sb(f"l0xs{i}", [128, 1024], F32) for i in range(NXS)]
    xr = [sb(f"l0xr{i}", [128, 1024], F32) for i in range(2)]
    hb = [sb(f"l0hb{i}", [128, 1024], BF16) for i in range(2)]
    hT = [sb(f"l0hT{i}", [128, 8, 256], BF16) for i in range(2)]
    NA = 5
    at = [sb(f"l0a{i}", [128, 2048], BF16) for i in range(NA)]
    szT = [sb(f"l0sz{i}", [128, 16, 256], BF16) for i in range(2)]
    mixT = sb("l0mx", [128, 16, 256], BF16)
    tmp = [sb(f"l0tmp{i}", [128, 1024], F32) for i in range(2)]
    stat = sb("l0stat", [128, NT0, 2, 4], F32)

    w_in = aps["pool_w_in"].rearrange("(k p) n -> p k n", p=128)
    for n in range(4):
        for kh in range(2):
            S.op("pool", DMA(W0[:, kh * 4:(kh + 1) * 4, n * 512:(n + 1) * 512], w_in[:, kh * 4:(kh + 1) * 4, n * 512:(n + 1) * 512]),
                 writes=[f"W0a{n}_{kh}"], dma_sem=f"W0a{n}_{kh}")
    S.op("pool", DMA(band.rearrange("p a g t -> p (a g t)"), aps["band"]), writes=["band"], dma_sem="c_band")
    for n in range(4):
        for kh in range(2):
            c0 = 2048 + n * 512
            S.op("pool", DMA(W0[:, kh * 4:(kh + 1) * 4, c0:c0 + 512], w_in[:, kh * 4:(kh + 1) * 4, c0:c0 + 512]),
                 writes=[f"W0z{n}_{kh}"], dma_sem=f"W0z{n}_{kh}")
    wg_in = aps["pool_w_group"].rearrange("g (kc p) d -> p (g kc) d", p=128)
    for g in range(4):
        S.op("pool", DMA(Wg[:, g * 4:(g + 1) * 4, :], wg_in[:, g * 4:(g + 1) * 4, :]), writes=[f"Wg{g}"], dma_sem=f"Wg{g}")
    S.op("sp", DMA(scl, aps["pool_scaleT"]), writes=["scl0"], dma_sem="c_scl")
    wo_in = aps["pool_w_out"].rearrange("(k p) n -> p k n", p=128)
    for kq in range(4):
        S.op("pool", DMA(Wo[:, kq * 4:(kq + 1) * 4, :], wo_in[:, kq * 4:(kq + 1) * 4, :]), writes=[f"Wo0_{kq}"], dma_sem=f"Wo0_{kq}")
    Wo_names = [f"Wo0_{kq}" for kq in range(4)]

    def load_x(t):
        sl = t % NXS
        S.op("sp", DMA(xs[sl], x_in[t * 128:(t + 1) * 128, :]), writes=[f"l0xs{sl}"], dma_sem=f"l0xs{sl}")

    def front_tile(t, hTb, hTn, col0):
        nxt = (lambda: load_x(t + 2)) if t + 2 < NT0 else None
        return norm_transpose_thunks(C, K, 0, xs[t % NXS], f"l0xs{t % NXS}", hTb, hTn, col0,
                                     stat[:, t, 0, :], f"l0stat{t}", hb[t % 2], f"l0hb{t % 2}", after_hb=nxt,
                                     rstd_on_pool=(t > 2))

    def proj_a_thunks(t, hTb, hTn, col0):
        a = at[t % NA]
        th = []
        for n in range(4):
            def f(n=n):
                b, bn = C.banks(1)
                pa = C.pf32(b)
                S.op("pe", [MM(pa, hTb[:, k, col0:col0 + 128], W0[:, k, n * 512:(n + 1) * 512], k == 0, k == 7) for k in range(8)],
                     reads=[hTn, f"W0a{n}_0", f"W0a{n}_1"], writes=bn)
                if n % 2 == 0:
                    S.op("act", ACT(a[:, n * 512:(n + 1) * 512], pa, AF.Copy), reads=bn, writes=[f"l0a{t % NA}_{n}"])
                else:
                    S.op("dve", CP(a[:, n * 512:(n + 1) * 512], pa), reads=bn, writes=[f"l0a{t % NA}_{n}"])
            th.append(f)
        return th

    def tiles_of(blk):
        return [1 + 2 * blk, 2 + 2 * blk]

    def stage_F(blk):
        p = blk % 2
        th = []
        for j, t in enumerate(tiles_of(blk)):
            th += front_tile(t, hT[p], f"l0hT{p}", j * 128)
        return th

    def stage_A(blk):
        p = blk % 2
        th = []
        for j, t in enumerate(tiles_of(blk)):
            th += proj_a_thunks(t, hT[p], f"l0hT{p}", j * 128)
        return th

    def stage_Z(blk):
        p = blk % 2
        th = []
        for d in range(16):
            def f(d=d):
                b, bn = C.banks(1)
                pz = C.pf32(b, 256)
                S.op("pe", [MM(pz, W0[:, k, 2048 + d * 128:2048 + (d + 1) * 128], hT[p][:, k, :], k == 0, k == 7) for k in range(8)],
                     reads=[f"l0hT{p}", f"W0z{d // 4}_0", f"W0z{d // 4}_1"], writes=bn)
                S.op("act", ACT(szT[p][:, d, :], pz, AF.Silu), reads=bn, writes=[f"l0sz{p}_{d}"])
            th.append(f)
        return th

    def stage_M(blk):
        th = []
        for j, t in enumerate(tiles_of(blk)):
            a_cur = at[t % NA]
            a_prev = at[(t - 1) % NA]
            cur_sel = 2 if t == 5 else 0
            for cq in range(4):
                def f(j=j, t=t, cq=cq, a_cur=a_cur, a_prev=a_prev, cur_sel=cur_sel):
                    b, bn = C.banks(1)
                    pm = C.pf32(b)
                    fns = []
                    for ci in range(4):
                        c = cq * 4 + ci
                        fns.append(MM(pm[:, ci * 128:(ci + 1) * 128], a_cur[:, c * 128:(c + 1) * 128], band[:, cur_sel, cq, :], True, False))
                        fns.append(MM(pm[:, ci * 128:ci * 128 + 16], a_prev[:, c * 128:(c + 1) * 128], band[:, 1, cq, 0:16], False, True))
                    S.op("pe", fns, reads=[f"l0a{t % NA}_{cq}", f"l0a{(t - 1) % NA}_{cq}", "band"], writes=bn)
                    src = pm.rearrange("p (c t) -> p c t", c=4)
                    dst = mixT[:, cq * 4:(cq + 1) * 4, j * 128:(j + 1) * 128]
                    if cq % 2 == 0:
                        S.op("dve", CP(dst, src), reads=bn, writes=[f"l0mx_{cq}_{j}"])
                    else:
                        S.op("act", ACT(dst, src, AF.Copy), reads=bn, writes=[f"l0mx_{cq}_{j}"])
                th.append(f)
        return th

    def stage_G(blk):
        p = blk % 2
        th = []
        for d in range(16):
            def f(d=d):
                g = d // 4
                b, bn = C.banks(1)
                pg = C.pf32(b, 256)
                S.op("pe", [MM(pg, Wg[:, g * 4 + kc, (d % 4) * 128:(d % 4 + 1) * 128], mixT[:, g * 4 + kc, :], kc == 0, kc == 3) for kc in range(4)],
                     reads=[f"l0mx_{g}_0", f"l0mx_{g}_1", f"Wg{g}"], writes=bn)
                S.op("dve", STT(szT[p][:, d, :], pg, scl[:, d:d + 1], szT[p][:, d, :], ALU.mult, ALU.mult),
                     reads=bn + [f"l0sz{p}_{d}", "scl0"], writes=[f"l0sz{p}_{d}"])
            th.append(f)
        return th

    out_toks = []

    def stage_Y(blk):
        p = blk % 2
        th = []
        for j, t in enumerate(tiles_of(blk)):
            hold = {}

            def mmy(j=j, t=t, hold=hold):
                S.op("sp", DMA(xr[j], x_in[t * 128:(t + 1) * 128, :]), writes=[f"l0xr{j}"], dma_sem=f"l0xr{j}")
                yb, ybn = C.banks(2)
                hold["y"] = (yb, ybn)
                for n in range(2):
                    py = C.pf32(yb + n)
                    S.op("pe", [MM(py, szT[p][:, kd, j * 128:(j + 1) * 128], Wo[:, kd, n * 512:(n + 1) * 512], kd == 0, kd == 15) for kd in range(16)],
                         reads=[f"l0sz{p}_{d}" for d in range(16)] + Wo_names, writes=[ybn[n]])
                yb, ybn = hold["y"]
                tm = tmp[t % 2]
                run(post_norm_thunks(C, K, 0, yb, ybn, stat[:, t, 1, :], f"l0stat{t}y", tm, f"l0tmp{t % 2}", xr[j], f"l0xr{j}"))
                tok = S.op("sp", DMA(x1_out[(t - 1) * 128:t * 128, :], tm), reads=[f"l0tmp{t % 2}"], writes=[f"x1d{t - 1}"],
                           dma_sem=f"l0out{t % 2}")
                out_toks.append(tok)
            th.append(mmy)
        return th

    load_x(0)
    load_x(1)
    run(front_tile(0, hT[1], "l0hT1", 0))
    run(proj_a_thunks(0, hT[1], "l0hT1", 0))
    NB = 10
    run(stage_F(0))
    run(stage_A(0))
    for blk in range(NB):
        nxt = blk + 1 < NB
        run(merge(stage_Z(blk), stage_F(blk + 1) if nxt else []))
        if not nxt and hoist is not None:
            run(hoist())
        run(merge(stage_M(blk), stage_A(blk + 1) if nxt else []))
        run(merge(stage_G(blk), stage_Y(blk - 1) if blk >= 1 else []))
    run(merge(stage_Y(NB - 1), early() if early is not None else []))
    return out_toks


def make_layer1(C, K, aps, x1_in, out):
    S = C.S
    sb = C.sb
    W1 = sb("W1", [128, 8, 4096], BF16)
    Wo = sb("Wo1", [128, 8, 1024], BF16)
    E = sb("Etab", [128, 16, 640], BF16)
    kT = sb("kT", [128, 8, 1024], BF16)
    vx = sb("vx", [128, 8, 16, 65], BF16)
    kval = sb("kval", [128, NT1], F32)
    gpost = sb("gpost1", [128, 1024], F32)
    NXS = 2
    xs = [sb(f"l1xs{i}", [128, 1024], F32) for i in range(NXS)]
    xr = sb("l1xr", [128, 1024], F32)
    hb = sb("l1hb", [128, 1024], BF16)
    hT = [sb(f"l1hT{i}", [128, 8, 256], BF16) for i in range(2)]
    stat = sb("l1stat", [128, NT1, 2, 4], F32)
    qE = [sb(f"l1qE{i}", [128, 8, 256], BF16) for i in range(2)]
    qO = [sb(f"l1qO{i}", [128, 8, 256], BF16) for i in range(2)]
    sz = [sb(f"l1sz{i}", [128, 2, 1024], BF16) for i in range(2)]
    zt = [sb(f"l1zt{i}", [128, 512], BF16) for i in range(2)]
    SKEW = 3
    NPE, NPT, NST = 2, SKEW + 2, 2
    pt_ = [sb(f"l1pt{i}", [128, 640], BF16) for i in range(NPT)]
    rden = sb("rden", [128, 4, 4], F32)
    ocp = [sb(f"l1ocp{i}", [128, 4, 65], F32) for i in range(2)]
    gt = sb("l1g", [128, 1024], BF16)
    gT = sb("l1gT", [128, 8, 128], BF16)
    tmp = sb("l1tmp", [128, 1024], F32)

    w_in = aps["att_w_in"].rearrange("(k p) n -> p k n", p=128)

    def hoist():
        th = []
        for n in [2, 3, 4, 5, 0, 1, 6, 7]:
            for kh in range(2):
                th.append(lambda n=n, kh=kh: S.op(
                    "pool", DMA(W1[:, kh * 4:(kh + 1) * 4, n * 512:(n + 1) * 512], w_in[:, kh * 4:(kh + 1) * 4, n * 512:(n + 1) * 512]),
                    writes=[f"W1_{n}_{kh}"], dma_sem=f"W1_{n}_{kh}"))
        return th

    wo_in = aps["att_w_out"].rearrange("(k p) n -> p k n", p=128)
    bg = aps["biasG"]

    def prologue():
        S.op("sp", DMA(gpost, aps["gpost_bc"][:, 1, :]), writes=["gpost1"], dma_sem="c_gpost1")
        for kh in range(2):
            S.op("pool", DMA(Wo[:, kh * 4:(kh + 1) * 4, :], wo_in[:, kh * 4:(kh + 1) * 4, :]), writes=[f"Wo1_{kh}"], dma_sem=f"Wo1_{kh}")
        S.op("sp", DMA(kval, aps["kvalid"]), writes=["kval"], dma_sem="c_kval")

    def etab_thunks():
        th = []
        for hq4 in range(4):
            th.append(lambda hq4=hq4: S.op("pool", DMA(E[:, hq4 * 4:(hq4 + 1) * 4, :], bg[:, hq4 * 4:(hq4 + 1) * 4, :]),
                                           writes=[f"Etab{hq4}"], dma_sem=f"Etab{hq4}"))

        def edges():
            S.op("pool", MEMSET(E[0:64, :, 64:128], -30000.0), writes=[f"Etab{q}" for q in range(4)])
            S.op("pool", MEMSET(E[64:128, :, 512:576], -30000.0), writes=[f"Etab{q}" for q in range(4)])
        th.append(edges)
        for i in range(2):
            th.append(lambda i=i: S.op("pool", MEMSET(qE[i], 0.0), writes=[f"l1qE{i}_{hp}" for hp in range(8)]))
            th.append(lambda i=i: S.op("pool", MEMSET(qO[i], 0.0), writes=[f"l1qO{i}_{hp}" for hp in range(8)]))
        return th

    Wn = lambda n: [f"W1_{n}_0", f"W1_{n}_1"]

    def load_x(u):
        sl = u % NXS
        S.op("sp", DMA(xs[sl], x1_in[u * 128:(u + 1) * 128, :]), reads=[f"x1d{u}"], writes=[f"l1xs{sl}"], dma_sem=f"l1xs{sl}")

    out_toks = []
    OBS = [7, 7]

    def finish_tile(u, j, p):
        b, bn = C.banks(1)
        ptr = C.pbf16(b)
        S.op("pe", [TR(ptr[:, k * 128:(k + 1) * 128], gt[:, k * 128:(k + 1) * 128], K["ident"]) for k in range(8)],
             reads=[f"l1g_{hg}" for hg in range(4)] + ["ident"], writes=bn)
        S.op("act", ACT(gT.rearrange("p k t -> p (k t)"), ptr, AF.Copy), reads=bn, writes=["l1gT"])
        S.op("sp", DMA(xr, x1_in[u * 128:(u + 1) * 128, :]), reads=[f"x1d{u}"], writes=["l1xr"], dma_sem="l1xr")
        yb, ybn = C.banks(2)
        for n in range(2):
            py = C.pf32(yb + n)
            S.op("pe", [MM(py, gT[:, k, :], Wo[:, k, n * 512:(n + 1) * 512], k == 0, k == 7) for k in range(8)],
                 reads=["l1gT", "Wo1_0", "Wo1_1"], writes=[ybn[n]])
        run(post_norm_thunks(C, K, 1, yb, ybn, stat[:, u, 1, :], f"l1stat{u}y", tmp, "l1tmp", xr, "l1xr", gpost, "gpost1"))
        tok = S.op("sp", DMA(out[(u - 4) * 128:(u - 3) * 128, :], tmp), reads=["l1tmp"], dma_sem="l1out")
        out_toks.append(tok)

    pending = []

    def emit_pv(unit):
        u, j, p, h, T0, pti = unit
        ptb = pt_[pti]
        hq, hg = h % 4, h // 4
        OB = OBS[hg % 2]
        obn = [f"bank{OB}"]
        po = C.pf32(OB)
        S.op("pe", [MM(po[:, hq * 65:hq * 65 + 65], ptb[:, jt * 128:(jt + 1) * 128], vx[:, (T0 + jt) % 8, h, :], jt == 0, jt == 4)
                    for jt in range(5)],
             reads=[f"l1pt{pti}"] + [f"vx{(T0 + jt) % 8}" for jt in range(5)], writes=obn)
        if hq == 3:
            oc = ocp[hg % 2]
            S.op("dve", CP(oc.rearrange("p h c -> p (h c)"), po[:, 0:260]), reads=obn, writes=[f"l1ocp{hg % 2}"])
            S.op("dve", RECIP(rden[:, hg, :], oc[:, :, 64]), reads=[f"l1ocp{hg % 2}"], writes=[f"rden{hg}"])
            S.op("dve", [STT(gt[:, (hg * 4 + q4) * 64:(hg * 4 + q4 + 1) * 64], oc[:, q4, 0:64], rden[:, hg, q4:q4 + 1],
                             sz[p][:, j, (hg * 4 + q4) * 64:(hg * 4 + q4 + 1) * 64], ALU.mult, ALU.mult) for q4 in range(4)],
                 reads=[f"l1ocp{hg % 2}", f"rden{hg}", f"l1sz{p}_{j}"], writes=[f"l1g_{hg}"])
        if h == 15:
            pending.append([2, (u, j, p)])
        for pf in list(pending):
            pf[0] -= 1
            if pf[0] < 0:
                pending.remove(pf)
                finish_tile(*pf[1])

    def stage_F(blk):
        p = blk % 2
        th = []
        for j, u in enumerate([2 * blk, 2 * blk + 1]):
            nxt = (lambda u=u: load_x(u + 2)) if u + 2 < NT1 else None
            th += norm_transpose_thunks(C, K, 1, xs[u % NXS], f"l1xs{u % NXS}", hT[p], f"l1hT{p}", j * 128,
                                        stat[:, u, 0, :], f"l1stat{u}", hb, "l1hb", after_hb=nxt)
        return th

    def stage_P(blk):
        p = blk % 2
        own = blk >= 2
        tiles = [2 * blk, 2 * blk + 1]
        th = []
        ring0 = (tiles[0] % 8) * 128
        for hp in range(8):
            def fk(hp=hp):
                b, bn = C.banks(1)
                pk = C.pf32(b, 256)
                S.op("pe", [MM(pk, W1[:, k, 1024 + hp * 128:1024 + (hp + 1) * 128], hT[p][:, k, :], k == 0, k == 7) for k in range(8)],
                     reads=[f"l1hT{p}"] + Wn(2 + hp // 4), writes=bn)
                dst = kT[:, hp, ring0:ring0 + 256]
                wn = [f"kT{tiles[0] % 8}_{hp}", f"kT{tiles[1] % 8}_{hp}"]
                if hp % 2 == 0:
                    S.op("act", ACT(dst, pk, AF.Copy), reads=bn, writes=wn)
                else:
                    S.op("dve", CP(dst, pk), reads=bn, writes=wn)
            th.append(fk)
        if own:
            for hp in range(8):
                def fq(hp=hp):
                    b, bn = C.banks(1)
                    pq = C.pf32(b, 256)
                    S.op("pe", [MM(pq, W1[:, k, hp * 128:(hp + 1) * 128], hT[p][:, k, :], k == 0, k == 7) for k in range(8)],
                         reads=[f"l1hT{p}"] + Wn(hp // 4), writes=bn)
                    S.op("act", ACT(qE[p][0:64, hp, :], pq[0:64, :], AF.Copy), reads=bn, writes=[f"l1qE{p}_{hp}"])
                    S.op("dve", CP(qO[p][64:128, hp, :], pq[64:128, :]), reads=bn, writes=[f"l1qO{p}_{hp}"])
                th.append(fq)
        for j, u in enumerate(tiles):
            rs_ = u % 8
            for n in range(2):
                def fv(j=j, u=u, n=n, rs_=rs_):
                    b, bn = C.banks(1)
                    pv = C.pf32(b)
                    S.op("pe", [MM(pv, hT[p][:, k, j * 128:(j + 1) * 128], W1[:, k, 2048 + n * 512:2048 + (n + 1) * 512], k == 0, k == 7) for k in range(8)],
                         reads=[f"l1hT{p}"] + Wn(4 + n), writes=bn)
                    src = pv.rearrange("p (h c) -> p h c", h=8)
                    dst = vx[:, rs_, n * 8:(n + 1) * 8, 0:64]
                    if n == 0:
                        S.op("dve", CP(dst, src), reads=bn, writes=[f"vx{rs_}"])
                    else:
                        S.op("act", ACT(dst, src, AF.Copy), reads=bn, writes=[f"vx{rs_}"])
                        S.op("pool", TS(vx[:, rs_, :, 64], kval[:, u:u + 1].to_broadcast([128, 16]), 2.0, None, ALU.mult), reads=["kval"], writes=[f"vx{rs_}"])
                th.append(fv)
            if own:
                for n in range(2):
                    def fz(j=j, n=n):
                        b, bn = C.banks(1)
                        pz = C.pf32(b)
                        S.op("pe", [MM(pz, hT[p][:, k, j * 128:(j + 1) * 128], W1[:, k, 3072 + n * 512:3072 + (n + 1) * 512], k == 0, k == 7) for k in range(8)],
                             reads=[f"l1hT{p}"] + Wn(6 + n), writes=bn)
                        zi = (2 * j + n) % 2
                        S.op("act", ACT(zt[zi], pz, AF.Tanh, scale=0.5), reads=bn, writes=[f"l1zt{zi}"])
                        S.op("dve", STT(sz[p][:, j, n * 512:(n + 1) * 512], zt[zi], 1.0, pz, ALU.add, ALU.mult),
                             reads=bn + [f"l1zt{zi}"], writes=[f"l1sz{p}_{j}"])
                    th.append(fz)
        return th

    units = []
    ucount = [0]

    def stage_ATT(blk):
        p = blk % 2
        th = []
        for j, u in enumerate([2 * blk, 2 * blk + 1]):
            T0 = u - 4
            for h in range(16):
                def f(j=j, u=u, T0=T0, h=h):
                    hp = h // 2
                    qsrc = (qE if h % 2 == 0 else qO)[p]
                    qn = f"l1q{'E' if h % 2 == 0 else 'O'}{p}_{hp}"
                    n = ucount[0]
                    ucount[0] += 1
                    sl, pti = n % NST, n % NPT
                    c0 = sl * 1024
                    fns = []
                    fns.append(MM(C.psum[:, c0:c0 + 512], K["ident8"], E[:, h, 0:512], True, False))
                    fns.append(MM(C.psum[:, c0 + 512:c0 + 640], K["ident8"], E[:, h, 512:640], True, False))
                    for jt in range(5):
                        rk = ((T0 + jt) % 8) * 128
                        dst = C.psum[:, c0 + jt * 128:c0 + (jt + 1) * 128]
                        fns.append(MM(dst, kT[:, hp, rk:rk + 128], qsrc[:, hp, j * 128:(j + 1) * 128], False, jt in (3, 4)))
                    S.op("pe", fns, reads=[qn, f"Etab{h // 4}", "ident8"] + [f"kT{(T0 + jt) % 8}_{hp}" for jt in range(5)], writes=[f"sT{sl}", f"bank{2 * sl}", f"bank{2 * sl + 1}"])
                    S.op("act", ACT(pt_[pti], C.psum[:, c0:c0 + 640], AF.Exp, scale=0.125),
                         reads=[f"sT{sl}", f"bank{2 * sl}", f"bank{2 * sl + 1}"], writes=[f"l1pt{pti}"])
                    units.append((u, j, p, h, T0, pti))
                    if len(units) > SKEW:
                        emit_pv(units.pop(0))
                th.append(f)

        def flush():
            while units:
                emit_pv(units.pop(0))
            while pending:
                finish_tile(*pending.pop(0)[1])
        th.append(flush)
        return th

    def early():
        def ld():
            load_x(0)
            load_x(1)
        return [ld] + stage_F(0)

    def body():
        C.ring = [4, 5, 6]
        C.bank_ptr = 0
        prologue()
        NB = 10
        et = etab_thunks()
        run(merge(stage_P(0), stage_F(1), et[:4]))
        run(merge(stage_P(1), stage_F(2), et[4:]))
        run(merge(stage_P(2), stage_F(3)))
        for blk in range(2, NB):
            run(merge(stage_ATT(blk),
                      stage_P(blk + 1) if blk + 1 < NB else [],
                      stage_F(blk + 2) if blk + 2 < NB else []))
        return out_toks

    return hoist, body, early


def build_program(mode="fused"):
    nc = bass.Bass("TRN2", target_bir_lowering=False)
    aps = {}

    def din(name, shape, dt=F32):
        aps[name] = nc.dram_tensor(name, list(shape), dt, kind="ExternalInput").ap()

    din("ident", [128, 128], BF16)
    din("ident8", [128, 128], BF16)
    din("gpreT", [128, 2, 8])
    din("gpost_bc", [128, 2, 1024])
    if mode in ("fused", "l0"):
        din("xin", [NT0 * 128, D])
        din("pool_w_in", [D, 4096])
        din("pool_w_group", [4, 512, 512])
        din("pool_scaleT", [128, 16])
        din("pool_w_out", [2048, D])
        din("band", [128, 3 * 4 * 128])
    if mode in ("fused", "l1"):
        din("att_w_in", [D, 4096])
        din("att_w_out", [D, D])
        din("biasG", [128, 16, 640])
        din("kvalid", [128, NT1])
    if mode == "l1":
        din("x1", [NT1 * 128, D])
        x1 = aps["x1"]
    elif mode == "l0":
        x1 = nc.dram_tensor("x1", [NT1 * 128, D], F32, kind="ExternalOutput").ap()
    else:
        x1 = nc.dram_tensor("x1_scratch", [NT1 * 128, D], F32, kind="Internal").ap()
    if mode in ("fused", "l1"):
        out = nc.dram_tensor("out", [SEG, D], F32, kind="ExternalOutput").ap()

    with ExitStack() as es:
        C = Ctx(nc, es)
        K = load_consts(C, aps)
        base = C.off
        toks = []
        hoist = body = early = None
        if mode in ("fused", "l1"):
            C.phase = 1
            hoist, body, early = make_layer1(C, K, aps, x1, out)
            C.off = base
        if mode in ("fused", "l0"):
            C.phase = 0
            toks = build_layer0(C, K, aps, aps["xin"], x1, hoist=hoist, early=early)
        elif hoist is not None:
            run(hoist())
            run(early())
        if body is not None:
            toks = body()
        C.S.wait_all("sp", toks)
        C.S.emit()
    return nc


def _band_consts(seg):
    band = np.zeros((128, 3, 4, 128), np.float32)
    tp = np.arange(128)[:, None]
    t = np.arange(128)[None, :]
    for g, w in enumerate(WINDOWS):
        win = ((tp <= t) & (tp > t - w)).astype(np.float32)
        band[:, 0, g, :] = win / w - np.eye(128, dtype=np.float32)
        band[:, 1, g, :] = ((tp - 128) > (t - w)).astype(np.float32) / w
        if seg == 0:
            cnt = np.minimum(t + 1, w).astype(np.float32)
            band[:, 2, g, :] = win / cnt - np.eye(128, dtype=np.float32)
        else:
            band[:, 2, g, :] = band[:, 0, g, :]
    return band.reshape(128, -1)


def _bias_gather(rel_bias):
    k = np.arange(128)[:, None, None]
    jt = np.arange(5)[None, :, None]
    q = np.arange(128)[None, None, :]
    idx = np.clip(q - k + 512 - 128 * jt, -256, 256) + 256
    g = rel_bias[:, idx]
    return np.ascontiguousarray(g.transpose(1, 0, 2, 3).reshape(128, 16, 640))


def make_in_maps(inputs, mode="fused", x1_full=None):
    x = np.asarray(inputs["x"], np.float32)
    norm_pre = np.asarray(inputs["norm_pre"], np.float32)
    norm_post = np.asarray(inputs["norm_post"], np.float32)
    ident = np.eye(128, dtype=np.float32).astype(ml_dtypes.bfloat16)
    gpreT = np.ascontiguousarray(norm_pre.reshape(2, 8, 128).transpose(2, 0, 1))
    gpost_bc = np.ascontiguousarray(np.broadcast_to(norm_post[None], (128, 2, 1024)))
    ident8 = (8.0 * np.eye(128, dtype=np.float32)).astype(ml_dtypes.bfloat16)
    common = {"ident": ident, "ident8": ident8, "gpreT": gpreT, "gpost_bc": gpost_bc}
    if mode in ("fused", "l0"):
        common.update({
            "pool_w_in": np.ascontiguousarray(inputs["pool_w_in"][0], np.float32),
            "pool_w_group": np.ascontiguousarray(inputs["pool_w_group"][0], np.float32),
            "pool_scaleT": np.ascontiguousarray(np.asarray(inputs["pool_scale"][0], np.float32).reshape(16, 128).T),
            "pool_w_out": np.ascontiguousarray(inputs["pool_w_out"][0], np.float32),
        })
    if mode in ("fused", "l1"):
        common.update({
            "att_w_in": np.ascontiguousarray(inputs["att_w_in"][0], np.float32),
            "att_w_out": np.ascontiguousarray(inputs["att_w_out"][0], np.float32),
            "biasG": _bias_gather(np.asarray(inputs["att_rel_bias"][0], np.float32)),
        })
    maps = []
    for c in range(NCORES):
        b, seg = c // 4, c % 4
        s = seg * SEG
        m = dict(common)
        if mode in ("fused", "l0"):
            xin = np.zeros((NT0 * 128, D), np.float32)
            lo = s - (HALO + 128)
            src_lo = max(lo, 0)
            xin[src_lo - lo:] = x[b, src_lo:s + SEG]
            m["xin"] = xin
            m["band"] = _band_consts(seg)
        if mode in ("fused", "l1"):
            pos = s - HALO + np.arange(NT1 * 128)
            m["kvalid"] = np.ascontiguousarray((pos >= 0).astype(np.float32).reshape(NT1, 128).T)
        if mode == "l1":
            x1c = np.zeros((NT1 * 128, D), np.float32)
            lo = s - HALO
            src_lo = max(lo, 0)
            x1c[src_lo - lo:] = x1_full[b, src_lo:s + SEG]
            m["x1"] = x1c
        maps.append(m)
    return maps


_NC_CACHE = {}


def _get_nc(mode):
    if mode not in _NC_CACHE:
        _NC_CACHE[mode] = build_program(mode)
    return _NC_CACHE[mode]


def kernel(x, norm_pre, norm_post, pool_w_in, pool_w_group, pool_scale, pool_w_out,
           att_w_in, att_rel_bias, att_w_out):
    inputs = dict(x=x, norm_pre=norm_pre, norm_post=norm_post, pool_w_in=pool_w_in,
                  pool_w_group=pool_w_group, pool_scale=pool_scale, pool_w_out=pool_w_out,
                  att_w_in=att_w_in, att_rel_bias=att_rel_bias, att_w_out=att_w_out)
    inputs = {k: np.asarray(v) for k, v in inputs.items()}
    nc = _get_nc("fused")
    maps = make_in_maps(inputs, "fused")
    res = run_bass_kernel_spmd(nc, maps, core_ids=list(range(NCORES)))
    out = np.empty((2, SEQ, D), np.float32)
    for c in range(NCORES):
        b, seg = c // 4, c % 4
        out[b, seg * SEG:(seg + 1) * SEG] = res.results[c]["out"]
    return out
```
